# Optimizing a Trainium2 kernel written in Bass

```python
import jax, jax.numpy as jnp
from jax import lax
import numpy as np

D_MODEL = 1024
BATCH = 8
SEQ = 2048
DEPTH = 2
DEC_BATCH = 32
DEC_SEQ = 16
PAST_LEN = 2048

CHUNK = 64
N_AB = (DEPTH + 1) // 2
N_C = DEPTH // 2
D_FF = 2816
D_MIX = D_MODEL
H_A = 4
DH_A = D_MIX // 2 // H_A
A_WIDTH = H_A * DH_A
H_B = 8
DH_B = D_MIX // 2 // H_B
B_WIDTH = H_B * DH_B
BAND_CHUNKS = 8
BAND_PAST = BAND_CHUNKS * CHUNK
MAX_REL = 128
PROJ_AB = 4 * A_WIDTH + 2 * H_A + 3 * B_WIDTH
R_WIDTH = D_MODEL
N_BLOCKS_C = 8
BW_C = R_WIDTH // N_BLOCKS_C
CONV_W = 4
LRU_C = 8.0
EPS = 1e-6
N_ADA = 9

kernel_name = 'hybrid_mlstm_band_rglru_stream_step'

F32 = jnp.float32


def rmsnorm(x, g):
    xf = x.astype(F32)
    y = xf * lax.rsqrt(jnp.mean(xf * xf, axis=-1, keepdims=True) + EPS)
    return (y * g.astype(F32)).astype(x.dtype)


def swiglu(h, w_in, w_out):
    gate, up = jnp.split(h @ w_in, 2, axis=-1)
    return (jax.nn.silu(gate) * up) @ w_out


def rel_bias(table, rel):
    return table[:, jnp.clip(rel, -MAX_REL, MAX_REL) + MAX_REL].astype(F32)


def mlstm_block(q, k, v, ig, lf, C, n, m):
    L = q.shape[1]
    bt = jnp.swapaxes(jnp.cumsum(lf, axis=1), 1, 2)
    igt = jnp.swapaxes(ig, 1, 2)
    causal = jnp.tril(jnp.ones((L, L), bool))
    logd = jnp.where(causal, bt[..., :, None] - bt[..., None, :] + igt[..., None, :], -jnp.inf)
    inter = bt + m[..., None]
    m_t = jnp.maximum(inter, jnp.max(logd, axis=-1))
    dw = jnp.exp(logd - m_t[..., None])
    iw = jnp.exp(inter - m_t)
    s = jnp.einsum('blhd,bshd->bhls', q, k) * dw
    num = iw[..., None] * jnp.einsum('blhd,bhde->bhle', q, C) + jnp.einsum('bhls,bshe->bhle', s, v)
    den = iw * jnp.einsum('blhd,bhd->bhl', q, n) + jnp.sum(s, axis=-1)
    h = num / jnp.maximum(jnp.abs(den), jnp.exp(-m_t))[..., None]
    b_last = bt[..., -1]
    g = b_last[..., None] - bt + igt
    m_new = jnp.maximum(b_last + m, jnp.max(g, axis=-1))
    w_prev = jnp.exp(b_last + m - m_new)
    w_s = jnp.exp(g - m_new[..., None])
    C_new = w_prev[..., None, None] * C + jnp.einsum('bhs,bshd,bshe->bhde', w_s, k, v)
    n_new = w_prev[..., None] * n + jnp.einsum('bhs,bshd->bhd', w_s, k)
    return jnp.swapaxes(h, 1, 2), C_new, n_new, m_new


def mlstm_chunked(q, k, v, ig, lf, C, n, m):
    Bn, S = q.shape[:2]
    nc = S // CHUNK

    def to_chunks(t):
        return jnp.moveaxis(t.reshape((Bn, nc, CHUNK) + t.shape[2:]), 1, 0)

    def step(carry, inp):
        h, C2, n2, m2 = mlstm_block(*inp, *carry)
        return (C2, n2, m2), h

    (C, n, m), hs = lax.scan(step, (C, n, m), (to_chunks(q), to_chunks(k), to_chunks(v), to_chunks(ig), to_chunks(lf)))
    return jnp.moveaxis(hs, 0, 1).reshape(q.shape[:3] + (v.shape[-1],)), C, n, m


def band_prompt(q, k, v, table):
    Bn, S, H, d = q.shape
    nc = S // CHUNK
    nb = BAND_CHUNKS + 1
    qc = q.reshape(Bn, nc, CHUNK, H, d)
    pad = ((0, 0), (BAND_CHUNKS, 0), (0, 0), (0, 0), (0, 0))
    kp = jnp.pad(k.reshape(Bn, nc, CHUNK, H, d), pad)
    vp = jnp.pad(v.reshape(Bn, nc, CHUNK, H, d), pad)
    kband = jnp.stack([kp[:, o:o + nc] for o in range(nb)], axis=2).reshape(Bn, nc, nb * CHUNK, H, d)
    vband = jnp.stack([vp[:, o:o + nc] for o in range(nb)], axis=2).reshape(Bn, nc, nb * CHUNK, H, d)
    key_off = jnp.arange(nb * CHUNK) - BAND_PAST
    bias = rel_bias(table, key_off[None, :] - jnp.arange(CHUNK)[:, None])
    valid = (jnp.arange(nc)[:, None] - BAND_CHUNKS + (jnp.arange(nb * CHUNK) // CHUNK)[None, :]) >= 0
    s = jnp.einsum('bnqhd,bnkhd->bnhqk', qc, kband).astype(F32) * (DH_B ** -0.5) + bias[None, None]
    s = jnp.where(valid[None, :, None, None, :], s, -jnp.inf)
    pr = jax.nn.softmax(s, axis=-1).astype(v.dtype)
    return jnp.einsum('bnhqk,bnkhd->bnqhd', pr, vband).reshape(Bn, S, H, d)


def band_sample(q, k, v, ck, cv, table):
    W, T = ck.shape[1], q.shape[1]
    kk = jnp.concatenate([ck.astype(k.dtype), k], axis=1)
    vv = jnp.concatenate([cv.astype(v.dtype), v], axis=1)
    bias = rel_bias(table, (jnp.arange(W + T) - W)[None, :] - jnp.arange(T)[:, None])
    s = jnp.einsum('bqhd,bkhd->bhqk', q, kk).astype(F32) * (DH_B ** -0.5) + bias[None]
    pr = jax.nn.softmax(s, axis=-1).astype(v.dtype)
    return jnp.einsum('bhqk,bkhd->bqhd', pr, vv)


def ab_mixer(h, p, i, a_state, b_cache):
    Bn, L, _ = h.shape
    u = h @ p['ab_w_in'][i]
    cuts = [A_WIDTH, 2 * A_WIDTH, 3 * A_WIDTH, 4 * A_WIDTH, 4 * A_WIDTH + 2 * H_A,
            4 * A_WIDTH + 2 * H_A + B_WIDTH, 4 * A_WIDTH + 2 * H_A + 2 * B_WIDTH]
    qa, ka, va, oa, ga, qb, kb, vb = jnp.split(u, cuts, axis=-1)
    qa = qa.reshape(Bn, L, H_A, DH_A).astype(F32)
    ka = ka.reshape(Bn, L, H_A, DH_A).astype(F32) * (DH_A ** -0.5)
    va = va.reshape(Bn, L, H_A, DH_A).astype(F32)
    ga = ga.astype(F32) + p['ab_gate_bias'][i].astype(F32)
    ig, lf = ga[..., :H_A], jax.nn.log_sigmoid(ga[..., H_A:])
    C0, n0, m0 = a_state
    if b_cache is None:
        ha, C1, n1, m1 = mlstm_chunked(qa, ka, va, ig, lf, C0, n0, m0)
    else:
        ha, C1, n1, m1 = mlstm_block(qa, ka, va, ig, lf, C0, n0, m0)
    ha = rmsnorm(ha, p['a_out_norm'][i]) * jax.nn.sigmoid(oa.reshape(Bn, L, H_A, DH_A).astype(F32))
    qb = rmsnorm(qb.reshape(Bn, L, H_B, DH_B), p['b_q_norm'][i])
    kb = rmsnorm(kb.reshape(Bn, L, H_B, DH_B), p['b_k_norm'][i])
    vb = vb.reshape(Bn, L, H_B, DH_B)
    table = p['b_rel_bias'][i]
    if b_cache is None:
        hb = band_prompt(qb, kb, vb, table)
        keep = min(BAND_PAST, L)
        new_k, new_v = kb[:, L - keep:], vb[:, L - keep:]
    else:
        hb = band_sample(qb, kb, vb, b_cache[0], b_cache[1], table)
        new_k, new_v = kb, vb
    mixed = jnp.concatenate([ha.reshape(Bn, L, A_WIDTH).astype(h.dtype), hb.reshape(Bn, L, B_WIDTH)], axis=-1)
    return mixed @ p['ab_w_out'][i], (C1, n1, m1, new_k, new_v)


def _lin_combine(left, right):
    a1, b1 = left
    a2, b2 = right
    return a1 * a2, a2 * b1 + b2


def rglru_mixer(h, p, i, conv_buf, h0):
    Bn, L, _ = h.shape
    gb, xb = jnp.split(h @ p['c_w_in'][i], 2, axis=-1)
    xp = jnp.concatenate([conv_buf.astype(xb.dtype), xb], axis=1)
    w = p['c_conv_w'][i]
    xc = p['c_conv_b'][i] + sum(xp[:, j:j + L] * w[j] for j in range(CONV_W))
    new_buf = xp[:, L:]
    gates = jnp.einsum('blnw,nwv->blnv', xc.reshape(Bn, L, N_BLOCKS_C, BW_C), p['c_gate_w'][i]).astype(F32)
    gates = gates.reshape(Bn, L, N_BLOCKS_C, 2, BW_C)
    gbias = p['c_gate_b'][i].astype(F32)
    r = jax.nn.sigmoid(gates[..., 0, :].reshape(Bn, L, R_WIDTH) + gbias[0])
    ii = jax.nn.sigmoid(gates[..., 1, :].reshape(Bn, L, R_WIDTH) + gbias[1])
    log_a = -LRU_C * r * jax.nn.softplus(-p['c_lambda'][i].astype(F32))
    a = jnp.exp(log_a)
    upd = jnp.sqrt(-jnp.expm1(2.0 * log_a)) * (ii * xc.astype(F32))
    a_cum, u_cum = lax.associative_scan(_lin_combine, (a, upd), axis=1)
    hs = a_cum * h0.astype(F32)[:, None] + u_cum
    y = (jax.nn.gelu(gb.astype(F32)) * hs).astype(h.dtype) @ p['c_w_out'][i]
    return y, (new_buf, hs[:, -1])


def run_trunk(x, c, p, st):
    prompt = st is None
    Bn = x.shape[0]
    out = {name: [] for name in ('a_C', 'a_n', 'a_m', 'b_k', 'b_v', 'c_conv', 'c_h')}
    for l in range(DEPTH):
        ada = (c @ p['ada_w'][l] + p['ada_b'][l]).reshape(Bn, N_ADA, 1, D_MODEL)

        def mod(z, g, j):
            return rmsnorm(z, g) * (1.0 + ada[:, 3 * j + 1]) + ada[:, 3 * j]

        x = x + 0.5 * ada[:, 2] * swiglu(mod(x, p['ffn1_norm'][l], 0), p['ffn1_w_in'][l], p['ffn1_w_out'][l])
        hm = mod(x, p['mix_norm'][l], 1)
        i = l // 2
        if l % 2 == 0:
            if prompt:
                a_state = (jnp.zeros((Bn, H_A, DH_A, DH_A), F32), jnp.zeros((Bn, H_A, DH_A), F32),
                           jnp.zeros((Bn, H_A), F32))
                b_cache = None
            else:
                a_state = (st['a_C'][i].astype(F32), st['a_n'][i].astype(F32), st['a_m'][i].astype(F32))
                b_cache = (st['b_k'][i], st['b_v'][i])
            y, (C1, n1, m1, nk, nv) = ab_mixer(hm, p, i, a_state, b_cache)
            out['a_C'].append(C1)
            out['a_n'].append(n1)
            out['a_m'].append(m1)
            out['b_k'].append(nk)
            out['b_v'].append(nv)
        else:
            if prompt:
                conv_buf = jnp.zeros((Bn, CONV_W - 1, R_WIDTH), x.dtype)
                h0 = jnp.zeros((Bn, R_WIDTH), F32)
            else:
                conv_buf, h0 = st['c_conv'][i], st['c_h'][i]
            y, (nb_, nh) = rglru_mixer(hm, p, i, conv_buf, h0)
            out['c_conv'].append(nb_)
            out['c_h'].append(nh)
        x = x + ada[:, 5] * y
        x = x + 0.5 * ada[:, 8] * swiglu(mod(x, p['ffn2_norm'][l], 2), p['ffn2_w_in'][l], p['ffn2_w_out'][l])
    return x, {name: jnp.stack(v_list, axis=0) for name, v_list in out.items()}


def setup_inputs(seed: int = 0) -> dict:
    key = jax.random.key(seed)
    ks = iter(jax.random.split(key, 48))
    nrm = lambda shape, s=1.0: jax.random.normal(next(ks), shape, F32) * s
    W_B = min(BAND_PAST, PAST_LEN)
    u_lam = jax.random.uniform(next(ks), (N_C, R_WIDTH), F32, 0.9, 0.999) ** (1.0 / LRU_C)
    return {
        'x_prompt': nrm((BATCH, SEQ, D_MODEL)),
        'x_sample': nrm((DEC_BATCH, DEC_SEQ, D_MODEL)),
        'state_a_C': nrm((N_AB, DEC_BATCH, H_A, DH_A, DH_A), 0.1),
        'state_a_n': nrm((N_AB, DEC_BATCH, H_A, DH_A), 0.1),
        'state_a_m': nrm((N_AB, DEC_BATCH, H_A)),
        'cache_b_k': nrm((N_AB, DEC_BATCH, W_B, H_B, DH_B)),
        'cache_b_v': nrm((N_AB, DEC_BATCH, W_B, H_B, DH_B)),
        'state_c_conv': nrm((N_C, DEC_BATCH, CONV_W - 1, R_WIDTH)),
        'state_c_h': nrm((N_C, DEC_BATCH, R_WIDTH), 0.5),
        'c_prompt': nrm((BATCH, D_MODEL)),
        'c_sample': nrm((DEC_BATCH, D_MODEL)),
        'ffn1_norm': 1.0 + nrm((DEPTH, D_MODEL), 0.01),
        'ffn1_w_in': nrm((DEPTH, D_MODEL, 2 * D_FF), D_MODEL ** -0.5),
        'ffn1_w_out': nrm((DEPTH, D_FF, D_MODEL), D_FF ** -0.5),
        'mix_norm': 1.0 + nrm((DEPTH, D_MODEL), 0.01),
        'ffn2_norm': 1.0 + nrm((DEPTH, D_MODEL), 0.01),
        'ffn2_w_in': nrm((DEPTH, D_MODEL, 2 * D_FF), D_MODEL ** -0.5),
        'ffn2_w_out': nrm((DEPTH, D_FF, D_MODEL), D_FF ** -0.5),
        'ada_w': nrm((DEPTH, D_MODEL, N_ADA * D_MODEL), 0.5 * D_MODEL ** -0.5),
        'ada_b': nrm((DEPTH, N_ADA * D_MODEL), 0.01),
        'ab_w_in': nrm((N_AB, D_MODEL, PROJ_AB), D_MODEL ** -0.5),
        'ab_gate_bias': jnp.concatenate([nrm((N_AB, H_A), 0.1), 3.0 + nrm((N_AB, H_A), 0.5)], axis=-1),
        'a_out_norm': 1.0 + nrm((N_AB, H_A, DH_A), 0.01),
        'b_q_norm': 1.0 + nrm((N_AB, DH_B), 0.01),
        'b_k_norm': 1.0 + nrm((N_AB, DH_B), 0.01),
        'b_rel_bias': nrm((N_AB, H_B, 2 * MAX_REL + 1), 0.1),
        'ab_w_out': nrm((N_AB, D_MIX, D_MODEL), D_MIX ** -0.5),
        'c_w_in': nrm((N_C, D_MODEL, 2 * R_WIDTH), D_MODEL ** -0.5),
        'c_conv_w': nrm((N_C, CONV_W, R_WIDTH), CONV_W ** -0.5),
        'c_conv_b': nrm((N_C, R_WIDTH), 0.01),
        'c_gate_w': nrm((N_C, N_BLOCKS_C, BW_C, 2 * BW_C), BW_C ** -0.5),
        'c_gate_b': nrm((N_C, 2, R_WIDTH), 0.01),
        'c_lambda': jnp.log(u_lam) - jnp.log1p(-u_lam),
        'c_w_out': nrm((N_C, R_WIDTH, D_MODEL), R_WIDTH ** -0.5),
    }


def reference(x_prompt, x_sample, state_a_C, state_a_n, state_a_m, cache_b_k, cache_b_v,
              state_c_conv, state_c_h, c_prompt, c_sample,
              ffn1_norm, ffn1_w_in, ffn1_w_out, mix_norm, ffn2_norm, ffn2_w_in, ffn2_w_out,
              ada_w, ada_b, ab_w_in, ab_gate_bias, a_out_norm, b_q_norm, b_k_norm, b_rel_bias,
              ab_w_out, c_w_in, c_conv_w, c_conv_b, c_gate_w, c_gate_b, c_lambda, c_w_out):
    p = dict(ffn1_norm=ffn1_norm, ffn1_w_in=ffn1_w_in, ffn1_w_out=ffn1_w_out, mix_norm=mix_norm,
             ffn2_norm=ffn2_norm, ffn2_w_in=ffn2_w_in, ffn2_w_out=ffn2_w_out, ada_w=ada_w, ada_b=ada_b,
             ab_w_in=ab_w_in, ab_gate_bias=ab_gate_bias, a_out_norm=a_out_norm, b_q_norm=b_q_norm,
             b_k_norm=b_k_norm, b_rel_bias=b_rel_bias, ab_w_out=ab_w_out, c_w_in=c_w_in,
             c_conv_w=c_conv_w, c_conv_b=c_conv_b, c_gate_w=c_gate_w, c_gate_b=c_gate_b,
             c_lambda=c_lambda, c_w_out=c_w_out)
    y_prompt, ps = run_trunk(x_prompt, c_prompt, p, None)
    st = dict(a_C=state_a_C, a_n=state_a_n, a_m=state_a_m, b_k=cache_b_k, b_v=cache_b_v,
              c_conv=state_c_conv, c_h=state_c_h)
    y_sample, ss = run_trunk(x_sample, c_sample, p, st)
    return (y_prompt, y_sample,
            ps['a_C'], ps['a_n'], ps['a_m'], ps['b_k'], ps['b_v'], ps['c_conv'], ps['c_h'],
            ss['a_C'], ss['a_n'], ss['a_m'], ss['b_k'], ss['b_v'], ss['c_conv'], ss['c_h'])
```

```python
import numpy as np
from contextlib import ExitStack
import concourse.bass as bass
import concourse.mybir as mybir
from concourse.bass_utils import run_bass_kernel_spmd

F32 = mybir.dt.float32
BF16 = mybir.dt.bfloat16
AF = mybir.ActivationFunctionType
ALU = mybir.AluOpType

D = 1024
NCH = 8
DFF = 2816
NJ = 22
SEQ = 2048
TP = 1024
TS = 64
NSQ = 4
LS = 16
TMAX = TP + TS
EPS = 1e-6
SLOT = 1024
NST = 4
NBF = 4
LOOK_D = 3
LOOK_C = 2
FW = 1104
FO_SPLIT = ((0, 8), (8, 15), (15, 22))
NEG = -30000.0

DEBUG = {}


class Tile:
    __slots__ = ("ap", "w", "r", "name")

    def __init__(self, ap, name=""):
        self.ap = ap
        self.w = None
        self.r = {}
        self.name = name


class TG:
    def __init__(self, ap, name=""):
        self.ap = ap
        self.name = name
        self.parts = [Tile(ap, name + "_p%d" % i) for i in range(3)]


class View(TG):
    def __init__(self, ap, parts, name=""):
        self.ap = ap
        self.name = name
        self.parts = list(parts)


def _flat(items):
    out = []
    for t in items:
        if isinstance(t, TG):
            out.extend(t.parts)
        elif isinstance(t, (list, tuple)):
            out.extend(_flat(t))
        else:
            out.append(t)
    return out


class Eng:
    def __init__(self, name, e, sem, step=1):
        self.name = name
        self.e = e
        self.sem = sem
        self.cnt = 0
        self.seen = {}
        self.step = step


class Builder:
    def __init__(self):
        self.nc = bass.Bass("TRN2", target_bir_lowering=False)
        self.es = ExitStack()
        nc = self.nc
        self.pe = Eng("pe", nc.tensor, self.sem("s_pe"))
        self.act = Eng("act", nc.scalar, self.sem("s_act"))
        self.dve = Eng("dve", nc.vector, self.sem("s_dve"))
        self.pool = Eng("pool", nc.gpsimd, self.sem("s_pool"))
        self.sp = Eng("sp", nc.sync, None)
        self.chans = [Eng("ch%d" % i, None, self.sem("s_ch%d" % i), 16) for i in range(12)]
        self.chan_i = 0
        self.units = []
        self.unit_keys = {}
        self.n_sb = 0
        self.dry = False

    def sem(self, name):
        return self.es.enter_context(self.nc.semaphore(name))

    def sb(self, shape, dtype, name=None):
        self.n_sb += 1
        t = self.es.enter_context(self.nc.sbuf_tensor("t_" + (name or ("sb%d" % self.n_sb)), list(shape), dtype))
        return t

    def tgroup(self, shape, dtype, name=None):
        t = self.sb(shape, dtype, name)
        return TG(t[tuple(slice(None) for _ in shape)], name or "")

    def tile(self, shape, dtype, name=None):
        t = self.sb(shape, dtype, name)
        return Tile(t[tuple(slice(None) for _ in shape)], name or "")

    def _waits(self, eng, reads, writes):
        need = {}

        def req(wc):
            if wc is None:
                return
            who, c = wc
            if need.get(who, 0) < c:
                need[who] = c
        for t in reads:
            req(t.w)
        for t in writes:
            req(t.w)
            for who, c in t.r.items():
                req((who, c))
        for who, c in need.items():
            if who is eng and eng is self.pe:
                continue
            if eng.seen.get(who, 0) >= c:
                continue
            assert c <= who.cnt, ("wait on a not-yet-emitted increment", eng.name, who.name, c, who.cnt)
            eng.e.wait_ge(who.sem, c)
            eng.seen[who] = c

    def op(self, eng, fn, reads=(), writes=(), inc=True):
        if self.dry:
            return None
        reads = _flat(reads)
        writes = _flat(writes)
        self._waits(eng, reads, writes)
        ins = fn()
        if inc:
            eng.cnt += 1
            ins.then_inc(eng.sem, 1)
            mark = eng.cnt
        else:
            mark = eng.cnt + 1
        for t in writes:
            t.w = (eng, mark)
            t.r = {}
        for t in reads:
            if t.r.get(eng, 0) < mark:
                t.r[eng] = mark
        return ins

    def dma(self, out_ap, in_ap, reads=(), writes=(), queue=None, chan=None):
        q = queue or self.sp
        if self.dry:
            return None
        reads = _flat(reads)
        writes = _flat(writes)
        if chan is None:
            chan = self.chans[self.chan_i % len(self.chans)]
            self.chan_i += 1
        if chan.cnt > 0 and q.seen.get(chan, 0) < chan.cnt:
            q.e.wait_ge(chan.sem, chan.cnt)
            q.seen[chan] = chan.cnt
        self._waits(q, reads, writes)
        ins = q.e.dma_start(out=out_ap, in_=in_ap)
        chan.cnt += 16
        ins.then_inc(chan.sem, 16)
        for t in writes:
            t.w = (chan, chan.cnt)
            t.r = {}
        for t in reads:
            t.r[chan] = chan.cnt
        return chan

    def finish(self):
        if self.dry:
            return
        for ch in self.chans + list(self.stream_ch):
            if ch.cnt > 0:
                self.sp.e.wait_ge(ch.sem, ch.cnt)
        for e in (self.pe, self.act, self.dve, self.pool):
            if e.cnt > 0:
                self.sp.e.wait_ge(e.sem, e.cnt)

    def mm(self, out_t, out_ap, l_t, l_ap, r_t, r_ap, start, stop, inc=None):
        if inc is None:
            inc = stop
        return self.op(self.pe, lambda: self.nc.tensor.matmul(out_ap, lhsT=l_ap, rhs=r_ap, start=start, stop=stop),
                       reads=[l_t, r_t], writes=[out_t], inc=inc)

    def A(self, out_t, out_ap, in_t, in_ap, func, bias=0.0, scale=1.0, extra=()):
        return self.op(self.act, lambda: self.nc.scalar.activation(out=out_ap, in_=in_ap, func=func, bias=bias, scale=scale),
                       reads=[in_t] + list(extra), writes=[out_t])

    def V(self, eng, meth, out_t, reads, **kw):
        e = eng.e
        outs = list(out_t) if isinstance(out_t, (list, tuple)) else [out_t]
        return self.op(eng, lambda: getattr(e, meth)(**kw), reads=list(reads), writes=outs)

    def plan_unit(self, key, n, cast=True, src="w"):
        assert n <= SLOT, (key, n)
        self.units.append(dict(key=key, n=n, cast=cast, src=src))

    def stream_setup(self, wdram, cdram, offs):
        self.stream_ch = [Eng("st%d" % i, None, self.sem("s_st%d" % i), 16) for i in range(NST)]
        self.st_tiles = [self.tile([128, SLOT], F32, "stg%d" % i) for i in range(NST)]
        self.bf_tiles = [self.tile([128, SLOT], BF16, "wbf%d" % i) for i in range(NBF)]
        self.wdram = wdram
        self.cdram = cdram
        self.offs = offs
        self.s_dma = 0
        self.s_cast = 0
        self.s_next = 0

    def _issue_dma(self, k):
        u = self.units[k]
        st = self.st_tiles[k % NST]
        src = self.wdram if u["src"] == "w" else self.cdram
        off = self.offs[(u["src"], u["key"])]
        self.dma(st.ap[:, 0:u["n"]], src[:, off:off + u["n"]], writes=[st], chan=self.stream_ch[k % NST])

    def _issue_cast(self, k):
        u = self.units[k]
        if not u["cast"]:
            return
        st = self.st_tiles[k % NST]
        bf = self.bf_tiles[k % NBF]
        n = u["n"]
        if u["key"][0] in ("f_in", "f_out", "ada"):
            self.n_cast = getattr(self, "n_cast", 0) + 1
            if self.n_cast % 2:
                self.op(self.act, lambda: self.nc.scalar.activation(out=bf.ap[:, 0:n], in_=st.ap[:, 0:n], func=AF.Copy), reads=[st], writes=[bf])
            else:
                self.op(self.dve, lambda: self.nc.vector.tensor_copy(out=bf.ap[:, 0:n], in_=st.ap[:, 0:n]), reads=[st], writes=[bf])
        else:
            self.op(self.act, lambda: self.nc.scalar.activation(out=bf.ap[:, 0:n], in_=st.ap[:, 0:n], func=AF.Copy), reads=[st], writes=[bf])

    def get(self, key):
        if self.dry:
            kind = key[0]
            n = {"ada": 1024, "f_in": 1024, "ab_gate": 64, "ab_in": 1024, "ab_out": 1024, "c_in": 1024, "c_gate": 256, "c_out": 1024, "kc": 512, "vc": 512}.get(kind)
            if kind == "f_out":
                j0, j1 = FO_SPLIT[key[4]]
                n = (j1 - j0) * 128
            self.plan_unit(key, n, src=("c" if kind in ("kc", "vc") else "w"))
            return self.bf_tiles[0]
        k = self.s_next
        u = self.units[k]
        assert u["key"] == key, (u["key"], key)
        last = len(self.units) - 1
        while self.s_dma <= min(k + LOOK_D, last):
            self._issue_dma(self.s_dma)
            self.s_dma += 1
        while self.s_cast <= min(k + LOOK_C, last):
            self._issue_cast(self.s_cast)
            self.s_cast += 1
        self.s_next += 1
        t = self.bf_tiles[k % NBF] if u["cast"] else self.st_tiles[k % NST]
        return t


class Prog(Builder):
    def __init__(self):
        super().__init__()
        self.declare_dram()
        self.alloc()
        self.dry = True
        self.build_body()
        self.dry = False
        self.ps_i = 0
        self.n_cast = 0
        self.offs = {}
        self.wtotal = 0
        self.ctotal = 0
        for u in self.units:
            k = (u["src"], u["key"])
            if k in self.offs:
                continue
            if u["src"] == "w":
                self.offs[k] = self.wtotal
                self.wtotal += u["n"]
            else:
                self.offs[k] = self.ctotal
                self.ctotal += u["n"]
        self.d_w = self.nc.dram_tensor("wst", [128, max(self.wtotal, 1)], F32, kind="ExternalInput").ap()
        self.d_c = self.nc.dram_tensor("cst", [128, max(self.ctotal, 1)], F32, kind="ExternalInput").ap()
        self.wdram, self.cdram = self.d_w, self.d_c

    def pump(self, n):
        lst = getattr(self, "ada_pending", None)
        for _ in range(n):
            if not lst:
                return
            l, cbk = lst.pop(0)
            self.ada_block(l, cbk)

    PV = dict(g0=0, g1=16, g2=32, aon=48, gq=52, gk=53, gbias=54, convw=62, convb=94, gateb=102, lam=118)
    NPV = 126
    SV = dict(an=0, am=16, conv=32, ch=128)
    NSV = 160

    def declare_dram(self):
        nc = self.nc
        di = lambda n, s: nc.dram_tensor(n, list(s), F32, kind="ExternalInput").ap()
        do = lambda n, s: nc.dram_tensor(n, list(s), F32, kind="ExternalOutput").ap()
        self.d_xT = di("xT", [128, NCH, SEQ + TS])
        self.d_cT = di("cT", [128, NCH, 5])
        self.d_pv = di("pv", [128, self.NPV])
        self.d_sv = di("sv", [128, self.NSV])
        self.d_adab = di("adab", [128, 2, 72])
        self.d_bias = di("biasT", [4, 128, 2, 640])
        self.d_aC = di("aC", [16, 128, 128])
        self.o_yT = do("o_yT", [128, NCH, SEQ + TS])
        self.o_paC = do("o_paC", [4, 128, 128])
        self.o_pan = do("o_pan", [128, 4])
        self.o_pam = do("o_pam", [1, 4])
        self.o_pbk = do("o_pbk", [128, 4, 512])
        self.o_pbv = do("o_pbv", [4, 128, 512])
        self.o_pconv = do("o_pconv", [128, NCH, 3])
        self.o_pch = do("o_pch", [128, NCH])
        self.o_saC = do("o_saC", [16, 128, 128])
        self.o_san = do("o_san", [128, 16])
        self.o_sam = do("o_sam", [1, 16])
        self.o_sbk = do("o_sbk", [128, 4, TS])
        self.o_sbv = do("o_sbv", [NSQ, LS, 512])
        self.o_sconv = do("o_sconv", [128, NCH, NSQ, 3])
        self.o_sch = do("o_sch", [128, NCH, NSQ])
        if DEBUG.get("dbg"):
            self.o_dbg = do("o_dbg", [128, NCH, TMAX])

    def alloc(self):
        nc = self.nc
        self.x = [self.tile([128, TMAX], F32, "x%d" % c) for c in range(NCH)]
        self.h = [self.tgroup([128, TMAX], BF16, "h%d" % c) for c in range(NCH)]
        self.a = []
        self.av32 = []
        for j in range(NJ // 2):
            t = self.sb([128, 2, TMAX], BF16, "apair%d" % j)
            g0, g1 = TG(t[:, 0, :], "a%d" % (2 * j)), TG(t[:, 1, :], "a%d" % (2 * j + 1))
            self.a += [g0, g1]
            self.av32.append(View(t[:, :, :].rearrange("p a b -> p (a b)").bitcast(F32), g0.parts + g1.parts, "av%d" % j))
        self.F = [self.tgroup([128, FW], F32, "F%d" % i) for i in range(7)]
        self.sg = [self.tile([128, 512], F32, "sg%d" % i) for i in range(2)]
        self.ps2 = [self.es.enter_context(nc.psum_tensor("ps%d" % i, [128, 1024], F32)) for i in range(4)]
        self.ps = []
        for i in range(4):
            self.ps.append(Tile(self.ps2[i][:, 0:512], "psb%d" % (2 * i)))
            self.ps.append(Tile(self.ps2[i][:, 512:1024], "psb%d" % (2 * i + 1)))
        self.ps_i = 0
        self.stream_setup(None, None, None)
        self.pv = self.tile([128, self.NPV], F32, "pv")
        self.sv = self.tile([128, self.NSV], F32, "sv")
        self.cT = self.tile([128, NCH, 5], F32, "cT")
        self.cTb = self.tile([128, NCH, 5], BF16, "cTb")
        self.cst = self.tile([128, 8], F32, "cst")
        self.ones_bf = self.tile([128, 128], BF16, "ones_bf")
        self.blk_bf = self.tile([128, 128], BF16, "blk_bf")
        self.ones_w = self.tile([128, TP], BF16, "ones_w")
        self.ident = self.tile([128, 128], F32, "ident")
        self.adaT = [self.tile([128, 72, 5], F32, "adaT%d" % l) for l in range(2)]
        self.adatok = [self.F[4], self.F[5]]
        self.adabT = self.tile([128, 2, 72], F32, "adabT")
        self.gs = [[self.tile([128, NCH, 5], F32, "gs%d_%d" % (l, m)) for m in range(3)] for l in range(2)]
        self.gt = [[self.tile([128, NCH, 5], F32, "gt%d_%d" % (l, m)) for m in range(3)] for l in range(2)]
        self.GS_tok = self.tile([128, NCH, TS], F32, "GS_tok")
        self.SH_tok = self.tile([128, NCH, TS], F32, "SH_tok")
        self.GT_tok = self.tile([128, NCH, TS], F32, "GT_tok")
        self.small = self.tile([128, 64], F32, "small")
        self.C = [self.tile([128, 128], F32, "C%d" % i) for i in range(4)]
        self.Cb = [self.tile([128, 128], BF16, "Cb%d" % i) for i in range(4)]
        self.nr = [self.tile([128, 128], F32, "nr%d" % i) for i in range(4)]
        self.nrb = [self.tile([128, 128], BF16, "nrb%d" % i) for i in range(4)]
        self.Gl = self.tile([128, 4], F32, "Gl")
        self.Ml = self.tile([128, 4], F32, "Ml")
        self.Cs4 = self.tile([128, NSQ, 128], F32, "Cs4")
        self.Cs4b = self.tile([128, NSQ, 128], BF16, "Cs4b")
        self.nr4b = self.tile([128, NSQ, 128], BF16, "nr4b")
        self.kw4 = self.tile([16, NSQ * 128], BF16, "kw4")
        self.cs4 = self.tile([128, 32], F32, "cs4")
        self.gw = self.tile([128, 64], BF16, "gw")
        self.wrep = self.tile([128, 1024], BF16, "wrep")
        self.cols = self.tile([128, 16], F32, "cols")
        self.m128 = [self.tile([128, 128], F32, "m128_%d" % i) for i in range(4)]
        self.b128 = [self.tile([128, 128], BF16, "b128_%d" % i) for i in range(3)]
        self.tokb = [self.tile([16, 512], BF16, "tokb%d" % i) for i in range(4)]
        self.tokf = self.sg[1]
        self.maskb = self.tile([128, 128], F32, "maskb")
        self.c8 = self.tgroup([128, 64], F32, "c8")
        self.outs = self.tile([128, 64], F32, "outs")
        self.kband = [self.tile([128, 512], BF16, "kband%d" % i) for i in range(4)]
        self.vband = self.tile([128, 4, 512], BF16, "vband")
        self.biasT = self.tile([128, 2, 640], F32, "biasTs")
        self.convc = self.tile([128, NCH, 3], F32, "convc")
        self.hc = self.tile([128, NCH], F32, "hc")
        self.XP = self.F[6]
        self.sconv = self.tile([128, NCH, NSQ, 3], F32, "sconv")
        self.sch = self.tile([128, NCH, NSQ], F32, "sch")

    def next_ps(self):
        if self.ps_i == 0 and DEBUG.get("verbose"):
            print("sbuf bytes remaining", self.nc.sbuf_bytes_remaining())
        t = self.ps[self.ps_i % 8]
        self.ps_i += 1
        return t

    def next_ps2(self):
        if self.ps_i % 2:
            self.ps_i += 1
        i = (self.ps_i % 8) // 2
        self.ps_i += 2
        return self.ps[2 * i], self.ps[2 * i + 1], self.ps2[i]

    def pvc(self, name, i=0, n=1):
        o = self.PV[name] + i
        return self.pv.ap[:, o:o + n]

    def svc(self, name, i=0, n=1):
        o = self.SV[name] + i
        return self.sv.ap[:, o:o + n]

    def cbs(self, pi):
        r = [(0, 512), (512, 1024)]
        if pi == 1:
            r.append((TP, TP + TS))
        return r

    def setup(self):
        nc = self.nc
        q = self.act
        self.dma(self.pv.ap[:, :], self.d_pv, writes=[self.pv], queue=q)
        self.dma(self.sv.ap[:, :], self.d_sv, writes=[self.sv], queue=q)
        self.dma(self.cT.ap[:, :, :], self.d_cT, writes=[self.cT], queue=q)
        self.dma(self.adabT.ap[:, :, :], self.d_adab, writes=[self.adabT], queue=q)
        P, D_ = self.pool, self.dve
        self.A(self.cTb, self.cTb.ap[:, :, :], self.cT, self.cT.ap[:, :, :], AF.Copy)
        c = self.cst
        for i, v in enumerate([EPS, 1.0, 0.5, 0.0]):
            self.V(P, "memset", c, [], ap=c.ap[:, i:i + 1], constant=v)
        self.V(P, "memset", self.ones_bf, [], ap=self.ones_bf.ap[:, :], constant=1.0)
        self.V(P, "memset", self.ones_w, [], ap=self.ones_w.ap[:, :], constant=1.0)
        self.V(P, "memset", self.blk_bf, [], ap=self.blk_bf.ap[:, :], constant=0.0)
        self.V(P, "memset", self.blk_bf, [], ap=self.blk_bf.ap[0:64, 0:64], constant=1.0)
        self.V(P, "memset", self.blk_bf, [], ap=self.blk_bf.ap[64:128, 64:128], constant=1.0)
        self.reg_neg = None if self.dry else nc.gpsimd.to_reg(NEG)
        self.V(P, "memset", self.ident, [], ap=self.ident.ap[:, :], constant=1.0)
        self.V(P, "affine_select", self.ident, [self.ident], out=self.ident.ap[:, :], in_=self.ident.ap[:, :],
               pattern=[[1, 128]], compare_op=ALU.is_equal, fill=0.0, base=0, channel_multiplier=-1)
        self.V(P, "memset", self.maskb, [], ap=self.maskb.ap[:, :], constant=0.0)
        self.V(P, "affine_select", self.maskb, [self.maskb], out=self.maskb.ap[:, :], in_=self.maskb.ap[:, :],
               pattern=[[1, 128]], compare_op=ALU.is_ge, fill=self.reg_neg, base=0, channel_multiplier=-1)
        for t in (self.convc, self.hc, self.Gl, self.Ml):
            self.V(P, "memset", t, [], ap=t.ap, constant=0.0)
        for hd in range(4):
            for t in (self.C[hd], self.nr[hd]):
                self.V(P, "memset", t, [], ap=t.ap[:, :], constant=0.0)
            for t in (self.Cb[hd], self.nrb[hd]):
                self.V(P, "memset", t, [], ap=t.ap[:, :], constant=0.0)
        s = self.small
        self.A(s, s.ap[:, 0:8], self.pv, self.pvc("lam", 0, 8), AF.Exp, scale=-1.0)
        self.A(s, s.ap[:, 0:8], s, s.ap[:, 0:8], AF.Ln, bias=self.cst.ap[:, 1:2], extra=[self.cst])
        self.V(D_, "tensor_scalar", s, [s], out=s.ap[:, 0:8], in0=s.ap[:, 0:8], scalar1=-4.0, scalar2=None, op0=ALU.mult)
        self.V(D_, "tensor_scalar", s, [self.pv], out=s.ap[:, 8:24], in0=self.pvc("gateb", 0, 16), scalar1=0.5, scalar2=None, op0=ALU.mult)
        self.V(D_, "tensor_scalar", s, [self.pv], out=s.ap[:, 24:28], in0=self.pvc("aon", 0, 4), scalar1=0.5, scalar2=None, op0=ALU.mult)
        self.V(D_, "tensor_scalar", s, [self.pv], out=s.ap[:, 28:29], in0=self.pvc("gq"), scalar1=0.125, scalar2=None, op0=ALU.mult)
        self.V(D_, "tensor_scalar", s, [self.pv], out=s.ap[:, 30:38], in0=self.pvc("gbias", 0, 8), scalar1=-1.0, scalar2=None, op0=ALU.mult)

    def ada_block(self, l, cbk):
        nc = self.nc
        D_ = self.dve
        aT = self.adaT[l]
        ps = self.next_ps()
        for kq in range(4):
            w = self.get(("ada", l, cbk, kq))
            for i in range(2):
                k = 2 * kq + i
                self.mm(ps, ps.ap[0:5, 0:512], self.cTb, self.cTb.ap[:, k, :], w, w.ap[:, i * 512:(i + 1) * 512],
                        start=(k == 0), stop=(k == 7), inc=(i == 1))
        tok = self.adatok[cbk % 2]
        self.A(tok, tok.ap[0:5, 0:512], ps, ps.ap[0:5, 0:512], AF.Copy)
        ps2 = self.next_ps()
        for i in range(4):
            self.op(self.pe, lambda i=i: nc.tensor.transpose(ps2.ap[:, i * 8:i * 8 + 5], tok.ap[0:5, i * 128:(i + 1) * 128], self.ident.ap[0:5, 0:5]),
                    reads=[tok, self.ident], writes=[ps2], inc=(i == 3))
        self.V(D_, "tensor_tensor", aT, [ps2, self.adabT], out=aT.ap[:, cbk * 4:(cbk + 1) * 4, :],
               in0=ps2.ap[:, 0:32].rearrange("p (a b) -> p a b", b=8)[:, :, 0:5],
               in1=self.adabT.ap[:, l, cbk * 4:(cbk + 1) * 4].unsqueeze(2).to_broadcast([128, 4, 5]), op=ALU.add)

    def ada_derive(self, l, ms=(0, 1, 2), scale=True, gate=True):
        D_ = self.dve
        aT = self.adaT[l]
        for m in ms:
            gname = ("g0", "g1", "g2")[m]
            gs, gt = self.gs[l][m], self.gt[l][m]
            if scale:
                self.V(D_, "tensor_scalar", gs, [aT], out=gs.ap[:, :, :], in0=aT.ap[:, (3 * m + 1) * 8:(3 * m + 2) * 8, :],
                       scalar1=1.0, scalar2=None, op0=ALU.add)
                self.V(D_, "tensor_tensor", gs, [gs, self.pv], out=gs.ap[:, :, :], in0=gs.ap[:, :, :],
                       in1=self.pvc(gname, l * 8, 8).unsqueeze(2).to_broadcast([128, NCH, 5]), op=ALU.mult)
            if gate:
                self.V(D_, "tensor_scalar", gt, [aT], out=gt.ap[:, :, :], in0=aT.ap[:, (3 * m + 2) * 8:(3 * m + 3) * 8, :],
                       scalar1=(1.0 if m == 1 else 0.5), scalar2=None, op0=ALU.mult)

    def shift_col(self, l, m, c, r=0):
        return self.adaT[l].ap[:, 3 * m * 8 + c, r:r + 1]

    def expand_tok(self, l, m):
        D_ = self.dve
        aT = self.adaT[l]
        for dst, src_t, src in ((self.GS_tok, self.gs[l][m], self.gs[l][m].ap[:, :, 1:5]),
                                (self.SH_tok, aT, aT.ap[:, 3 * m * 8:(3 * m + 1) * 8, 1:5]),
                                (self.GT_tok, self.gt[l][m], self.gt[l][m].ap[:, :, 1:5])):
            for c in range(NCH):
                self.V(D_, "tensor_copy", dst, [src_t], out=dst.ap[:, c, :].rearrange("p (s t) -> p s t", t=LS),
                       in_=src[:, c, :].unsqueeze(2).to_broadcast([128, NSQ, LS]))

    def norm_mod(self, l, m, pi):
        D_ = self.dve
        T = TP + (TS if pi == 1 else 0)
        if pi == 1:
            self.expand_tok(l, m)
        sq = [self.a[14 + c] for c in range(NCH)]
        for c in range(NCH):
            self.A(sq[c], sq[c].ap[:, 0:T], self.x[c], self.x[c].ap[:, 0:T], AF.Square)
        lnt, rstd = self.F[0], self.F[1]
        for gi, (c0, c1) in enumerate(self.cbs(pi)):
            ps = self.next_ps()
            for c in range(NCH):
                self.mm(ps, ps.ap[:, 0:c1 - c0], self.ones_bf, self.ones_bf.ap[:, :], sq[c], sq[c].ap[:, c0:c1], start=(c == 0), stop=(c == 7))
            self.A(lnt.parts[gi], lnt.ap[:, c0:c1], ps, ps.ap[:, 0:c1 - c0], AF.Ln, bias=self.cst.ap[:, 0:1], scale=1.0 / D, extra=[self.cst])
            self.A(rstd.parts[gi], rstd.ap[:, c0:c1], lnt.parts[gi], lnt.ap[:, c0:c1], AF.Exp, scale=-0.5)
        for gi, (c0, c1) in enumerate(self.cbs(pi)):
            for c in range(NCH):
                tmp = self.F[2 + (c % 2)]
                tp, hp_, rp = tmp.parts[gi], self.h[c].parts[gi], rstd.parts[gi]
                if c0 < TP:
                    self.V(D_, "scalar_tensor_tensor", tp, [self.x[c], self.gs[l][m], rp], out=tmp.ap[:, c0:c1], in0=self.x[c].ap[:, c0:c1],
                           scalar=self.gs[l][m].ap[:, c, 0:1], in1=rstd.ap[:, c0:c1], op0=ALU.mult, op1=ALU.mult)
                    self.A(hp_, self.h[c].ap[:, c0:c1], tp, tmp.ap[:, c0:c1], AF.Identity, bias=self.shift_col(l, m, c), extra=[self.adaT[l]])
                else:
                    self.V(D_, "tensor_tensor", tp, [self.x[c], rp], out=tmp.ap[:, TP:T], in0=self.x[c].ap[:, TP:T], in1=rstd.ap[:, TP:T], op=ALU.mult)
                    self.V(D_, "tensor_tensor", tp, [tp, self.GS_tok], out=tmp.ap[:, TP:T], in0=tmp.ap[:, TP:T], in1=self.GS_tok.ap[:, c, :], op=ALU.mult)
                    self.V(D_, "tensor_tensor", hp_, [tp, self.SH_tok], out=self.h[c].ap[:, TP:T], in0=tmp.ap[:, TP:T], in1=self.SH_tok.ap[:, c, :], op=ALU.add)

    def resid(self, l, m, pi, i, ps, c0, c1):
        D_ = self.dve
        x = self.x[i]
        if c0 < TP:
            e = min(c1, TP)
            self.V(D_, "scalar_tensor_tensor", x, [ps, self.gt[l][m], x], out=x.ap[:, c0:e], in0=ps.ap[:, 0:e - c0],
                   scalar=self.gt[l][m].ap[:, i, 0:1], in1=x.ap[:, c0:e], op0=ALU.mult, op1=ALU.add)
        if c1 > TP:
            b = max(c0, TP)
            tmp = self.sg[0]
            self.V(D_, "tensor_tensor", tmp, [ps, self.GT_tok], out=tmp.ap[:, 0:c1 - b], in0=ps.ap[:, b - c0:c1 - c0], in1=self.GT_tok.ap[:, i, b - TP:c1 - TP], op=ALU.mult)
            self.V(D_, "tensor_tensor", x, [tmp, x], out=x.ap[:, b:c1], in0=tmp.ap[:, 0:c1 - b], in1=x.ap[:, b:c1], op=ALU.add)

    def parts_of(self, tg, c0, c1):
        ps = []
        if c0 < 512:
            ps.append(tg.parts[0])
        if c1 > 512 and c0 < 1024:
            ps.append(tg.parts[1])
        if c1 > 1024:
            ps.append(tg.parts[2])
        return ps

    def ffn(self, l, f, pi):
        D_ = self.dve
        m = 0 if f == 1 else 2
        cbs = self.cbs(pi) if pi == 0 else [(0, 363), (363, 726), (726, TMAX)]
        self.norm_mod(l, m, pi)
        sgi = 0
        step = 0
        pumping = (pi == 0 and l == 0 and f == 1)
        for j in range(NJ):
            wg = self.get(("f_in", l, f, j, 0))
            pg = [self.next_ps() for _ in cbs]
            for ci, (c0, c1) in enumerate(cbs):
                for k in range(NCH):
                    self.mm(pg[ci], pg[ci].ap[:, 0:c1 - c0], wg, wg.ap[:, k * 128:(k + 1) * 128], self.parts_of(self.h[k], c0, c1), self.h[k].ap[:, c0:c1], start=(k == 0), stop=(k == 7))
            wu = self.get(("f_in", l, f, j, 1))
            pu = [self.next_ps() for _ in cbs]
            for ci, (c0, c1) in enumerate(cbs):
                for k in range(NCH):
                    self.mm(pu[ci], pu[ci].ap[:, 0:c1 - c0], wu, wu.ap[:, k * 128:(k + 1) * 128], self.parts_of(self.h[k], c0, c1), self.h[k].ap[:, c0:c1], start=(k == 0), stop=(k == 7))
            for ci, (c0, c1) in enumerate(cbs):
                sg = self.sg[sgi % 2]
                sgi += 1
                n = c1 - c0
                self.A(sg, sg.ap[:, 0:n], pg[ci], pg[ci].ap[:, 0:n], AF.Silu)
                self.V(D_, "tensor_tensor", self.parts_of(self.a[j], c0, c1), [sg, pu[ci]], out=self.a[j].ap[:, c0:c1], in0=sg.ap[:, 0:n], in1=pu[ci].ap[:, 0:n], op=ALU.mult)
            if pumping and step % 4 == 0:
                self.pump(1)
            step += 1
        if pumping:
            self.ada_derive(0, (0,), scale=False)
        for i in range(NCH):
            ps = [self.next_ps() for _ in cbs]
            for hf, (j0, j1) in enumerate(FO_SPLIT):
                w = self.get(("f_out", l, f, i, hf))
                for ci, (c0, c1) in enumerate(cbs):
                    for j in range(j0, j1):
                        self.mm(ps[ci], ps[ci].ap[:, 0:c1 - c0], w, w.ap[:, (j - j0) * 128:(j - j0 + 1) * 128], self.parts_of(self.a[j], c0, c1), self.a[j].ap[:, c0:c1],
                                start=(j == 0), stop=(j == NJ - 1), inc=(j == j1 - 1))
            for ci, (c0, c1) in enumerate(cbs):
                self.resid(l, m, pi, i, ps[ci], c0, c1)
            if pumping and step % 4 == 0:
                self.pump(1)
            step += 1
        if pumping:
            self.ada_derive(0, (1,))

    def load_x(self, pi):
        for c in range(NCH):
            self.dma(self.x[c].ap[:, 0:TP], self.d_xT[:, c, pi * TP:(pi + 1) * TP], writes=[self.x[c]], queue=self.act)
            if pi == 1:
                self.dma(self.x[c].ap[:, TP:TMAX], self.d_xT[:, c, SEQ:SEQ + TS], writes=[self.x[c]], queue=self.act)

    def store_x(self, pi, dst=None):
        dst = dst if dst is not None else self.o_yT
        for c in range(NCH):
            self.dma(dst[:, c, pi * TP:(pi + 1) * TP], self.x[c].ap[:, 0:TP], reads=[self.x[c]])
            if pi == 1:
                self.dma(dst[:, c, SEQ:SEQ + TS], self.x[c].ap[:, TP:TMAX], reads=[self.x[c]])

    def interleave(self, *gens):
        gens = list(gens)
        while gens:
            for g in list(gens):
                try:
                    next(g)
                except StopIteration:
                    gens.remove(g)

    def interleave_g(self, *gens):
        gens = list(gens)
        while gens:
            for g in list(gens):
                try:
                    next(g)
                    yield
                except StopIteration:
                    gens.remove(g)

    def mixer1_chunk(self, pi, n, bufs):
        D_ = self.dve
        T = TP + (TS if pi == 1 else 0)
        cbs = self.cbs(pi)
        XP = self.XP
        xs0 = 3 + TP
        XPs = XP.ap[:, xs0:xs0 + 76].rearrange("p (s t) -> p s t", t=19)
        small = self.small
        xc, tr, ti, av, tq, gbs, xcb = bufs
        hs = ti
        xcs = xc.ap[:, TP:TP + TS].rearrange("p (s t) -> p s t", t=LS)
        self.V(D_, "tensor_copy", XP, [self.convc], out=XP.ap[:, 0:3], in_=self.convc.ap[:, n, :])
        if pi == 1:
            self.V(D_, "tensor_copy", XP, [self.sv], out=XPs[:, :, 0:3], in_=self.svc("conv", n * 12, 12).rearrange("p (s t) -> p s t", t=3))
        yield
        w = self.get(("c_in", 0, n))
        for (c0, c1) in cbs:
            ps = self.next_ps()
            for k in range(NCH):
                self.mm(ps, ps.ap[:, 0:c1 - c0], w, w.ap[:, k * 128:(k + 1) * 128], self.h[k].parts[0 if c0 < 512 else (1 if c0 < 1024 else 2)], self.h[k].ap[:, c0:c1], start=(k == 0), stop=(k == 7))
            self.A(gbs, gbs.ap[:, c0:c1], ps, ps.ap[:, 0:c1 - c0], AF.Copy)
            self.A(tq, tq.ap[:, c0:c1], ps, ps.ap[:, 0:c1 - c0], AF.Square)
            yield
        w = self.get(("c_in", 1, n))
        for (c0, c1) in cbs:
            ps = self.next_ps()
            for k in range(NCH):
                self.mm(ps, ps.ap[:, 0:c1 - c0], w, w.ap[:, k * 128:(k + 1) * 128], self.h[k].parts[0 if c0 < 512 else (1 if c0 < 1024 else 2)], self.h[k].ap[:, c0:c1], start=(k == 0), stop=(k == 7))
            if c0 < TP:
                self.A(XP, XP.ap[:, 3 + c0:3 + c1], ps, ps.ap[:, 0:c1 - c0], AF.Copy)
            else:
                self.A(XP, XPs[:, :, 3:19], ps, ps.ap[:, 0:TS].rearrange("p (s t) -> p s t", t=LS), AF.Copy)
            yield

        def gelu_gen():
            self.V(D_, "tensor_scalar", tq, [tq], out=tq.ap[:, 0:T], in0=tq.ap[:, 0:T], scalar1=0.044715, scalar2=1.0, op0=ALU.mult, op1=ALU.add)
            yield
            self.V(D_, "tensor_tensor", tq, [tq, gbs], out=tq.ap[:, 0:T], in0=tq.ap[:, 0:T], in1=gbs.ap[:, 0:T], op=ALU.mult)
            yield
            yield
            self.A(tq, tq.ap[:, 0:T], tq, tq.ap[:, 0:T], AF.Tanh, scale=0.7978845608028654)
            yield
            yield
            self.V(D_, "scalar_tensor_tensor", tq, [tq, gbs], out=tq.ap[:, 0:T], in0=tq.ap[:, 0:T], scalar=1.0, in1=gbs.ap[:, 0:T], op0=ALU.add, op1=ALU.mult)
            yield

        half = []

        def xb_gen():
            cw = lambda j: self.pvc("convw", n * 4 + j)
            self.V(D_, "tensor_scalar", xc, [XP, self.pv], out=xc.ap[:, 0:TP], in0=XP.ap[:, 0:TP], scalar1=cw(0), scalar2=self.pvc("convb", n), op0=ALU.mult, op1=ALU.add)
            yield
            for j in range(1, 4):
                self.V(D_, "scalar_tensor_tensor", xc, [XP, self.pv, xc], out=xc.ap[:, 0:TP], in0=XP.ap[:, j:j + TP], scalar=cw(j), in1=xc.ap[:, 0:TP], op0=ALU.mult, op1=ALU.add)
                yield
            if pi == 1:
                self.V(D_, "tensor_scalar", xc, [XP, self.pv], out=xcs, in0=XPs[:, :, 0:LS], scalar1=cw(0), scalar2=self.pvc("convb", n), op0=ALU.mult, op1=ALU.add)
                for j in range(1, 4):
                    self.V(D_, "scalar_tensor_tensor", xc, [XP, self.pv, xc], out=xcs, in0=XPs[:, :, j:j + LS], scalar=cw(j), in1=xcs, op0=ALU.mult, op1=ALU.add)
                yield
            self.V(D_, "tensor_copy", self.convc, [XP], out=self.convc.ap[:, n, :], in_=XP.ap[:, TP:TP + 3])
            if pi == 1:
                self.V(D_, "tensor_copy", self.sconv, [XP], out=self.sconv.ap[:, n, :, :], in_=XPs[:, :, 16:19])
            self.A(xcb, xcb.ap[:, 0:T], xc, xc.ap[:, 0:T], AF.Copy)
            yield
            wg = self.get(("c_gate", n))
            for gi, dst in ((0, tr), (1, ti)):
                for (c0, c1) in cbs:
                    ps = self.next_ps()
                    self.mm(ps, ps.ap[:, 0:c1 - c0], wg, wg.ap[:, gi * 128:(gi + 1) * 128], xcb, xcb.ap[:, c0:c1], start=True, stop=True)
                    self.A(dst, dst.ap[:, c0:c1], ps, ps.ap[:, 0:c1 - c0], AF.Tanh, bias=small.ap[:, 8 + n * 2 + gi:9 + n * 2 + gi], scale=0.5, extra=[small])
                yield
            half.append(1)
            c1h = small.ap[:, n:n + 1]
            self.A(av, av.ap[:, 0:T], tr, tr.ap[:, 0:T], AF.Exp, bias=c1h, scale=c1h, extra=[small])
            yield
            self.V(D_, "scalar_tensor_tensor", tr, [av], out=tr.ap[:, 0:T], in0=av.ap[:, 0:T], scalar=0.99999994, in1=av.ap[:, 0:T], op0=ALU.min, op1=ALU.mult)
            yield
            self.A(tr, tr.ap[:, 0:T], tr, tr.ap[:, 0:T], AF.Sqrt, bias=self.cst.ap[:, 1:2], scale=-1.0, extra=[self.cst])
            self.V(D_, "scalar_tensor_tensor", ti, [ti, xc], out=ti.ap[:, 0:T], in0=ti.ap[:, 0:T], scalar=1.0, in1=xc.ap[:, 0:T], op0=ALU.add, op1=ALU.mult)
            yield
            self.V(D_, "scalar_tensor_tensor", ti, [ti, tr], out=ti.ap[:, 0:T], in0=ti.ap[:, 0:T], scalar=0.5, in1=tr.ap[:, 0:T], op0=ALU.mult, op1=ALU.mult)
            yield
            self.V(D_, "tensor_tensor_scan", hs, [av, ti, self.hc], out=hs.ap[:, 0:TP], data0=av.ap[:, 0:TP], data1=ti.ap[:, 0:TP],
                   initial=self.hc.ap[:, n:n + 1], op0=ALU.mult, op1=ALU.add)
            self.V(D_, "tensor_copy", self.hc, [hs], out=self.hc.ap[:, n:n + 1], in_=hs.ap[:, TP - 1:TP])
            if pi == 1:
                for sq in range(NSQ):
                    cs = TP + LS * sq
                    self.V(D_, "tensor_tensor_scan", hs, [av, ti, self.sv], out=hs.ap[:, cs:cs + LS], data0=av.ap[:, cs:cs + LS], data1=ti.ap[:, cs:cs + LS],
                           initial=self.svc("ch", n * 4 + sq), op0=ALU.mult, op1=ALU.add)
                self.V(D_, "tensor_copy", self.sch, [hs], out=self.sch.ap[:, n, :], in_=hs.ap[:, TP:TP + TS].rearrange("p (s t) -> p s t", t=LS)[:, :, LS - 1])
            yield
        sent = False
        for _ in self.interleave_g(xb_gen(), gelu_gen()):
            if half and not sent:
                sent = True
                yield "HALF"
            else:
                yield
        self.V(D_, "scalar_tensor_tensor", self.a[n], [tq, hs], out=self.a[n].ap[:, 0:T], in0=tq.ap[:, 0:T], scalar=0.5, in1=hs.ap[:, 0:T], op0=ALU.mult, op1=ALU.mult)
        yield

    def mixer1(self, pi):
        l, m = 1, 1
        self.norm_mod(l, m, pi)
        F = self.F
        V_ = self.av32
        sets = [(F[0], F[1], F[2], F[3], F[4], F[5], self.a[8]),
                (V_[5], V_[6], V_[7], V_[8], V_[9], V_[10], self.a[9])]
        gens = [self.mixer1_chunk(pi, n, sets[n % 2]) for n in range(NCH)]
        active = [gens[0]]
        nxt = 1
        want = False
        while active:
            for g in list(active):
                try:
                    tok = next(g)
                except StopIteration:
                    active.remove(g)
                    continue
                if tok == "HALF":
                    want = True
            if want and nxt < NCH and len(active) < 2:
                active.append(gens[nxt])
                nxt += 1
                want = False
            if not active and nxt < NCH:
                active.append(gens[nxt])
                nxt += 1
        self.out_proj(l, pi, "c_out")

    def out_proj(self, l, pi, kind):
        cbs = self.cbs(pi)
        for i in range(NCH):
            w = self.get((kind, i))
            for (c0, c1) in cbs:
                ps = self.next_ps()
                for k in range(NCH):
                    self.mm(ps, ps.ap[:, 0:c1 - c0], w, w.ap[:, k * 128:(k + 1) * 128], self.a[k], self.a[k].ap[:, c0:c1], start=(k == 0), stop=(k == 7))
                self.resid(l, 1, pi, i, ps, c0, c1)

    def proj_fm(self, w, pi, evac):
        for (c0, c1) in self.cbs(pi):
            ps = self.next_ps()
            for k in range(NCH):
                self.mm(ps, ps.ap[:, 0:c1 - c0], w, w.ap[:, k * 128:(k + 1) * 128], self.h[k].parts[0 if c0 < 512 else (1 if c0 < 1024 else 2)], self.h[k].ap[:, c0:c1], start=(k == 0), stop=(k == 7))
            evac(ps, c0, c1)

    def proj_tm(self, w, pi, evac, evac_s):
        for g in range(2):
            ps = self.next_ps()
            for t4 in range(4):
                tt_ = g * 4 + t4
                for k in range(NCH):
                    self.mm(ps, ps.ap[:, t4 * 128:(t4 + 1) * 128], self.h[k].parts[tt_ // 4], self.h[k].ap[:, tt_ * 128:(tt_ + 1) * 128], w, w.ap[:, k * 128:(k + 1) * 128],
                            start=(k == 0), stop=(k == 7))
            evac(ps, g)
        if pi == 1:
            ps = self.next_ps()
            for sq in range(NSQ):
                cs = TP + LS * sq
                for k in range(NCH):
                    self.mm(ps, ps.ap[0:LS, sq * 128:(sq + 1) * 128], self.h[k].parts[2], self.h[k].ap[:, cs:cs + LS], w, w.ap[:, k * 128:(k + 1) * 128],
                            start=(k == 0), stop=(k == 7))
            evac_s(ps)

    def mixer0(self, pi):
        l, m = 0, 1
        P = self.pool
        self.norm_mod(l, m, pi)
        w = self.get(("ab_gate",))
        self.V(P, "tensor_copy", self.gw, [w], out=self.gw.ap[:, 0:64], in_=w.ap[:, 0:64])
        for _ in self.mlstm_P(pi, 0):
            pass
        for hd in range(4):
            if pi == 0:
                self.pump(3)
            cg = self.mlstm_C(pi, hd)
            pg = self.mlstm_P(pi, hd + 1) if hd < 3 else None
            started = False
            for tok in cg:
                if tok == "P_OK":
                    started = True
                if started and pg is not None:
                    try:
                        next(pg)
                    except StopIteration:
                        pg = None
            if pg is not None:
                for _ in pg:
                    pass
        for _ in self.attn_pro(pi, 0):
            pass
        for hp in range(4):
            if pi == 0:
                self.pump(3)
            gens = [self.attn_loop(pi, hp)]
            if hp < 3:
                gens.append(self.attn_pro(pi, hp + 1))
            self.interleave(*gens)
            self.attn_sample(pi, hp)
        if pi == 0:
            self.pump(100)
            self.ada_derive(0, (2,))
            self.ada_derive(1)
        self.out_proj(l, pi, "ab_out")

    def mlstm_chunk(self, c0, L, k_t, k_ap, v_t, v_ap, Mp_t, Mp_ap, C, Cb, nr, nrb, qT, kT, ig, M, emt, hT):
        D_, P = self.dve, self.pool
        cols = self.cols
        m0, m1, m2, _ = self.m128
        Pb, qs, kw = self.b128
        acol = cols.ap[0:L, 1:2]
        self.V(D_, "scalar_tensor_tensor", [m0, cols], [ig, self.ident], out=m0.ap[0:L, 0:L], in0=ig.ap[0:L, c0:c0 + L], scalar=1.0,
               in1=self.ident.ap[0:L, 0:L], op0=ALU.mult, op1=ALU.mult, accum_out=acol)
        self.V(D_, "tensor_scalar", m0, [M, cols], out=m0.ap[0:L, 0:L], in0=M.ap[0:L, c0:c0 + L], scalar1=-1.0, scalar2=acol, op0=ALU.mult, op1=ALU.add)
        self.V(P, "affine_select", m0, [m0], out=m0.ap[0:L, 0:L], in_=m0.ap[0:L, 0:L], pattern=[[1, L]], compare_op=ALU.is_ge, fill=self.reg_neg,
               base=0, channel_multiplier=-1)
        self.A(m0, m0.ap[0:L, 0:L], m0, m0.ap[0:L, 0:L], AF.Exp)
        psS = self.next_ps()
        self.mm(psS, psS.ap[0:L, 0:L], kT, kT.ap[:, c0:c0 + L], qT, qT.ap[:, c0:c0 + L], start=True, stop=True)
        self.V(D_, "tensor_tensor", Pb, [psS, m0], out=Pb.ap[0:L, 0:L], in0=psS.ap[0:L, 0:L], in1=m0.ap[0:L, 0:L], op=ALU.mult)
        self.A(m1, m1.ap[:, 0:L], M, M.ap[:, c0:c0 + L], AF.Exp, bias=Mp_ap, scale=-1.0, extra=[Mp_t])
        self.V(D_, "tensor_tensor", qs, [qT, m1], out=qs.ap[:, 0:L], in0=qT.ap[:, c0:c0 + L], in1=m1.ap[:, 0:L], op=ALU.mult)
        psN = self.next_ps()
        self.mm(psN, psN.ap[:, 0:L], v_t, v_ap, Pb, Pb.ap[0:L, 0:L], start=True, stop=False, inc=True)
        self.mm(psN, psN.ap[:, 0:L], Cb, Cb.ap[:, :], qs, qs.ap[:, 0:L], start=False, stop=True)
        psD = self.next_ps()
        self.mm(psD, psD.ap[:, 0:L], self.ones_bf, self.ones_bf.ap[0:L, :], Pb, Pb.ap[0:L, 0:L], start=True, stop=False, inc=True)
        self.mm(psD, psD.ap[:, 0:L], nrb, nrb.ap[:, :], qs, qs.ap[:, 0:L], start=False, stop=True)
        self.A(m2, m2.ap[:, 0:L], psD, psD.ap[:, 0:L], AF.Abs)
        self.V(D_, "tensor_tensor", m2, [m2, emt], out=m2.ap[:, 0:L], in0=m2.ap[:, 0:L], in1=emt.ap[:, c0:c0 + L], op=ALU.max)
        self.V(D_, "reciprocal", m2, [m2], out=m2.ap[:, 0:L], in_=m2.ap[:, 0:L])
        self.V(D_, "tensor_tensor", hT, [psN, m2], out=hT.ap[:, c0:c0 + L], in0=psN.ap[:, 0:L], in1=m2.ap[:, 0:L], op=ALU.mult)
        self.V(D_, "tensor_tensor", cols, [M, cols], out=cols.ap[0:L, 2:3], in0=M.ap[0:L, c0 + L - 1:c0 + L], in1=acol, op=ALU.subtract)
        self.A(cols, cols.ap[0:L, 3:4], cols, cols.ap[0:L, 2:3], AF.Exp, scale=-1.0)
        self.A(cols, cols.ap[:, 4:5], M, M.ap[:, c0 + L - 1:c0 + L], AF.Exp, bias=Mp_ap, scale=-1.0, extra=[Mp_t])
        self.V(D_, "tensor_scalar", kw, [k_t, cols], out=kw.ap[0:L, :], in0=k_ap, scalar1=cols.ap[0:L, 3:4], scalar2=None, op0=ALU.mult)
        psC = self.next_ps()
        self.mm(psC, psC.ap[:, 0:128], kw, kw.ap[0:L, :], v_t, v_ap, start=True, stop=True)
        self.V(D_, "scalar_tensor_tensor", C, [C, cols, psC], out=C.ap[:, :], in0=C.ap[:, :], scalar=cols.ap[:, 4:5], in1=psC.ap[:, 0:128], op0=ALU.mult, op1=ALU.add)
        self.A(Cb, Cb.ap[:, :], C, C.ap[:, :], AF.Copy)
        psNn = self.next_ps()
        self.mm(psNn, psNn.ap[:, 0:128], kw, kw.ap[0:L, :], self.ones_bf, self.ones_bf.ap[0:L, :], start=True, stop=True)
        self.V(D_, "scalar_tensor_tensor", nr, [nr, cols, psNn], out=nr.ap[:, :], in0=nr.ap[:, :], scalar=cols.ap[:, 4:5], in1=psNn.ap[:, 0:128], op0=ALU.mult, op1=ALU.add)
        self.A(nrb, nrb.ap[:, :], nr, nr.ap[:, :], AF.Copy)

    def mlstm_sample(self, hd, ig_t, G_t, M_t, emt_t, hT_t, qT_t, kT_t, tkb, tvb, ig, G, M, emt, hT, qT, kT):
        D_ = self.dve
        outs = self.outs
        cs4 = self.cs4
        c = cs4.ap
        acol4, tcol4, wcol4, dec4, mend4, nnew4 = c[0:LS, 0:4], c[0:LS, 4:8], c[0:LS, 8:12], c[:, 12:16], c[:, 16:20], c[:, 20:24]
        W16 = self.m128[0]
        iw = self.m128[1]
        dn = self.m128[2]
        P16, qs, _ = self.b128
        Cs4, Cs4b, nr4b, kw4 = self.Cs4, self.Cs4b, self.nr4b, self.kw4
        S0 = TP
        sl = slice(S0, S0 + TS)
        m0c = self.svc("am", 0, 16).rearrange("p (s h) -> p s h", h=4)[:, :, hd]
        n0c = self.svc("an", 0, 16).rearrange("p (s h) -> p s h", h=4)[:, :, hd]
        self.dma(Cs4.ap[:, :, :], self.d_aC.rearrange("(s h) p d -> h p s d", h=4)[hd], writes=[Cs4], queue=self.act)
        self.A(Cs4b, Cs4b.ap[:, :, :], Cs4, Cs4.ap[:, :, :], AF.Copy)
        self.V(D_, "tensor_copy", nr4b, [self.sv], out=nr4b.ap[:, :, :], in_=n0c.unsqueeze(2).to_broadcast([128, NSQ, 128]))
        for sq in range(NSQ):
            cs = S0 + LS * sq
            self.V(D_, "scalar_tensor_tensor", [W16, cs4], [ig_t, self.ident], out=W16.ap[0:LS, 0:LS], in0=ig[0:LS, cs:cs + LS], scalar=1.0,
                   in1=self.ident.ap[0:LS, 0:LS], op0=ALU.mult, op1=ALU.mult, accum_out=c[0:LS, sq:sq + 1])
        Mv16 = M[0:LS, sl].rearrange("p (s l) -> p s l", l=LS)
        Wv = W16.ap[0:LS, 0:TS].rearrange("p (s l) -> p s l", l=LS)
        self.V(D_, "tensor_tensor", W16, [cs4, M_t], out=Wv, in0=acol4.unsqueeze(2).to_broadcast([LS, NSQ, LS]), in1=Mv16, op=ALU.subtract)
        self.V(D_, "tensor_tensor", W16, [W16, self.maskb], out=Wv, in0=Wv, in1=self.maskb.ap[0:LS, 0:LS].unsqueeze(1).to_broadcast([LS, NSQ, LS]), op=ALU.add)
        self.A(W16, W16.ap[0:LS, 0:TS], W16, W16.ap[0:LS, 0:TS], AF.Exp)
        psS = self.next_ps()
        for sq in range(NSQ):
            cs = S0 + LS * sq
            self.mm(psS, psS.ap[0:LS, sq * LS:(sq + 1) * LS], kT_t, kT[:, cs:cs + LS], qT_t, qT[:, cs:cs + LS], start=True, stop=True)
        self.V(D_, "tensor_tensor", P16, [psS, W16], out=P16.ap[0:LS, 0:TS], in0=psS.ap[0:LS, 0:TS], in1=W16.ap[0:LS, 0:TS], op=ALU.mult)
        Mv = M[:, sl].rearrange("p (s l) -> p s l", l=LS)
        iwv = iw.ap[:, 0:TS].rearrange("p (s l) -> p s l", l=LS)
        self.V(D_, "tensor_tensor", iw, [self.sv, M_t], out=iwv, in0=m0c.unsqueeze(2).to_broadcast([128, NSQ, LS]), in1=Mv, op=ALU.subtract)
        self.A(iw, iw.ap[:, 0:TS], iw, iw.ap[:, 0:TS], AF.Exp)
        self.V(D_, "tensor_tensor", qs, [qT_t, iw], out=qs.ap[:, 0:TS], in0=qT[:, sl], in1=iw.ap[:, 0:TS], op=ALU.mult)
        psN, psD = self.next_ps(), self.next_ps()
        for sq in range(NSQ):
            o = slice(sq * LS, (sq + 1) * LS)
            self.mm(psN, psN.ap[:, o], tvb, tvb.ap[0:LS, sq * 128:(sq + 1) * 128], P16, P16.ap[0:LS, o], start=True, stop=False, inc=True)
            self.mm(psN, psN.ap[:, o], Cs4b, Cs4b.ap[:, sq, :], qs, qs.ap[:, o], start=False, stop=True)
        for sq in range(NSQ):
            o = slice(sq * LS, (sq + 1) * LS)
            self.mm(psD, psD.ap[:, o], self.ones_bf, self.ones_bf.ap[0:LS, :], P16, P16.ap[0:LS, o], start=True, stop=False, inc=True)
            self.mm(psD, psD.ap[:, o], nr4b, nr4b.ap[:, sq, :], qs, qs.ap[:, o], start=False, stop=True)
        self.A(dn, dn.ap[:, 0:TS], psD, psD.ap[:, 0:TS], AF.Abs)
        self.V(D_, "tensor_tensor", dn, [dn, emt_t], out=dn.ap[:, 0:TS], in0=dn.ap[:, 0:TS], in1=emt[:, sl], op=ALU.max)
        self.A(dn, dn.ap[:, 0:TS], dn, dn.ap[:, 0:TS], AF.Ln)
        self.A(dn, dn.ap[:, 0:TS], dn, dn.ap[:, 0:TS], AF.Exp, scale=-1.0)
        self.V(D_, "tensor_tensor", hT_t, [psN, dn], out=hT[:, sl], in0=psN.ap[:, 0:TS], in1=dn.ap[:, 0:TS], op=ALU.mult)
        self.V(D_, "tensor_copy", cs4, [M_t], out=mend4, in_=Mv[:, :, LS - 1])
        self.V(D_, "tensor_tensor", cs4, [cs4], out=tcol4, in0=c[0:LS, 16:20], in1=acol4, op=ALU.subtract)
        self.A(cs4, wcol4, cs4, tcol4, AF.Exp, scale=-1.0)
        self.V(D_, "tensor_tensor", cs4, [self.sv, cs4], out=c[:, 24:28], in0=m0c, in1=mend4, op=ALU.subtract)
        self.A(cs4, dec4, cs4, c[:, 24:28], AF.Exp)
        self.V(D_, "tensor_tensor", kw4, [tkb, cs4], out=kw4.ap[0:LS, :].rearrange("p (s d) -> p s d", d=128), in0=tkb.ap[0:LS, 0:NSQ * 128].rearrange("p (s d) -> p s d", d=128),
               in1=wcol4.unsqueeze(2).to_broadcast([LS, NSQ, 128]), op=ALU.mult)
        psC = self.next_ps()
        psn = self.next_ps()
        for sq in range(NSQ):
            self.mm(psC, psC.ap[:, sq * 128:(sq + 1) * 128], kw4, kw4.ap[0:LS, sq * 128:(sq + 1) * 128], tvb, tvb.ap[0:LS, sq * 128:(sq + 1) * 128], start=True, stop=True)
        for sq in range(NSQ):
            self.mm(psn, psn.ap[:, sq:sq + 1], kw4, kw4.ap[0:LS, sq * 128:(sq + 1) * 128], self.ones_bf, self.ones_bf.ap[0:LS, 0:1], start=True, stop=True)
        for sq in range(NSQ):
            self.V(D_, "scalar_tensor_tensor", Cs4, [Cs4, cs4, psC], out=Cs4.ap[:, sq, :], in0=Cs4.ap[:, sq, :], scalar=c[:, 12 + sq:13 + sq], in1=psC.ap[:, sq * 128:(sq + 1) * 128],
                   op0=ALU.mult, op1=ALU.add)
        self.dma(self.o_saC.rearrange("(s h) p d -> h p s d", h=4)[hd], Cs4.ap[:, :, :], reads=[Cs4], queue=self.act)
        ov = outs.ap[:, 0:16].rearrange("p (s h) -> p s h", h=4)[:, :, hd]
        self.V(D_, "tensor_tensor", cs4, [self.sv, cs4], out=nnew4, in0=n0c, in1=dec4, op=ALU.mult)
        self.V(D_, "tensor_tensor", outs, [cs4, psn], out=ov, in0=nnew4, in1=psn.ap[:, 0:NSQ], op=ALU.add)
        mv = outs.ap[:, 16:32].rearrange("p (s h) -> p s h", h=4)[:, :, hd]
        self.V(D_, "tensor_tensor", outs, [cs4, G_t], out=mv, in0=mend4, in1=G[:, sl].rearrange("p (s l) -> p s l", l=LS)[:, :, LS - 1], op=ALU.subtract)

    def mlstm_bufs(self, hd):
        a = self.a
        if hd % 2 == 0:
            return a[8], a[9], a[10], a[11], self.tokb[0], self.tokb[1]
        return a[18], a[19], a[20], a[21], self.tokb[2], self.tokb[3]

    def mlstm_P(self, pi, hd):
        qT, kT, ktok, vtok, tkb, tvb = self.mlstm_bufs(hd)
        KS = 128 ** -0.5

        def pidx(c0):
            return 0 if c0 < 512 else (1 if c0 < 1024 else 2)
        w = self.get(("ab_in", "qa", hd))
        yield from self.proj_fm_g(w, pi, lambda ps, c0, c1: self.A(qT.parts[pidx(c0)], qT.ap[:, c0:c1], ps, ps.ap[:, 0:c1 - c0], AF.Copy))
        w = self.get(("ab_in", "ka", hd))
        yield from self.proj_fm_g(w, pi, lambda ps, c0, c1: self.A(kT.parts[pidx(c0)], kT.ap[:, c0:c1], ps, ps.ap[:, 0:c1 - c0], AF.Copy, scale=KS))
        yield from self.proj_tm_g(w, pi, lambda ps, g: self.A(ktok.parts[g], ktok.ap[:, g * 512:(g + 1) * 512], ps, ps.ap[:, 0:512], AF.Copy, scale=KS),
                                  lambda ps: self.A(tkb, tkb.ap[0:LS, 0:512], ps, ps.ap[0:LS, 0:512], AF.Copy, scale=KS))
        w = self.get(("ab_in", "va", hd))
        yield from self.proj_tm_g(w, pi, lambda ps, g: self.A(vtok.parts[g], vtok.ap[:, g * 512:(g + 1) * 512], ps, ps.ap[:, 0:512], AF.Copy),
                                  lambda ps: self.A(tvb, tvb.ap[0:LS, 0:512], ps, ps.ap[0:LS, 0:512], AF.Copy))

    def mlstm_C(self, pi, hd):
        D_, P = self.dve, self.pool
        yield
        T = TP + (TS if pi == 1 else 0)
        yield
        cbs = self.cbs(pi)
        yield
        H2 = ((0, 512), (512, 1024))
        yield
        F = self.F
        yield
        small = self.small
        yield
        ig, G, M, emt, tho, hT, W_ = F[0], F[1], F[2], F[3], F[4], F[5], F[6]
        yield
        qT, kT, ktok, vtok, tkb, tvb = self.mlstm_bufs(hd)
        yield
        sqh = self.a[12]
        yield
        P_all, qs_all, kw_all, Cb_all, nrb_all = self.a[13], self.a[14], self.a[15], self.a[16], self.a[17]
        yield
        KS = 128 ** -0.5
        yield
        ow = self.ones_w
        yield
        c8 = self.c8
        yield
        c8a = c8.ap
        yield
        m0 = self.m128[0]
        yield
        Cst, Cbst, nrst, nrbst = self.C[hd], self.Cb[hd], self.nr[hd], self.nrb[hd]
        yield

        def pidx(c0):
            return 0 if c0 < 512 else (1 if c0 < 1024 else 2)
        for gi, gcol in enumerate((hd, 4 + hd)):
            wr_t = self.wrep if gi == 0 else sqh
            wr_ap = wr_t.ap[:, 0:1024]
            self.V(D_, "tensor_copy", wr_t, [self.gw], out=wr_ap.rearrange("p (k c) -> p k c", c=128),
                   in_=self.gw.ap[:, 0:64].rearrange("p (k g) -> p k g", g=8)[:, :, gcol].unsqueeze(2).to_broadcast([128, 8, 128]))
            yield

            def ev(ps, c0, c1, gi=gi):
                g = pidx(c0)
                if gi == 0:
                    self.A(ig.parts[g], ig.ap[:, c0:c1], ps, ps.ap[:, 0:c1 - c0], AF.Identity, bias=self.pvc("gbias", hd), extra=[self.pv])
                else:
                    self.A(G.parts[g], G.ap[:, c0:c1], ps, ps.ap[:, 0:c1 - c0], AF.Exp, bias=small.ap[:, 34 + hd:35 + hd], scale=-1.0, extra=[small])
            self.proj_fm(wr_t, pi, ev)
            yield
        for (c0, c1) in cbs:
            g = pidx(c0)
            yield
            self.A(G.parts[g], G.ap[:, c0:c1], G.parts[g], G.ap[:, c0:c1], AF.Ln, bias=self.cst.ap[:, 1:2], extra=[self.cst])
            yield
        w = self.get(("ab_in", "oa", hd))
        yield
        self.proj_fm(w, pi, lambda ps, c0, c1: self.A(tho.parts[pidx(c0)], tho.ap[:, c0:c1], ps, ps.ap[:, 0:c1 - c0], AF.Tanh, scale=0.5))
        yield
        yield "P_OK"
        self.V(D_, "tensor_copy", self.cols, [self.Ml], out=self.cols.ap[:, 0:1], in_=self.Ml.ap[:, hd:hd + 1])
        yield
        for g, (c0, c1) in enumerate(H2):
            gi_t, gi_ap = (self.Gl, self.Gl.ap[:, hd:hd + 1]) if g == 0 else (G.parts[0], G.ap[:, 511:512])
            yield
            mi_t, mi_ap = (self.Ml, self.Ml.ap[:, hd:hd + 1]) if g == 0 else (M.parts[0], M.ap[:, 511:512])
            yield
            self.V(D_, "tensor_tensor_scan", G.parts[g], [ow, G.parts[g], gi_t], out=G.ap[:, c0:c1], data0=ow.ap[:, 0:512], data1=G.ap[:, c0:c1],
                   initial=gi_ap, op0=ALU.mult, op1=ALU.add)
            yield
            self.V(D_, "tensor_tensor", ig.parts[g], [ig.parts[g], G.parts[g]], out=ig.ap[:, c0:c1], in0=ig.ap[:, c0:c1], in1=G.ap[:, c0:c1], op=ALU.add)
            yield
            self.V(D_, "tensor_tensor_scan", M.parts[g], [ow, ig.parts[g], mi_t], out=M.ap[:, c0:c1], data0=ow.ap[:, 0:512], data1=ig.ap[:, c0:c1],
                   initial=mi_ap, op0=ALU.mult, op1=ALU.max)
            yield
            self.V(D_, "tensor_tensor", emt.parts[g], [G.parts[g], M.parts[g]], out=emt.ap[:, c0:c1], in0=G.ap[:, c0:c1], in1=M.ap[:, c0:c1], op=ALU.subtract)
            yield
            self.A(emt.parts[g], emt.ap[:, c0:c1], emt.parts[g], emt.ap[:, c0:c1], AF.Exp)
            yield
        if pi == 1:
            g = 2
            yield
            for sq in range(NSQ):
                cs = TP + LS * sq
                yield
                self.V(D_, "tensor_tensor_scan", G.parts[g], [ow, G.parts[g]], out=G.ap[:, cs:cs + LS], data0=ow.ap[:, 0:LS], data1=G.ap[:, cs:cs + LS],
                       initial=0.0, op0=ALU.mult, op1=ALU.add)
                yield
            self.V(D_, "tensor_tensor", ig.parts[g], [ig.parts[g], G.parts[g]], out=ig.ap[:, TP:T], in0=ig.ap[:, TP:T], in1=G.ap[:, TP:T], op=ALU.add)
            yield
            for sq in range(NSQ):
                cs = TP + LS * sq
                yield
                self.V(D_, "tensor_tensor_scan", M.parts[g], [ow, ig.parts[g], self.sv], out=M.ap[:, cs:cs + LS], data0=ow.ap[:, 0:LS], data1=ig.ap[:, cs:cs + LS],
                       initial=self.svc("am", sq * 4 + hd), op0=ALU.mult, op1=ALU.max)
                yield
            self.V(D_, "tensor_tensor", emt.parts[g], [G.parts[g], M.parts[g]], out=emt.ap[:, TP:T], in0=G.ap[:, TP:T], in1=M.ap[:, TP:T], op=ALU.subtract)
            yield
            self.A(emt.parts[g], emt.ap[:, TP:T], emt.parts[g], emt.ap[:, TP:T], AF.Exp)
            yield
        self.V(D_, "tensor_copy", self.Gl, [G.parts[1]], out=self.Gl.ap[:, hd:hd + 1], in_=G.ap[:, TP - 1:TP])
        yield
        Mv = M.ap[:, 0:TP].rearrange("p (t l) -> p t l", l=128)
        yield
        Wv = W_.ap[:, 0:TP].rearrange("p (t l) -> p t l", l=128)
        yield
        for g in range(2):
            cg = c8.parts[g]
            yield
            s4 = slice(4 * g, 4 * g + 4)
            yield
            for tt_ in range(4 * g, 4 * g + 4):
                self.V(D_, "scalar_tensor_tensor", [m0, cg], [ig.parts[g], self.ident], out=m0.ap[:, :], in0=ig.ap[:, tt_ * 128:(tt_ + 1) * 128], scalar=1.0,
                       in1=self.ident.ap[:, :], op0=ALU.mult, op1=ALU.mult, accum_out=c8a[:, tt_:tt_ + 1])
                yield
            if g == 0:
                self.V(D_, "tensor_copy", cg, [self.cols], out=c8a[:, 8:9], in_=self.cols.ap[:, 0:1])
                yield
                self.V(D_, "tensor_copy", cg, [M.parts[0]], out=c8a[:, 9:12], in_=Mv[:, 0:3, 127])
                yield
            else:
                self.V(D_, "tensor_copy", cg, [M.parts[0], M.parts[1]], out=c8a[:, 12:16], in_=Mv[:, 3:7, 127])
                yield
            self.V(D_, "tensor_copy", cg, [M.parts[g]], out=c8a[:, 16 + 4 * g:20 + 4 * g], in_=Mv[:, s4, 127])
            yield
            self.V(D_, "tensor_tensor", cg, [cg], out=c8a[:, 48 + 4 * g:52 + 4 * g], in0=c8a[:, 16 + 4 * g:20 + 4 * g], in1=c8a[:, 4 * g:4 * g + 4], op=ALU.subtract)
            yield
            self.A(cg, c8a[:, 24 + 4 * g:28 + 4 * g], cg, c8a[:, 48 + 4 * g:52 + 4 * g], AF.Exp, scale=-1.0)
            yield
            self.V(D_, "tensor_tensor", cg, [cg], out=c8a[:, 48 + 4 * g:52 + 4 * g], in0=c8a[:, 8 + 4 * g:12 + 4 * g], in1=c8a[:, 16 + 4 * g:20 + 4 * g], op=ALU.subtract)
            yield
            self.A(cg, c8a[:, 32 + 4 * g:36 + 4 * g], cg, c8a[:, 48 + 4 * g:52 + 4 * g], AF.Exp)
            yield
            self.V(D_, "tensor_tensor", W_.parts[g], [cg, M.parts[g]], out=Wv[:, s4, :], in0=c8a[:, s4].unsqueeze(2).to_broadcast([128, 4, 128]), in1=Mv[:, s4, :], op=ALU.subtract)
            yield
            self.V(D_, "tensor_tensor", W_.parts[g], [W_.parts[g], self.maskb], out=Wv[:, s4, :], in0=Wv[:, s4, :],
                   in1=self.maskb.ap[:, :].unsqueeze(1).to_broadcast([128, 4, 128]), op=ALU.add)
            yield
            self.A(W_.parts[g], W_.ap[:, g * 512:(g + 1) * 512], W_.parts[g], W_.ap[:, g * 512:(g + 1) * 512], AF.Exp)
            yield
        CN = G
        yield
        CNtd = CN.ap[:, 0:TP].rearrange("p (d t) -> p t d", t=8)
        yield
        CNp = [CN.parts[0], CN.parts[1]]
        yield
        for g, (c0h, c1h) in enumerate(H2):
            cg = c8.parts[g]
            yield
            s4 = slice(4 * g, 4 * g + 4)
            yield
            hs = slice(c0h, c1h)
            yield
            ps = self.next_ps()
            yield
            for t4 in range(4):
                c0 = (g * 4 + t4) * 128
                yield
                self.mm(ps, ps.ap[:, t4 * 128:(t4 + 1) * 128], kT.parts[g], kT.ap[:, c0:c0 + 128], qT.parts[g], qT.ap[:, c0:c0 + 128], start=True, stop=True)
                yield
            self.V(D_, "tensor_tensor", P_all.parts[g], [ps, W_.parts[g]], out=P_all.ap[:, hs], in0=ps.ap[:, 0:512], in1=W_.ap[:, hs], op=ALU.mult)
            yield
            self.V(D_, "tensor_tensor", W_.parts[g], [cg, M.parts[g]], out=Wv[:, s4, :], in0=c8a[:, 8 + 4 * g:12 + 4 * g].unsqueeze(2).to_broadcast([128, 4, 128]),
                   in1=Mv[:, s4, :], op=ALU.subtract)
            yield
            self.A(W_.parts[g], W_.ap[:, hs], W_.parts[g], W_.ap[:, hs], AF.Exp)
            yield
            self.V(D_, "tensor_tensor", qs_all.parts[g], [qT.parts[g], W_.parts[g]], out=qs_all.ap[:, hs], in0=qT.ap[:, hs], in1=W_.ap[:, hs], op=ALU.mult)
            yield
            self.V(D_, "tensor_tensor", kw_all.parts[g], [ktok.parts[g], cg], out=kw_all.ap[:, hs].rearrange("p (t d) -> p t d", d=128),
                   in0=ktok.ap[:, hs].rearrange("p (t d) -> p t d", d=128), in1=c8a[:, 24 + 4 * g:28 + 4 * g].unsqueeze(2).to_broadcast([128, 4, 128]), op=ALU.mult)
            yield
            ps = self.next_ps()
            yield
            for t4 in range(4):
                c0 = (g * 4 + t4) * 128
                yield
                self.mm(ps, ps.ap[:, t4 * 128:(t4 + 1) * 128], kw_all.parts[g], kw_all.ap[:, c0:c0 + 128], vtok.parts[g], vtok.ap[:, c0:c0 + 128], start=True, stop=True)
                yield
            self.op(self.act, lambda ps=ps, g=g: self.nc.scalar.activation(out=CNtd[:, g * 4:(g + 1) * 4, :], in_=ps.ap[:, 0:512].rearrange("p (t d) -> p t d", d=128), func=AF.Copy),
                    reads=[ps], writes=CNp)
            yield
            psn = self.next_ps()
            yield
            for t4 in range(4):
                c0 = (g * 4 + t4) * 128
                yield
                self.mm(psn, psn.ap[:, t4:t4 + 1], kw_all.parts[g], kw_all.ap[:, c0:c0 + 128], self.ones_bf, self.ones_bf.ap[:, 0:1], start=True, stop=True)
                yield
            self.V(D_, "tensor_copy", cg, [psn], out=c8a[:, 40 + 4 * g:44 + 4 * g], in_=psn.ap[:, 0:4])
            yield
        c80, c81 = c8.parts[0], c8.parts[1]
        yield
        self.V(D_, "scalar_tensor_tensor", CNp, [Cst, c80] + CNp, out=CNtd[:, 0, :], in0=Cst.ap[:, :], scalar=c8a[:, 32:33], in1=CNtd[:, 0, :], op0=ALU.mult, op1=ALU.add)
        yield
        self.V(D_, "scalar_tensor_tensor", c80, [nrst, c80], out=c8a[:, 40:41], in0=nrst.ap[:, 0:1], scalar=c8a[:, 32:33], in1=c8a[:, 40:41], op0=ALU.mult, op1=ALU.add)
        yield
        self.V(D_, "memset", c80, [], ap=c8a[:, 32:33], constant=0.0)
        yield
        Wp = [W_.parts[0], W_.parts[1]]
        yield
        self.V(D_, "tensor_copy", Wp, [c80, c81], out=W_.ap[:, 0:TP].rearrange("p (d t) -> p d t", t=8), in_=c8a[:, 32:40].unsqueeze(1).to_broadcast([128, 128, 8]))
        yield
        self.V(D_, "tensor_tensor_scan", CNp, CNp + Wp, out=CN.ap[:, 0:TP], data0=W_.ap[:, 0:TP], data1=CN.ap[:, 0:TP], initial=0.0, op0=ALU.mult, op1=ALU.add)
        yield
        self.V(D_, "tensor_tensor_scan", [c80, c81], [c80, c81], out=c8a[:, 40:48], data0=c8a[:, 32:40], data1=c8a[:, 40:48], initial=0.0, op0=ALU.mult, op1=ALU.add)
        yield
        Cbp = [Cb_all.parts[0], Cb_all.parts[1]]
        yield
        self.op(self.act, lambda: self.nc.scalar.activation(out=Cb_all.ap[:, 0:TP].rearrange("p (t d) -> p t d", d=128), in_=CNtd, func=AF.Copy), reads=CNp, writes=Cbp)
        yield
        nbp = [nrb_all.parts[0], nrb_all.parts[1]]
        yield
        self.V(D_, "tensor_copy", nbp, [c80, c81], out=nrb_all.ap[:, 0:TP].rearrange("p (t d) -> p t d", d=128), in_=c8a[:, 40:48].unsqueeze(2).to_broadcast([128, 8, 128]))
        yield
        for g, (c0h, c1h) in enumerate(H2):
            hs = slice(c0h, c1h)
            yield
            psN, psD = self.next_ps(), self.next_ps()
            yield
            for (psX, l_first_t, l_first, l_all) in ((psN, None, None, Cb_all), (psD, self.ones_bf, self.ones_bf.ap[:, :], nrb_all)):
                for t4 in range(4):
                    tt_ = g * 4 + t4
                    yield
                    c0 = tt_ * 128
                    yield
                    o = psX.ap[:, t4 * 128:(t4 + 1) * 128]
                    yield
                    if psX is psN:
                        self.mm(psX, o, vtok.parts[g], vtok.ap[:, c0:c0 + 128], P_all.parts[g], P_all.ap[:, c0:c0 + 128], start=True, stop=False, inc=True)
                        yield
                        st_t, st_ap = (Cbst, Cbst.ap[:, :]) if tt_ == 0 else (Cbp[(tt_ - 1) // 4], Cb_all.ap[:, c0 - 128:c0])
                        yield
                    else:
                        self.mm(psX, o, self.ones_bf, self.ones_bf.ap[:, :], P_all.parts[g], P_all.ap[:, c0:c0 + 128], start=True, stop=False, inc=True)
                        yield
                        st_t, st_ap = (nrbst, nrbst.ap[:, :]) if tt_ == 0 else (nbp[(tt_ - 1) // 4], nrb_all.ap[:, c0 - 128:c0])
                        yield
                    self.mm(psX, o, st_t, st_ap, qs_all.parts[g], qs_all.ap[:, c0:c0 + 128], start=False, stop=True)
                    yield
            Wg = W_.parts[g]
            yield
            self.A(Wg, W_.ap[:, hs], psD, psD.ap[:, 0:512], AF.Abs)
            yield
            self.V(D_, "tensor_tensor", Wg, [Wg, emt.parts[g]], out=W_.ap[:, hs], in0=W_.ap[:, hs], in1=emt.ap[:, hs], op=ALU.max)
            yield
            self.A(Wg, W_.ap[:, hs], Wg, W_.ap[:, hs], AF.Ln)
            yield
            self.A(Wg, W_.ap[:, hs], Wg, W_.ap[:, hs], AF.Exp, scale=-1.0)
            yield
            self.V(D_, "tensor_tensor", hT.parts[g], [psN, Wg], out=hT.ap[:, hs], in0=psN.ap[:, 0:512], in1=W_.ap[:, hs], op=ALU.mult)
            yield
        self.V(D_, "tensor_copy", Cst, CNp, out=Cst.ap[:, :], in_=CNtd[:, 7, :])
        yield
        self.op(self.act, lambda: self.nc.scalar.activation(out=Cbst.ap[:, :], in_=CNtd[:, 7, :], func=AF.Copy), reads=CNp, writes=[Cbst])
        yield
        self.V(D_, "tensor_copy", nrst, [c81], out=nrst.ap[:, :], in_=c8a[:, 47:48].to_broadcast([128, 128]))
        yield
        self.V(D_, "tensor_copy", nrbst, [c81], out=nrbst.ap[:, :], in_=c8a[:, 47:48].to_broadcast([128, 128]))
        yield
        self.V(D_, "tensor_copy", self.Ml, [M.parts[1]], out=self.Ml.ap[:, hd:hd + 1], in_=M.ap[:, TP - 1:TP])
        yield
        if pi == 1:
            outs = self.outs
            yield
            self.V(D_, "tensor_copy", outs, [self.nr[hd]], out=outs.ap[:, 32 + hd:33 + hd], in_=self.nr[hd].ap[:, 0:1])
            yield
            self.V(D_, "tensor_tensor", outs, [self.Ml, self.Gl], out=outs.ap[:, 36 + hd:37 + hd], in0=self.Ml.ap[:, hd:hd + 1], in1=self.Gl.ap[:, hd:hd + 1], op=ALU.subtract)
            yield
            self.dma(self.o_paC[hd], self.C[hd].ap[:, :], reads=[self.C[hd]], queue=self.act)
            yield
            self.mlstm_sample(hd, ig.parts[2], G.parts[2], M.parts[2], emt.parts[2], hT.parts[2], qT.parts[2], kT.parts[2], tkb, tvb,
                              ig.ap, G.ap, M.ap, emt.ap, hT.ap, qT.ap, kT.ap)
            yield
        lnt = ig
        yield
        for (c0, c1) in cbs:
            g = pidx(c0)
            yield
            cs_ = slice(c0, c1)
            yield
            self.A(sqh.parts[g], sqh.ap[:, cs_], hT.parts[g], hT.ap[:, cs_], AF.Square)
            yield
            ps = self.next_ps()
            yield
            self.mm(ps, ps.ap[:, 0:c1 - c0], self.ones_bf, self.ones_bf.ap[:, :], sqh.parts[g], sqh.ap[:, cs_], start=True, stop=True)
            yield
            self.A(lnt.parts[g], lnt.ap[:, cs_], ps, ps.ap[:, 0:c1 - c0], AF.Ln, bias=self.cst.ap[:, 0:1], scale=1.0 / 128, extra=[self.cst])
            yield
            self.A(lnt.parts[g], lnt.ap[:, cs_], lnt.parts[g], lnt.ap[:, cs_], AF.Exp, scale=-0.5)
            yield
            self.V(D_, "tensor_tensor", hT.parts[g], [hT.parts[g], lnt.parts[g]], out=hT.ap[:, cs_], in0=hT.ap[:, cs_], in1=lnt.ap[:, cs_], op=ALU.mult)
            yield
            self.V(D_, "scalar_tensor_tensor", hT.parts[g], [tho.parts[g], hT.parts[g]], out=hT.ap[:, cs_], in0=tho.ap[:, cs_], scalar=1.0, in1=hT.ap[:, cs_], op0=ALU.add, op1=ALU.mult)
            yield
            self.V(D_, "tensor_scalar", self.a[hd].parts[g], [hT.parts[g], small], out=self.a[hd].ap[:, cs_], in0=hT.ap[:, cs_], scalar1=small.ap[:, 24 + hd:25 + hd], scalar2=None, op0=ALU.mult)
            yield

    def proj_fm_g(self, w, pi, evac):
        for (c0, c1) in self.cbs(pi):
            ps = self.next_ps()
            for k in range(NCH):
                self.mm(ps, ps.ap[:, 0:c1 - c0], w, w.ap[:, k * 128:(k + 1) * 128], self.h[k].parts[0 if c0 < 512 else (1 if c0 < 1024 else 2)], self.h[k].ap[:, c0:c1], start=(k == 0), stop=(k == 7))
            evac(ps, c0, c1)
            yield

    def proj_tm_g(self, w, pi, evac, evac_s):
        for g in range(2):
            ps = self.next_ps()
            for t4 in range(4):
                tt_ = g * 4 + t4
                for k in range(NCH):
                    self.mm(ps, ps.ap[:, t4 * 128:(t4 + 1) * 128], self.h[k].parts[tt_ // 4], self.h[k].ap[:, tt_ * 128:(tt_ + 1) * 128], w, w.ap[:, k * 128:(k + 1) * 128],
                            start=(k == 0), stop=(k == 7))
                if t4 == 1:
                    yield
            evac(ps, g)
            yield
        if pi == 1:
            ps = self.next_ps()
            for sq in range(NSQ):
                cs = TP + LS * sq
                for k in range(NCH):
                    self.mm(ps, ps.ap[0:LS, sq * 128:(sq + 1) * 128], self.h[k].parts[2], self.h[k].ap[:, cs:cs + LS], w, w.ap[:, k * 128:(k + 1) * 128],
                            start=(k == 0), stop=(k == 7))
            evac_s(ps)
            yield

    def attn_bufs(self, hp):
        a = self.a
        if hp % 2 == 0:
            return a[8], a[9], a[10], self.tokb[0]
        return a[18], a[19], a[20], self.tokb[1]

    def attn_pro(self, pi, hp):
        D_ = self.dve
        T = TP + (TS if pi == 1 else 0)
        cbs = self.cbs(pi)
        F = self.F
        small = self.small
        Fq, Fk, rs = F[0], F[1], F[2]
        sqb = self.a[12]
        qn, kn, vcur, tvb = self.attn_bufs(hp)

        def qk_norm(name, raw):
            w = self.get(("ab_in", name, hp))

            def ev(ps, c0, c1):
                self.A(raw, raw.ap[:, c0:c1], ps, ps.ap[:, 0:c1 - c0], AF.Copy)
                self.A(sqb, sqb.ap[:, c0:c1], ps, ps.ap[:, 0:c1 - c0], AF.Square)
            yield from self.proj_fm_g(w, pi, ev)
            for (c0, c1) in cbs:
                ps = self.next_ps()
                self.mm(ps, ps.ap[:, 0:c1 - c0], self.blk_bf, self.blk_bf.ap[:, :], sqb, sqb.ap[:, c0:c1], start=True, stop=True)
                self.A(rs, rs.ap[:, c0:c1], ps, ps.ap[:, 0:c1 - c0], AF.Ln, bias=self.cst.ap[:, 0:1], scale=1.0 / 64, extra=[self.cst])
            yield
            self.A(rs, rs.ap[:, 0:T], rs, rs.ap[:, 0:T], AF.Exp, scale=-0.5)
            yield
        yield from qk_norm("qb", Fq)
        self.V(D_, "scalar_tensor_tensor", qn, [Fq, small, rs], out=qn.ap[:, 0:T], in0=Fq.ap[:, 0:T], scalar=small.ap[:, 28:29], in1=rs.ap[:, 0:T], op0=ALU.mult, op1=ALU.mult)
        yield
        yield from qk_norm("kb", Fk)
        self.V(D_, "scalar_tensor_tensor", Fk, [Fk, self.pv, rs], out=Fk.ap[:, 0:T], in0=Fk.ap[:, 0:T], scalar=self.pvc("gk"), in1=rs.ap[:, 0:T], op0=ALU.mult, op1=ALU.mult)
        yield
        self.A(kn, kn.ap[:, 0:T], Fk, Fk.ap[:, 0:T], AF.Copy)
        if pi == 1:
            self.dma(self.o_pbk[:, hp, :], Fk.ap[:, 512:1024], reads=[Fk], queue=self.act)
            self.dma(self.o_sbk[:, hp, :], Fk.ap[:, TP:T], reads=[Fk], queue=self.act)
        yield
        w = self.get(("ab_in", "vb", hp))

        def ev_v(ps, g):
            self.A(vcur, vcur.ap[:, g * 512:(g + 1) * 512], ps, ps.ap[:, 0:512], AF.Copy)
            if pi == 1 and g == 1:
                vf = F[5]
                self.A(vf, vf.ap[:, 0:512], ps, ps.ap[:, 0:512], AF.Copy)
                self.dma(self.o_pbv[:, :, hp * 128:(hp + 1) * 128].rearrange("t p f -> p t f"), vf.ap[:, 0:512].rearrange("p (t f) -> p t f", f=128), reads=[vf], queue=self.act)

        def ev_vs(ps):
            self.A(tvb, tvb.ap[0:LS, 0:512], ps, ps.ap[0:LS, 0:512], AF.Copy)
            self.A(self.tokf, self.tokf.ap[0:LS, 0:512], ps, ps.ap[0:LS, 0:512], AF.Copy)
            self.dma(self.o_sbv[:, :, hp * 128:(hp + 1) * 128].rearrange("s t f -> t s f"), self.tokf.ap[0:LS, 0:512].rearrange("p (s f) -> p s f", f=128), reads=[self.tokf], queue=self.act)
        yield from self.proj_tm_g(w, pi, ev_v, ev_vs)

    def attn_loop(self, pi, hp):
        D_, P = self.dve, self.pool
        F = self.F
        qn, kn, vcur, tvb = self.attn_bufs(hp)
        tmpSs = (F[3], F[4], F[6])
        Pbfs = (self.a[11], self.a[13], self.a[14])
        bT = self.biasT
        mix = self.a[4 + hp]
        self.dma(bT.ap[:, :, :], self.d_bias[hp], writes=[bT], queue=self.act)
        self.V(P, "memset", bT, [], ap=bT.ap[64:128, :, 512:576], constant=NEG)
        self.V(P, "memset", bT, [], ap=bT.ap[0:64, :, 64:128], constant=NEG)
        iters = [(qt, hh) for qt in range(8) for hh in range(2)]

        def srcs(qt, o):
            aq = 8 * pi + qt
            ka = aq - 4 + o
            if ka >= 8 * pi:
                c = (ka - 8 * pi) * 128
                return kn, kn.ap[:, c:c + 128], vcur, vcur.ap[:, c:c + 128]
            c = (ka - 4) * 128
            return self.kband[hp], self.kband[hp].ap[:, c:c + 128], self.vband, self.vband.ap[:, ka - 4, hp * 128:(hp + 1) * 128]

        def stageA(i):
            qt, hh = iters[i]
            r0 = 64 * hh
            offs = [o for o in range(5) if 8 * pi + qt - 4 + o >= 0]
            tmpS, Pbf = tmpSs[i % 3], Pbfs[i % 3]
            t0, t1, pS = self.next_ps2()
            for o in offs:
                k_t, k_ap, _, _ = srcs(qt, o)
                self.mm(t0 if o < 4 else t1, pS[:, o * 128:(o + 1) * 128], k_t, k_ap[r0:r0 + 64, :], qn, qn.ap[r0:r0 + 64, qt * 128:(qt + 1) * 128], start=True, stop=True)
            n0, n1 = offs[0] * 128, 640
            self.V(D_, "tensor_tensor", tmpS, [t0, t1, bT], out=tmpS.ap[:, n0:n1], in0=pS[:, n0:n1], in1=bT.ap[:, hh, n0:n1], op=ALU.add)
            self.A(Pbf, Pbf.ap[:, n0:n1], tmpS, tmpS.ap[:, n0:n1], AF.Exp)
        pend = {}

        def stageB1(i):
            qt, hh = iters[i]
            r0 = 64 * hh
            offs = [o for o in range(5) if 8 * pi + qt - 4 + o >= 0]
            Pbf = Pbfs[i % 3]
            psB = self.next_ps()
            for idx, o in enumerate(offs):
                _, _, v_t, v_ap = srcs(qt, o)
                self.mm(psB, psB.ap[:, 0:128], v_t, v_ap, Pbf, Pbf.ap[:, o * 128:(o + 1) * 128], start=(idx == 0), stop=(idx == len(offs) - 1))
            for idx, o in enumerate(offs):
                self.mm(psB, psB.ap[:, 128:256], self.ones_bf, self.ones_bf.ap[:, :], Pbf, Pbf.ap[:, o * 128:(o + 1) * 128], start=(idx == 0), stop=(idx == len(offs) - 1))
            rc = self.m128[2 + hh]
            self.A(rc, rc.ap[r0:r0 + 64, :], psB, psB.ap[r0:r0 + 64, 128:256], AF.Ln)
            self.A(rc, rc.ap[r0:r0 + 64, :], rc, rc.ap[r0:r0 + 64, :], AF.Exp, scale=-1.0)
            pend[i] = psB

        def stageB2(i):
            qt, hh = iters[i]
            r0 = 64 * hh
            psB = pend.pop(i)
            rc = self.m128[2 + hh]
            self.V(D_, "tensor_tensor", mix, [psB, rc], out=mix.ap[r0:r0 + 64, qt * 128:(qt + 1) * 128], in0=psB.ap[r0:r0 + 64, 0:128], in1=rc.ap[r0:r0 + 64, :], op=ALU.mult)
        stageA(0)
        stageA(1)
        yield
        for i in range(len(iters)):
            if i + 2 < len(iters):
                stageA(i + 2)
            stageB1(i)
            if i >= 1:
                stageB2(i - 1)
            yield
        stageB2(len(iters) - 1)

    def attn_sample(self, pi, hp):
        D_, P = self.dve, self.pool
        F = self.F
        qn, kn, vcur, tvb = self.attn_bufs(hp)
        tmpSs = (F[3], F[4], F[6])
        Pbfs = (self.a[11], self.a[13], self.a[14])
        bT = self.biasT
        mix = self.a[4 + hp]
        if pi == 1:
            tS4 = (self.m128[0], self.m128[1], F[3], F[4])
            Pb4 = (self.b128[0], self.b128[1], self.b128[2], self.a[11])

            def sA(sq):
                cs = TP + LS * sq
                kc = self.get(("kc", sq, hp))
                for hh in range(2):
                    r0 = 64 * hh
                    tmpS, Pbf = tS4[2 * (sq % 2) + hh], Pb4[2 * (sq % 2) + hh]
                    psS = self.next_ps()
                    qa = qn.ap[r0:r0 + 64, cs:cs + LS]
                    for o in range(4):
                        self.mm(psS, psS.ap[:, o * LS:(o + 1) * LS], kc, kc.ap[r0:r0 + 64, o * 128:(o + 1) * 128], qn, qa, start=True, stop=True)
                    self.mm(psS, psS.ap[0:LS, 64:64 + LS], kn, kn.ap[r0:r0 + 64, cs:cs + LS], qn, qa, start=True, stop=True)
                    self.V(D_, "tensor_tensor", tmpS, [psS, bT], out=tmpS.ap[:, 0:64].rearrange("p (k i) -> p k i", i=LS),
                           in0=psS.ap[:, 0:64].rearrange("p (k i) -> p k i", i=LS),
                           in1=bT.ap[:, hh, 0:512].rearrange("p (k i) -> p k i", i=128)[:, :, 0:LS], op=ALU.add)
                    self.V(D_, "tensor_tensor", tmpS, [psS, bT], out=tmpS.ap[0:LS, 64:64 + LS], in0=psS.ap[0:LS, 64:64 + LS], in1=bT.ap[0:LS, hh, 512:512 + LS], op=ALU.add)
                    self.A(Pbf, Pbf.ap[:, 0:64], tmpS, tmpS.ap[:, 0:64], AF.Exp)
                    self.A(Pbf, Pbf.ap[0:LS, 64:64 + LS], tmpS, tmpS.ap[0:LS, 64:64 + LS], AF.Exp)

            def sB(sq):
                cs = TP + LS * sq
                vc = self.get(("vc", sq, hp))
                for hh in range(2):
                    r0 = 64 * hh
                    Pbf = Pb4[2 * (sq % 2) + hh]
                    psO = self.next_ps()
                    for o in range(4):
                        self.mm(psO, psO.ap[:, 0:LS], vc, vc.ap[:, o * 128:(o + 1) * 128], Pbf, Pbf.ap[:, o * LS:(o + 1) * LS], start=(o == 0), stop=False, inc=(o == 3))
                    self.mm(psO, psO.ap[:, 0:LS], tvb, tvb.ap[0:LS, sq * 128:(sq + 1) * 128], Pbf, Pbf.ap[0:LS, 64:64 + LS], start=False, stop=True)
                    for o in range(4):
                        self.mm(psO, psO.ap[:, 128:128 + LS], self.ones_bf, self.ones_bf.ap[:, :], Pbf, Pbf.ap[:, o * LS:(o + 1) * LS], start=(o == 0), stop=False, inc=(o == 3))
                    self.mm(psO, psO.ap[:, 128:128 + LS], self.ones_bf, self.ones_bf.ap[0:LS, :], Pbf, Pbf.ap[0:LS, 64:64 + LS], start=False, stop=True)
                    rc = self.m128[2 + hh]
                    self.A(rc, rc.ap[r0:r0 + 64, 0:LS], psO, psO.ap[r0:r0 + 64, 128:128 + LS], AF.Ln)
                    self.A(rc, rc.ap[r0:r0 + 64, 0:LS], rc, rc.ap[r0:r0 + 64, 0:LS], AF.Exp, scale=-1.0)
                    self.V(D_, "tensor_tensor", mix, [psO, rc], out=mix.ap[r0:r0 + 64, cs:cs + LS], in0=psO.ap[r0:r0 + 64, 0:LS], in1=rc.ap[r0:r0 + 64, 0:LS], op=ALU.mult)
            sA(0)
            for sq in range(NSQ):
                if sq + 1 < NSQ:
                    sA(sq + 1)
                sB(sq)
        if pi == 0:
            self.V(P, "tensor_copy", self.kband[hp], [kn], out=self.kband[hp].ap[:, :], in_=kn.ap[:, 512:1024])
            self.V(P, "tensor_copy", self.vband, [vcur], out=self.vband.ap[:, :, hp * 128:(hp + 1) * 128], in_=vcur.ap[:, 512:1024].rearrange("p (t f) -> p t f", f=128))

    def final_outputs(self):
        q = self.act
        o = self.outs
        self.dma(self.o_pan, o.ap[:, 32:36], reads=[o], queue=q)
        self.dma(self.o_pam, o.ap[0:1, 36:40], reads=[o], queue=q)
        self.dma(self.o_san, o.ap[:, 0:16], reads=[o], queue=q)
        self.dma(self.o_sam, o.ap[0:1, 16:32], reads=[o], queue=q)
        self.dma(self.o_pconv, self.convc.ap, reads=[self.convc], queue=q)
        self.dma(self.o_pch, self.hc.ap, reads=[self.hc], queue=q)
        self.dma(self.o_sconv, self.sconv.ap, reads=[self.sconv], queue=q)
        self.dma(self.o_sch, self.sch.ap, reads=[self.sch], queue=q)

    def build(self):
        self.build_body()
        return self.nc

    def build_body(self):
        stop = DEBUG.get("stop")
        self.setup()
        done = False
        for pi in range(2):
            if not (pi == 1 and stop is None):
                self.load_x(pi)
            if stop == "setup":
                done = True
                self.store_x(pi)
                continue
            if pi == 0:
                self.ada_pending = [(0, c) for c in range(4, 18)] + [(1, c) for c in range(18)]
                for cbk in range(4):
                    self.ada_block(0, cbk)
                self.ada_derive(0, (0,), gate=False)
            if stop == "ada":
                done = True
                self.store_x(pi)
                continue
            for l in range(2):
                self.ffn(l, 1, pi)
                if stop == "l%df1" % l:
                    done = True
                    break
                if l == 0:
                    self.mixer0(pi)
                else:
                    self.mixer1(pi)
                if DEBUG.get("dump") == "mixed" and stop == "l%dmix" % l:
                    T_ = TP + (TS if pi == 1 else 0)
                    for c in range(NCH):
                        self.V(self.dve, "tensor_copy", self.x[c], [self.a[c]], out=self.x[c].ap[:, 0:T_], in_=self.a[c].ap[:, 0:T_])
                if stop == "l%dmix" % l:
                    done = True
                    break
                self.ffn(l, 2, pi)
                if stop == "l%df2" % l:
                    done = True
                    break
            if pi == 0 and stop is None:
                for c in range(NCH):
                    self.dma(self.o_yT[:, c, 0:TP], self.x[c].ap[:, 0:TP], reads=[self.x[c]])
                    self.dma(self.x[c].ap[:, 0:TP], self.d_xT[:, c, TP:2 * TP], writes=[self.x[c]])
                    self.dma(self.x[c].ap[:, TP:TMAX], self.d_xT[:, c, SEQ:SEQ + TS], writes=[self.x[c]])
            else:
                self.store_x(pi)
            if done and DEBUG.get("one_pass"):
                break
        if not done:
            self.final_outputs()
        self.finish()


def _unit_w(key, W):
    kind = key[0]

    def colblk(M, c0, n=128):
        return np.ascontiguousarray(M[:, c0:c0 + n].reshape(8, 128, n).transpose(1, 0, 2)).reshape(128, 8 * n)
    if kind == "ada":
        _, l, cbk, kq = key
        M = W["ada_w"][l][kq * 256:(kq + 1) * 256, cbk * 512:(cbk + 1) * 512]
        return np.ascontiguousarray(M.reshape(2, 128, 512).transpose(1, 0, 2)).reshape(128, 1024)
    if kind == "f_in":
        _, l, f, j, g = key
        Wi = W["ffn1_w_in"] if f == 1 else W["ffn2_w_in"]
        return colblk(Wi[l], g * DFF + j * 128)
    if kind == "f_out":
        _, l, f, i, hf = key
        j0, j1 = FO_SPLIT[hf]
        Wo = W["ffn1_w_out"] if f == 1 else W["ffn2_w_out"]
        M = Wo[l][j0 * 128:j1 * 128, i * 128:(i + 1) * 128]
        return np.ascontiguousarray(M.reshape(j1 - j0, 128, 128).transpose(1, 0, 2)).reshape(128, (j1 - j0) * 128)
    if kind == "ab_gate":
        return colblk(W["ab_w_in"][0], 2048, 8)
    if kind == "ab_in":
        _, nm, idx = key
        base = dict(qa=0, ka=512, va=1024, oa=1536, qb=2056, kb=2568, vb=3080)[nm]
        return colblk(W["ab_w_in"][0], base + idx * 128)
    if kind == "ab_out":
        return colblk(W["ab_w_out"][0], key[1] * 128)
    if kind == "c_in":
        _, which, n = key
        return colblk(W["c_w_in"][0], which * 1024 + n * 128)
    if kind == "c_gate":
        return np.ascontiguousarray(W["c_gate_w"][0][key[1]])
    if kind == "c_out":
        return colblk(W["c_w_out"][0], key[1] * 128)
    raise KeyError(key)


def _unit_c(key, ck, cv):
    kind, sq, hp = key
    if kind == "kc":
        return np.ascontiguousarray(ck[sq][:, 2 * hp:2 * hp + 2, :].reshape(512, 128).T)
    if kind == "vc":
        M = cv[sq][:, 2 * hp:2 * hp + 2, :].reshape(4, 128, 128)
        return np.ascontiguousarray(M.transpose(1, 0, 2)).reshape(128, 512)
    raise KeyError(key)


_CACHE = {}


def _get_prog():
    key = repr(sorted(DEBUG.items()))
    if key not in _CACHE:
        p = Prog()
        p.build()
        _CACHE[key] = p
    return _CACHE[key]


def kernel(**inp):
    inp = {k: np.asarray(v) for k, v in inp.items()}
    p = _get_prog()
    f32 = np.float32
    wst = np.zeros((128, max(p.wtotal, 1)), f32)
    for (src, key), off in p.offs.items():
        if src == "w":
            u = _unit_w(key, inp)
            wst[:, off:off + u.shape[1]] = u
    pv = np.zeros((128, p.NPV), f32)

    def fm(v):
        return np.asarray(v, f32).reshape(8, 128).T
    for nm, src in (("g0", "ffn1_norm"), ("g1", "mix_norm"), ("g2", "ffn2_norm")):
        for l in range(2):
            pv[:, p.PV[nm] + l * 8:p.PV[nm] + l * 8 + 8] = fm(inp[src][l])
    pv[:, p.PV["aon"]:p.PV["aon"] + 4] = inp["a_out_norm"][0].T
    pv[:, p.PV["gq"]] = np.tile(inp["b_q_norm"][0], 2)
    pv[:, p.PV["gk"]] = np.tile(inp["b_k_norm"][0], 2)
    pv[:, p.PV["gbias"]:p.PV["gbias"] + 8] = np.broadcast_to(inp["ab_gate_bias"][0][None, :], (128, 8))
    pv[:, p.PV["convw"]:p.PV["convw"] + 32] = inp["c_conv_w"][0].reshape(4, 8, 128).transpose(2, 1, 0).reshape(128, 32)
    pv[:, p.PV["convb"]:p.PV["convb"] + 8] = fm(inp["c_conv_b"][0])
    pv[:, p.PV["gateb"]:p.PV["gateb"] + 16] = inp["c_gate_b"][0].reshape(2, 8, 128).transpose(2, 1, 0).reshape(128, 16)
    pv[:, p.PV["lam"]:p.PV["lam"] + 8] = fm(inp["c_lambda"][0])
    adab = np.ascontiguousarray(inp["ada_b"].reshape(2, 72, 128).transpose(2, 0, 1), dtype=f32)
    tb = inp["b_rel_bias"][0]
    j = np.arange(640)[:, None]
    i = np.arange(128)[None, :]
    idx = np.clip(j - 512 - i, -128, 128) + 128
    bt = tb[:, idx]
    bt = bt.reshape(4, 2, 5, 128, 128).transpose(0, 3, 1, 2, 4).reshape(4, 128, 2, 640)
    biasT = np.ascontiguousarray(bt, f32)
    in_maps = []
    for c in range(8):
        sl = slice(4 * c, 4 * c + 4)
        xT = np.concatenate([inp["x_prompt"][c], inp["x_sample"][sl].reshape(TS, D)], axis=0)
        xT = np.ascontiguousarray(xT.T.reshape(8, 128, SEQ + TS).transpose(1, 0, 2))
        cc = np.concatenate([inp["c_prompt"][c:c + 1], inp["c_sample"][sl]], axis=0)
        cT = np.ascontiguousarray(cc.T.reshape(8, 128, 5).transpose(1, 0, 2))
        cst = np.zeros((128, max(p.ctotal, 1)), f32)
        ck, cv = inp["cache_b_k"][0][sl], inp["cache_b_v"][0][sl]
        for (src, key), off in p.offs.items():
            if src == "c":
                u = _unit_c(key, ck, cv)
                cst[:, off:off + u.shape[1]] = u
        sv = np.zeros((128, p.NSV), f32)
        sv[:, 0:16] = inp["state_a_n"][0][sl].reshape(16, 128).T
        sv[:, 16:32] = np.broadcast_to(inp["state_a_m"][0][sl].reshape(1, 16), (128, 16))
        sv[:, 32:128] = inp["state_c_conv"][0][sl].reshape(4, 3, 8, 128).transpose(3, 2, 0, 1).reshape(128, 96)
        sv[:, 128:160] = inp["state_c_h"][0][sl].reshape(4, 8, 128).transpose(2, 1, 0).reshape(128, 32)
        aC = np.ascontiguousarray(inp["state_a_C"][0][sl].reshape(16, 128, 128))
        in_maps.append(dict(xT=xT, cT=cT, wst=wst, cst=cst, pv=pv, sv=sv, adab=adab, biasT=biasT, aC=aC))
    res = run_bass_kernel_spmd(p.nc, in_maps, core_ids=list(range(8)))
    R = res.results
    if DEBUG.get("raw"):
        return R
    B = 8

    def cat(fn):
        return np.stack([fn(R[c]) for c in range(B)], axis=0)
    yT = cat(lambda r: r["o_yT"])
    y_all = yT.transpose(0, 3, 2, 1).reshape(B, SEQ + TS, D)
    y_prompt = np.ascontiguousarray(y_all[:, :SEQ])
    y_sample = np.ascontiguousarray(y_all[:, SEQ:].reshape(B * NSQ, LS, D))
    p_a_C = cat(lambda r: r["o_paC"])[None]
    p_a_n = cat(lambda r: r["o_pan"].T)[None]
    p_a_m = cat(lambda r: r["o_pam"][0])[None]
    p_b_k = cat(lambda r: r["o_pbk"].transpose(2, 1, 0).reshape(512, 8, 64))[None]
    p_b_v = cat(lambda r: r["o_pbv"].reshape(512, 8, 64))[None]
    p_c_conv = cat(lambda r: r["o_pconv"].transpose(2, 1, 0).reshape(3, D))[None]
    p_c_h = cat(lambda r: r["o_pch"].T.reshape(D))[None]
    s_a_C = cat(lambda r: r["o_saC"].reshape(4, 4, 128, 128)).reshape(1, 32, 4, 128, 128)
    s_a_n = cat(lambda r: r["o_san"].T.reshape(4, 4, 128)).reshape(1, 32, 4, 128)
    s_a_m = cat(lambda r: r["o_sam"][0].reshape(4, 4)).reshape(1, 32, 4)
    s_b_k = cat(lambda r: r["o_sbk"].reshape(128, 4, NSQ, LS).transpose(2, 3, 1, 0).reshape(NSQ, LS, 8, 64)).reshape(1, 32, LS, 8, 64)
    s_b_v = cat(lambda r: r["o_sbv"].reshape(NSQ, LS, 8, 64)).reshape(1, 32, LS, 8, 64)
    s_c_conv = cat(lambda r: r["o_sconv"].transpose(2, 3, 1, 0).reshape(NSQ, 3, D)).reshape(1, 32, 3, D)
    s_c_h = cat(lambda r: r["o_sch"].transpose(2, 1, 0).reshape(NSQ, D)).reshape(1, 32, D)
    outs = (y_prompt, y_sample, p_a_C, p_a_n, p_a_m, p_b_k, p_b_v, p_c_conv, p_c_h,
            s_a_C, s_a_n, s_a_m, s_b_k, s_b_v, s_c_conv, s_c_h)
    return tuple(np.ascontiguousarray(o, dtype=np.float32) for o in outs)
```

```python
import numpy as np
from contextlib import ExitStack
import concourse.bass as bass
import concourse.mybir as mybir
from concourse.bass_utils import run_bass_kernel_spmd

F32 = mybir.dt.float32
BF16 = mybir.dt.bfloat16
AF = mybir.ActivationFunctionType
ALU = mybir.AluOpType

D = 1024
NCH = 8
DFF = 2816
NJ = 22
SEQ = 2048
TP = 1024
TS = 64
NSQ = 4
LS = 16
TMAX = TP + TS
EPS = 1e-6
SLOT = 1024
NST = 4
NBF = 4
LOOK_D = 3
LOOK_C = 2
FW = 1104
FO_SPLIT = ((0, 8), (8, 15), (15, 22))
NEG = -30000.0

DEBUG = {}


class Tile:
    __slots__ = ("ap", "w", "r", "name")

    def __init__(self, ap, name=""):
        self.ap = ap
        self.w = None
        self.r = {}
        self.name = name


class TG:
    def __init__(self, ap, name=""):
        self.ap = ap
        self.name = name
        self.parts = [Tile(ap, name + "_p%d" % i) for i in range(3)]


class View(TG):
    def __init__(self, ap, parts, name=""):
        self.ap = ap
        self.name = name
        self.parts = list(parts)


def _flat(items):
    out = []
    for t in items:
        if isinstance(t, TG):
            out.extend(t.parts)
        elif isinstance(t, (list, tuple)):
            out.extend(_flat(t))
        else:
            out.append(t)
    return out


class Eng:
    def __init__(self, name, e, sem, step=1):
        self.name = name
        self.e = e
        self.sem = sem
        self.cnt = 0
        self.seen = {}
        self.step = step


class Builder:
    def __init__(self):
        self.nc = bass.Bass("TRN2", target_bir_lowering=False)
        self.es = ExitStack()
        nc = self.nc
        self.pe = Eng("pe", nc.tensor, self.sem("s_pe"))
        self.act = Eng("act", nc.scalar, self.sem("s_act"))
        self.dve = Eng("dve", nc.vector, self.sem("s_dve"))
        self.pool = Eng("pool", nc.gpsimd, self.sem("s_pool"))
        self.sp = Eng("sp", nc.sync, None)
        self.chans = [Eng("ch%d" % i, None, self.sem("s_ch%d" % i), 16) for i in range(12)]
        self.chan_i = 0
        self.units = []
        self.unit_keys = {}
        self.n_sb = 0
        self.dry = False

    def sem(self, name):
        return self.es.enter_context(self.nc.semaphore(name))

    def sb(self, shape, dtype, name=None):
        self.n_sb += 1
        t = self.es.enter_context(self.nc.sbuf_tensor("t_" + (name or ("sb%d" % self.n_sb)), list(shape), dtype))
        return t

    def tgroup(self, shape, dtype, name=None):
        t = self.sb(shape, dtype, name)
        return TG(t[tuple(slice(None) for _ in shape)], name or "")

    def tile(self, shape, dtype, name=None):
        t = self.sb(shape, dtype, name)
        return Tile(t[tuple(slice(None) for _ in shape)], name or "")

    def _waits(self, eng, reads, writes):
        need = {}

        def req(wc):
            if wc is None:
                return
            who, c = wc
            if need.get(who, 0) < c:
                need[who] = c
        for t in reads:
            req(t.w)
        for t in writes:
            req(t.w)
            for who, c in t.r.items():
                req((who, c))
        for who, c in need.items():
            if who is eng and eng is self.pe:
                continue
            if eng.seen.get(who, 0) >= c:
                continue
            assert c <= who.cnt, ("wait on a not-yet-emitted increment", eng.name, who.name, c, who.cnt)
            eng.e.wait_ge(who.sem, c)
            eng.seen[who] = c

    def op(self, eng, fn, reads=(), writes=(), inc=True):
        if self.dry:
            return None
        reads = _flat(reads)
        writes = _flat(writes)
        self._waits(eng, reads, writes)
        ins = fn()
        if inc:
            eng.cnt += 1
            ins.then_inc(eng.sem, 1)
            mark = eng.cnt
        else:
            mark = eng.cnt + 1
        for t in writes:
            t.w = (eng, mark)
            t.r = {}
        for t in reads:
            if t.r.get(eng, 0) < mark:
                t.r[eng] = mark
        return ins

    def dma(self, out_ap, in_ap, reads=(), writes=(), queue=None, chan=None):
        q = queue or self.sp
        if self.dry:
            return None
        reads = _flat(reads)
        writes = _flat(writes)
        if chan is None:
            chan = self.chans[self.chan_i % len(self.chans)]
            self.chan_i += 1
        if chan.cnt > 0 and q.seen.get(chan, 0) < chan.cnt:
            q.e.wait_ge(chan.sem, chan.cnt)
            q.seen[chan] = chan.cnt
        self._waits(q, reads, writes)
        ins = q.e.dma_start(out=out_ap, in_=in_ap)
        chan.cnt += 16
        ins.then_inc(chan.sem, 16)
        for t in writes:
            t.w = (chan, chan.cnt)
            t.r = {}
        for t in reads:
            t.r[chan] = chan.cnt
        return chan

    def finish(self):
        if self.dry:
            return
        for ch in self.chans + list(self.stream_ch):
            if ch.cnt > 0:
                self.sp.e.wait_ge(ch.sem, ch.cnt)
        for e in (self.pe, self.act, self.dve, self.pool):
            if e.cnt > 0:
                self.sp.e.wait_ge(e.sem, e.cnt)

    def mm(self, out_t, out_ap, l_t, l_ap, r_t, r_ap, start, stop, inc=None):
        if inc is None:
            inc = stop
        return self.op(self.pe, lambda: self.nc.tensor.matmul(out_ap, lhsT=l_ap, rhs=r_ap, start=start, stop=stop),
                       reads=[l_t, r_t], writes=[out_t], inc=inc)

    def A(self, out_t, out_ap, in_t, in_ap, func, bias=0.0, scale=1.0, extra=()):
        return self.op(self.act, lambda: self.nc.scalar.activation(out=out_ap, in_=in_ap, func=func, bias=bias, scale=scale),
                       reads=[in_t] + list(extra), writes=[out_t])

    def V(self, eng, meth, out_t, reads, **kw):
        e = eng.e
        outs = list(out_t) if isinstance(out_t, (list, tuple)) else [out_t]
        return self.op(eng, lambda: getattr(e, meth)(**kw), reads=list(reads), writes=outs)

    def plan_unit(self, key, n, cast=True, src="w"):
        assert n <= SLOT, (key, n)
        self.units.append(dict(key=key, n=n, cast=cast, src=src))

    def stream_setup(self, wdram, cdram, offs):
        self.stream_ch = [Eng("st%d" % i, None, self.sem("s_st%d" % i), 16) for i in range(NST)]
        self.st_tiles = [self.tile([128, SLOT], F32, "stg%d" % i) for i in range(NST)]
        self.bf_tiles = [self.tile([128, SLOT], BF16, "wbf%d" % i) for i in range(NBF)]
        self.wdram = wdram
        self.cdram = cdram
        self.offs = offs
        self.s_dma = 0
        self.s_cast = 0
        self.s_next = 0

    def _issue_dma(self, k):
        u = self.units[k]
        st = self.st_tiles[k % NST]
        src = self.wdram if u["src"] == "w" else self.cdram
        off = self.offs[(u["src"], u["key"])]
        self.dma(st.ap[:, 0:u["n"]], src[:, off:off + u["n"]], writes=[st], chan=self.stream_ch[k % NST])

    def _issue_cast(self, k):
        u = self.units[k]
        if not u["cast"]:
            return
        st = self.st_tiles[k % NST]
        bf = self.bf_tiles[k % NBF]
        n = u["n"]
        if u["key"][0] in ("f_in", "f_out", "ada"):
            self.n_cast = getattr(self, "n_cast", 0) + 1
            if self.n_cast % 2:
                self.op(self.act, lambda: self.nc.scalar.activation(out=bf.ap[:, 0:n], in_=st.ap[:, 0:n], func=AF.Copy), reads=[st], writes=[bf])
            else:
                self.op(self.dve, lambda: self.nc.vector.tensor_copy(out=bf.ap[:, 0:n], in_=st.ap[:, 0:n]), reads=[st], writes=[bf])
        else:
            self.op(self.act, lambda: self.nc.scalar.activation(out=bf.ap[:, 0:n], in_=st.ap[:, 0:n], func=AF.Copy), reads=[st], writes=[bf])

    def get(self, key):
        if self.dry:
            kind = key[0]
            n = {"ada": 1024, "f_in": 1024, "ab_gate": 64, "ab_in": 1024, "ab_out": 1024, "c_in": 1024, "c_gate": 256, "c_out": 1024, "kc": 512, "vc": 512}.get(kind)
            if kind == "f_out":
                j0, j1 = FO_SPLIT[key[4]]
                n = (j1 - j0) * 128
            self.plan_unit(key, n, src=("c" if kind in ("kc", "vc") else "w"))
            return self.bf_tiles[0]
        k = self.s_next
        u = self.units[k]
        assert u["key"] == key, (u["key"], key)
        last = len(self.units) - 1
        while self.s_dma <= min(k + LOOK_D, last):
            self._issue_dma(self.s_dma)
            self.s_dma += 1
        while self.s_cast <= min(k + LOOK_C, last):
            self._issue_cast(self.s_cast)
            self.s_cast += 1
        self.s_next += 1
        t = self.bf_tiles[k % NBF] if u["cast"] else self.st_tiles[k % NST]
        return t


class Prog(Builder):
    def __init__(self):
        super().__init__()
        self.declare_dram()
        self.alloc()
        self.dry = True
        self.build_body()
        self.dry = False
        self.ps_i = 0
        self.n_cast = 0
        self.offs = {}
        self.wtotal = 0
        self.ctotal = 0
        for u in self.units:
            k = (u["src"], u["key"])
            if k in self.offs:
                continue
            if u["src"] == "w":
                self.offs[k] = self.wtotal
                self.wtotal += u["n"]
            else:
                self.offs[k] = self.ctotal
                self.ctotal += u["n"]
        self.d_w = self.nc.dram_tensor("wst", [128, max(self.wtotal, 1)], F32, kind="ExternalInput").ap()
        self.d_c = self.nc.dram_tensor("cst", [128, max(self.ctotal, 1)], F32, kind="ExternalInput").ap()
        self.wdram, self.cdram = self.d_w, self.d_c

    def pump(self, n):
        lst = getattr(self, "ada_pending", None)
        for _ in range(n):
            if not lst:
                return
            l, cbk = lst.pop(0)
            self.ada_block(l, cbk)

    PV = dict(g0=0, g1=16, g2=32, aon=48, gq=52, gk=53, gbias=54, convw=62, convb=94, gateb=102, lam=118)
    NPV = 126
    SV = dict(an=0, am=16, conv=32, ch=128)
    NSV = 160

    def declare_dram(self):
        nc = self.nc
        di = lambda n, s: nc.dram_tensor(n, list(s), F32, kind="ExternalInput").ap()
        do = lambda n, s: nc.dram_tensor(n, list(s), F32, kind="ExternalOutput").ap()
        self.d_xT = di("xT", [128, NCH, SEQ + TS])
        self.d_cT = di("cT", [128, NCH, 5])
        self.d_pv = di("pv", [128, self.NPV])
        self.d_sv = di("sv", [128, self.NSV])
        self.d_adab = di("adab", [128, 2, 72])
        self.d_bias = di("biasT", [4, 128, 2, 640])
        self.d_aC = di("aC", [16, 128, 128])
        self.o_yT = do("o_yT", [128, NCH, SEQ + TS])
        self.o_paC = do("o_paC", [4, 128, 128])
        self.o_pan = do("o_pan", [128, 4])
        self.o_pam = do("o_pam", [1, 4])
        self.o_pbk = do("o_pbk", [128, 4, 512])
        self.o_pbv = do("o_pbv", [4, 128, 512])
        self.o_pconv = do("o_pconv", [128, NCH, 3])
        self.o_pch = do("o_pch", [128, NCH])
        self.o_saC = do("o_saC", [16, 128, 128])
        self.o_san = do("o_san", [128, 16])
        self.o_sam = do("o_sam", [1, 16])
        self.o_sbk = do("o_sbk", [128, 4, TS])
        self.o_sbv = do("o_sbv", [NSQ, LS, 512])
        self.o_sconv = do("o_sconv", [128, NCH, NSQ, 3])
        self.o_sch = do("o_sch", [128, NCH, NSQ])
        if DEBUG.get("dbg"):
            self.o_dbg = do("o_dbg", [128, NCH, TMAX])

    def alloc(self):
        nc = self.nc
        self.x = [self.tile([128, TMAX], F32, "x%d" % c) for c in range(NCH)]
        self.h = [self.tgroup([128, TMAX], BF16, "h%d" % c) for c in range(NCH)]
        self.a = []
        self.av32 = []
        for j in range(NJ // 2):
            t = self.sb([128, 2, TMAX], BF16, "apair%d" % j)
            g0, g1 = TG(t[:, 0, :], "a%d" % (2 * j)), TG(t[:, 1, :], "a%d" % (2 * j + 1))
            self.a += [g0, g1]
            self.av32.append(View(t[:, :, :].rearrange("p a b -> p (a b)").bitcast(F32), g0.parts + g1.parts, "av%d" % j))
        self.F = [self.tgroup([128, FW], F32, "F%d" % i) for i in range(7)]
        self.sg = [self.tile([128, 512], F32, "sg%d" % i) for i in range(2)]
        self.ps2 = [self.es.enter_context(nc.psum_tensor("ps%d" % i, [128, 1024], F32)) for i in range(4)]
        self.ps = []
        for i in range(4):
            self.ps.append(Tile(self.ps2[i][:, 0:512], "psb%d" % (2 * i)))
            self.ps.append(Tile(self.ps2[i][:, 512:1024], "psb%d" % (2 * i + 1)))
        self.ps_i = 0
        self.stream_setup(None, None, None)
        self.pv = self.tile([128, self.NPV], F32, "pv")
        self.sv = self.tile([128, self.NSV], F32, "sv")
        self.cT = self.tile([128, NCH, 5], F32, "cT")
        self.cTb = self.tile([128, NCH, 5], BF16, "cTb")
        self.cst = self.tile([128, 8], F32, "cst")
        self.ones_bf = self.tile([128, 128], BF16, "ones_bf")
        self.blk_bf = self.tile([128, 128], BF16, "blk_bf")
        self.ones_w = self.tile([128, TP], BF16, "ones_w")
        self.ident = self.tile([128, 128], F32, "ident")
        self.adaT = [self.tile([128, 72, 5], F32, "adaT%d" % l) for l in range(2)]
        self.adatok = [self.F[4], self.F[5]]
        self.adabT = self.tile([128, 2, 72], F32, "adabT")
        self.gs = [[self.tile([128, NCH, 5], F32, "gs%d_%d" % (l, m)) for m in range(3)] for l in range(2)]
        self.gt = [[self.tile([128, NCH, 5], F32, "gt%d_%d" % (l, m)) for m in range(3)] for l in range(2)]
        self.GS_tok = self.tile([128, NCH, TS], F32, "GS_tok")
        self.SH_tok = self.tile([128, NCH, TS], F32, "SH_tok")
        self.GT_tok = self.tile([128, NCH, TS], F32, "GT_tok")
        self.small = self.tile([128, 64], F32, "small")
        self.C = [self.tile([128, 128], F32, "C%d" % i) for i in range(4)]
        self.Cb = [self.tile([128, 128], BF16, "Cb%d" % i) for i in range(4)]
        self.nr = [self.tile([128, 128], F32, "nr%d" % i) for i in range(4)]
        self.nrb = [self.tile([128, 128], BF16, "nrb%d" % i) for i in range(4)]
        self.Gl = self.tile([128, 4], F32, "Gl")
        self.Ml = self.tile([128, 4], F32, "Ml")
        self.Cs4 = self.tile([128, NSQ, 128], F32, "Cs4")
        self.Cs4b = self.tile([128, NSQ, 128], BF16, "Cs4b")
        self.nr4b = self.tile([128, NSQ, 128], BF16, "nr4b")
        self.kw4 = self.tile([16, NSQ * 128], BF16, "kw4")
        self.cs4 = self.tile([128, 32], F32, "cs4")
        self.gw = self.tile([128, 64], BF16, "gw")
        self.wrep = self.tile([128, 1024], BF16, "wrep")
        self.cols = self.tile([128, 16], F32, "cols")
        self.m128 = [self.tile([128, 128], F32, "m128_%d" % i) for i in range(4)]
        self.b128 = [self.tile([128, 128], BF16, "b128_%d" % i) for i in range(3)]
        self.tokb = [self.tile([16, 512], BF16, "tokb%d" % i) for i in range(4)]
        self.tokf = self.sg[1]
        self.maskb = self.tile([128, 128], F32, "maskb")
        self.c8 = self.tgroup([128, 64], F32, "c8")
        self.outs = self.tile([128, 64], F32, "outs")
        self.kband = [self.tile([128, 512], BF16, "kband%d" % i) for i in range(4)]
        self.vband = self.tile([128, 4, 512], BF16, "vband")
        self.biasT = self.tile([128, 2, 640], F32, "biasTs")
        self.convc = self.tile([128, NCH, 3], F32, "convc")
        self.hc = self.tile([128, NCH], F32, "hc")
        self.XP = self.F[6]
        self.sconv = self.tile([128, NCH, NSQ, 3], F32, "sconv")
        self.sch = self.tile([128, NCH, NSQ], F32, "sch")

    def next_ps(self):
        if self.ps_i == 0 and DEBUG.get("verbose"):
            print("sbuf bytes remaining", self.nc.sbuf_bytes_remaining())
        t = self.ps[self.ps_i % 8]
        self.ps_i += 1
        return t

    def next_ps2(self):
        if self.ps_i % 2:
            self.ps_i += 1
        i = (self.ps_i % 8) // 2
        self.ps_i += 2
        return self.ps[2 * i], self.ps[2 * i + 1], self.ps2[i]

    def pvc(self, name, i=0, n=1):
        o = self.PV[name] + i
        return self.pv.ap[:, o:o + n]

    def svc(self, name, i=0, n=1):
        o = self.SV[name] + i
        return self.sv.ap[:, o:o + n]

    def cbs(self, pi):
        r = [(0, 512), (512, 1024)]
        if pi == 1:
            r.append((TP, TP + TS))
        return r

    def setup(self):
        nc = self.nc
        q = self.act
        self.dma(self.pv.ap[:, :], self.d_pv, writes=[self.pv], queue=q)
        self.dma(self.sv.ap[:, :], self.d_sv, writes=[self.sv], queue=q)
        self.dma(self.cT.ap[:, :, :], self.d_cT, writes=[self.cT], queue=q)
        self.dma(self.adabT.ap[:, :, :], self.d_adab, writes=[self.adabT], queue=q)
        P, D_ = self.pool, self.dve
        self.A(self.cTb, self.cTb.ap[:, :, :], self.cT, self.cT.ap[:, :, :], AF.Copy)
        c = self.cst
        for i, v in enumerate([EPS, 1.0, 0.5, 0.0]):
            self.V(P, "memset", c, [], ap=c.ap[:, i:i + 1], constant=v)
        self.V(P, "memset", self.ones_bf, [], ap=self.ones_bf.ap[:, :], constant=1.0)
        self.V(P, "memset", self.ones_w, [], ap=self.ones_w.ap[:, :], constant=1.0)
        self.V(P, "memset", self.blk_bf, [], ap=self.blk_bf.ap[:, :], constant=0.0)
        self.V(P, "memset", self.blk_bf, [], ap=self.blk_bf.ap[0:64, 0:64], constant=1.0)
        self.V(P, "memset", self.blk_bf, [], ap=self.blk_bf.ap[64:128, 64:128], constant=1.0)
        self.reg_neg = None if self.dry else nc.gpsimd.to_reg(NEG)
        self.V(P, "memset", self.ident, [], ap=self.ident.ap[:, :], constant=1.0)
        self.V(P, "affine_select", self.ident, [self.ident], out=self.ident.ap[:, :], in_=self.ident.ap[:, :],
               pattern=[[1, 128]], compare_op=ALU.is_equal, fill=0.0, base=0, channel_multiplier=-1)
        self.V(P, "memset", self.maskb, [], ap=self.maskb.ap[:, :], constant=0.0)
        self.V(P, "affine_select", self.maskb, [self.maskb], out=self.maskb.ap[:, :], in_=self.maskb.ap[:, :],
               pattern=[[1, 128]], compare_op=ALU.is_ge, fill=self.reg_neg, base=0, channel_multiplier=-1)
        for t in (self.convc, self.hc, self.Gl, self.Ml):
            self.V(P, "memset", t, [], ap=t.ap, constant=0.0)
        for hd in range(4):
            for t in (self.C[hd], self.nr[hd]):
                self.V(P, "memset", t, [], ap=t.ap[:, :], constant=0.0)
            for t in (self.Cb[hd], self.nrb[hd]):
                self.V(P, "memset", t, [], ap=t.ap[:, :], constant=0.0)
        s = self.small
        self.A(s, s.ap[:, 0:8], self.pv, self.pvc("lam", 0, 8), AF.Exp, scale=-1.0)
        self.A(s, s.ap[:, 0:8], s, s.ap[:, 0:8], AF.Ln, bias=self.cst.ap[:, 1:2], extra=[self.cst])
        self.V(D_, "tensor_scalar", s, [s], out=s.ap[:, 0:8], in0=s.ap[:, 0:8], scalar1=-4.0, scalar2=None, op0=ALU.mult)
        self.V(D_, "tensor_scalar", s, [self.pv], out=s.ap[:, 8:24], in0=self.pvc("gateb", 0, 16), scalar1=0.5, scalar2=None, op0=ALU.mult)
        self.V(D_, "tensor_scalar", s, [self.pv], out=s.ap[:, 24:28], in0=self.pvc("aon", 0, 4), scalar1=0.5, scalar2=None, op0=ALU.mult)
        self.V(D_, "tensor_scalar", s, [self.pv], out=s.ap[:, 28:29], in0=self.pvc("gq"), scalar1=0.125, scalar2=None, op0=ALU.mult)
        self.V(D_, "tensor_scalar", s, [self.pv], out=s.ap[:, 30:38], in0=self.pvc("gbias", 0, 8), scalar1=-1.0, scalar2=None, op0=ALU.mult)

    def ada_block(self, l, cbk):
        nc = self.nc
        D_ = self.dve
        aT = self.adaT[l]
        ps = self.next_ps()
        for kq in range(4):
            w = self.get(("ada", l, cbk, kq))
            for i in range(2):
                k = 2 * kq + i
                self.mm(ps, ps.ap[0:5, 0:512], self.cTb, self.cTb.ap[:, k, :], w, w.ap[:, i * 512:(i + 1) * 512],
                        start=(k == 0), stop=(k == 7), inc=(i == 1))
        tok = self.adatok[cbk % 2]
        self.A(tok, tok.ap[0:5, 0:512], ps, ps.ap[0:5, 0:512], AF.Copy)
        ps2 = self.next_ps()
        for i in range(4):
            self.op(self.pe, lambda i=i: nc.tensor.transpose(ps2.ap[:, i * 8:i * 8 + 5], tok.ap[0:5, i * 128:(i + 1) * 128], self.ident.ap[0:5, 0:5]),
                    reads=[tok, self.ident], writes=[ps2], inc=(i == 3))
        self.V(D_, "tensor_tensor", aT, [ps2, self.adabT], out=aT.ap[:, cbk * 4:(cbk + 1) * 4, :],
               in0=ps2.ap[:, 0:32].rearrange("p (a b) -> p a b", b=8)[:, :, 0:5],
               in1=self.adabT.ap[:, l, cbk * 4:(cbk + 1) * 4].unsqueeze(2).to_broadcast([128, 4, 5]), op=ALU.add)

    def ada_derive(self, l, ms=(0, 1, 2), scale=True, gate=True):
        D_ = self.dve
        aT = self.adaT[l]
        for m in ms:
            gname = ("g0", "g1", "g2")[m]
            gs, gt = self.gs[l][m], self.gt[l][m]
            if scale:
                self.V(D_, "tensor_scalar", gs, [aT], out=gs.ap[:, :, :], in0=aT.ap[:, (3 * m + 1) * 8:(3 * m + 2) * 8, :],
                       scalar1=1.0, scalar2=None, op0=ALU.add)
                self.V(D_, "tensor_tensor", gs, [gs, self.pv], out=gs.ap[:, :, :], in0=gs.ap[:, :, :],
                       in1=self.pvc(gname, l * 8, 8).unsqueeze(2).to_broadcast([128, NCH, 5]), op=ALU.mult)
            if gate:
                self.V(D_, "tensor_scalar", gt, [aT], out=gt.ap[:, :, :], in0=aT.ap[:, (3 * m + 2) * 8:(3 * m + 3) * 8, :],
                       scalar1=(1.0 if m == 1 else 0.5), scalar2=None, op0=ALU.mult)

    def shift_col(self, l, m, c, r=0):
        return self.adaT[l].ap[:, 3 * m * 8 + c, r:r + 1]

    def expand_tok(self, l, m):
        D_ = self.dve
        aT = self.adaT[l]
        for dst, src_t, src in ((self.GS_tok, self.gs[l][m], self.gs[l][m].ap[:, :, 1:5]),
                                (self.SH_tok, aT, aT.ap[:, 3 * m * 8:(3 * m + 1) * 8, 1:5]),
                                (self.GT_tok, self.gt[l][m], self.gt[l][m].ap[:, :, 1:5])):
            for c in range(NCH):
                self.V(D_, "tensor_copy", dst, [src_t], out=dst.ap[:, c, :].rearrange("p (s t) -> p s t", t=LS),
                       in_=src[:, c, :].unsqueeze(2).to_broadcast([128, NSQ, LS]))

    def norm_mod(self, l, m, pi):
        D_ = self.dve
        T = TP + (TS if pi == 1 else 0)
        if pi == 1:
            self.expand_tok(l, m)
        sq = [self.a[14 + c] for c in range(NCH)]
        for c in range(NCH):
            self.A(sq[c], sq[c].ap[:, 0:T], self.x[c], self.x[c].ap[:, 0:T], AF.Square)
        lnt, rstd = self.F[0], self.F[1]
        for gi, (c0, c1) in enumerate(self.cbs(pi)):
            ps = self.next_ps()
            for c in range(NCH):
                self.mm(ps, ps.ap[:, 0:c1 - c0], self.ones_bf, self.ones_bf.ap[:, :], sq[c], sq[c].ap[:, c0:c1], start=(c == 0), stop=(c == 7))
            self.A(lnt.parts[gi], lnt.ap[:, c0:c1], ps, ps.ap[:, 0:c1 - c0], AF.Ln, bias=self.cst.ap[:, 0:1], scale=1.0 / D, extra=[self.cst])
            self.A(rstd.parts[gi], rstd.ap[:, c0:c1], lnt.parts[gi], lnt.ap[:, c0:c1], AF.Exp, scale=-0.5)
        for gi, (c0, c1) in enumerate(self.cbs(pi)):
            for c in range(NCH):
                tmp = self.F[2 + (c % 2)]
                tp, hp_, rp = tmp.parts[gi], self.h[c].parts[gi], rstd.parts[gi]
                if c0 < TP:
                    self.V(D_, "scalar_tensor_tensor", tp, [self.x[c], self.gs[l][m], rp], out=tmp.ap[:, c0:c1], in0=self.x[c].ap[:, c0:c1],
                           scalar=self.gs[l][m].ap[:, c, 0:1], in1=rstd.ap[:, c0:c1], op0=ALU.mult, op1=ALU.mult)
                    self.A(hp_, self.h[c].ap[:, c0:c1], tp, tmp.ap[:, c0:c1], AF.Identity, bias=self.shift_col(l, m, c), extra=[self.adaT[l]])
                else:
                    self.V(D_, "tensor_tensor", tp, [self.x[c], rp], out=tmp.ap[:, TP:T], in0=self.x[c].ap[:, TP:T], in1=rstd.ap[:, TP:T], op=ALU.mult)
                    self.V(D_, "tensor_tensor", tp, [tp, self.GS_tok], out=tmp.ap[:, TP:T], in0=tmp.ap[:, TP:T], in1=self.GS_tok.ap[:, c, :], op=ALU.mult)
                    self.V(D_, "tensor_tensor", hp_, [tp, self.SH_tok], out=self.h[c].ap[:, TP:T], in0=tmp.ap[:, TP:T], in1=self.SH_tok.ap[:, c, :], op=ALU.add)

    def resid(self, l, m, pi, i, ps, c0, c1):
        D_ = self.dve
        x = self.x[i]
        if c0 < TP:
            e = min(c1, TP)
            self.V(D_, "scalar_tensor_tensor", x, [ps, self.gt[l][m], x], out=x.ap[:, c0:e], in0=ps.ap[:, 0:e - c0],
                   scalar=self.gt[l][m].ap[:, i, 0:1], in1=x.ap[:, c0:e], op0=ALU.mult, op1=ALU.add)
        if c1 > TP:
            b = max(c0, TP)
            tmp = self.sg[0]
            self.V(D_, "tensor_tensor", tmp, [ps, self.GT_tok], out=tmp.ap[:, 0:c1 - b], in0=ps.ap[:, b - c0:c1 - c0], in1=self.GT_tok.ap[:, i, b - TP:c1 - TP], op=ALU.mult)
            self.V(D_, "tensor_tensor", x, [tmp, x], out=x.ap[:, b:c1], in0=tmp.ap[:, 0:c1 - b], in1=x.ap[:, b:c1], op=ALU.add)

    def parts_of(self, tg, c0, c1):
        ps = []
        if c0 < 512:
            ps.append(tg.parts[0])
        if c1 > 512 and c0 < 1024:
            ps.append(tg.parts[1])
        if c1 > 1024:
            ps.append(tg.parts[2])
        return ps

    def ffn(self, l, f, pi):
        D_ = self.dve
        m = 0 if f == 1 else 2
        cbs = self.cbs(pi) if pi == 0 else [(0, 363), (363, 726), (726, TMAX)]
        self.norm_mod(l, m, pi)
        sgi = 0
        step = 0
        pumping = (pi == 0 and l == 0 and f == 1)
        for j in range(NJ):
            wg = self.get(("f_in", l, f, j, 0))
            pg = [self.next_ps() for _ in cbs]
            for ci, (c0, c1) in enumerate(cbs):
                for k in range(NCH):
                    self.mm(pg[ci], pg[ci].ap[:, 0:c1 - c0], wg, wg.ap[:, k * 128:(k + 1) * 128], self.parts_of(self.h[k], c0, c1), self.h[k].ap[:, c0:c1], start=(k == 0), stop=(k == 7))
            wu = self.get(("f_in", l, f, j, 1))
            pu = [self.next_ps() for _ in cbs]
            for ci, (c0, c1) in enumerate(cbs):
                for k in range(NCH):
                    self.mm(pu[ci], pu[ci].ap[:, 0:c1 - c0], wu, wu.ap[:, k * 128:(k + 1) * 128], self.parts_of(self.h[k], c0, c1), self.h[k].ap[:, c0:c1], start=(k == 0), stop=(k == 7))
            for ci, (c0, c1) in enumerate(cbs):
                sg = self.sg[sgi % 2]
                sgi += 1
                n = c1 - c0
                self.A(sg, sg.ap[:, 0:n], pg[ci], pg[ci].ap[:, 0:n], AF.Silu)
                self.V(D_, "tensor_tensor", self.parts_of(self.a[j], c0, c1), [sg, pu[ci]], out=self.a[j].ap[:, c0:c1], in0=sg.ap[:, 0:n], in1=pu[ci].ap[:, 0:n], op=ALU.mult)
            if pumping and step % 4 == 0:
                self.pump(1)
            step += 1
        if pumping:
            self.ada_derive(0, (0,), scale=False)
        for i in range(NCH):
            ps = [self.next_ps() for _ in cbs]
            for hf, (j0, j1) in enumerate(FO_SPLIT):
                w = self.get(("f_out", l, f, i, hf))
                for ci, (c0, c1) in enumerate(cbs):
                    for j in range(j0, j1):
                        self.mm(ps[ci], ps[ci].ap[:, 0:c1 - c0], w, w.ap[:, (j - j0) * 128:(j - j0 + 1) * 128], self.parts_of(self.a[j], c0, c1), self.a[j].ap[:, c0:c1],
                                start=(j == 0), stop=(j == NJ - 1), inc=(j == j1 - 1))
            for ci, (c0, c1) in enumerate(cbs):
                self.resid(l, m, pi, i, ps[ci], c0, c1)
            if pumping and step % 4 == 0:
                self.pump(1)
            step += 1
        if pumping:
            self.ada_derive(0, (1,))

    def load_x(self, pi):
        for c in range(NCH):
            self.dma(self.x[c].ap[:, 0:TP], self.d_xT[:, c, pi * TP:(pi + 1) * TP], writes=[self.x[c]], queue=self.act)
            if pi == 1:
                self.dma(self.x[c].ap[:, TP:TMAX], self.d_xT[:, c, SEQ:SEQ + TS], writes=[self.x[c]], queue=self.act)

    def store_x(self, pi, dst=None):
        dst = dst if dst is not None else self.o_yT
        for c in range(NCH):
            self.dma(dst[:, c, pi * TP:(pi + 1) * TP], self.x[c].ap[:, 0:TP], reads=[self.x[c]])
            if pi == 1:
                self.dma(dst[:, c, SEQ:SEQ + TS], self.x[c].ap[:, TP:TMAX], reads=[self.x[c]])

    def interleave(self, *gens):
        gens = list(gens)
        while gens:
            for g in list(gens):
                try:
                    next(g)
                except StopIteration:
                    gens.remove(g)

    def interleave_g(self, *gens):
        gens = list(gens)
        while gens:
            for g in list(gens):
                try:
                    next(g)
                    yield
                except StopIteration:
                    gens.remove(g)

    def mixer1_chunk(self, pi, n, bufs):
        D_ = self.dve
        T = TP + (TS if pi == 1 else 0)
        cbs = self.cbs(pi)
        XP = self.XP
        xs0 = 3 + TP
        XPs = XP.ap[:, xs0:xs0 + 76].rearrange("p (s t) -> p s t", t=19)
        small = self.small
        xc, tr, ti, av, tq, gbs, xcb = bufs
        hs = ti
        xcs = xc.ap[:, TP:TP + TS].rearrange("p (s t) -> p s t", t=LS)
        self.V(D_, "tensor_copy", XP, [self.convc], out=XP.ap[:, 0:3], in_=self.convc.ap[:, n, :])
        if pi == 1:
            self.V(D_, "tensor_copy", XP, [self.sv], out=XPs[:, :, 0:3], in_=self.svc("conv", n * 12, 12).rearrange("p (s t) -> p s t", t=3))
        yield
        w = self.get(("c_in", 0, n))
        for (c0, c1) in cbs:
            ps = self.next_ps()
            for k in range(NCH):
                self.mm(ps, ps.ap[:, 0:c1 - c0], w, w.ap[:, k * 128:(k + 1) * 128], self.h[k].parts[0 if c0 < 512 else (1 if c0 < 1024 else 2)], self.h[k].ap[:, c0:c1], start=(k == 0), stop=(k == 7))
            self.A(gbs, gbs.ap[:, c0:c1], ps, ps.ap[:, 0:c1 - c0], AF.Copy)
            self.A(tq, tq.ap[:, c0:c1], ps, ps.ap[:, 0:c1 - c0], AF.Square)
            yield
        w = self.get(("c_in", 1, n))
        for (c0, c1) in cbs:
            ps = self.next_ps()
            for k in range(NCH):
                self.mm(ps, ps.ap[:, 0:c1 - c0], w, w.ap[:, k * 128:(k + 1) * 128], self.h[k].parts[0 if c0 < 512 else (1 if c0 < 1024 else 2)], self.h[k].ap[:, c0:c1], start=(k == 0), stop=(k == 7))
            if c0 < TP:
                self.A(XP, XP.ap[:, 3 + c0:3 + c1], ps, ps.ap[:, 0:c1 - c0], AF.Copy)
            else:
                self.A(XP, XPs[:, :, 3:19], ps, ps.ap[:, 0:TS].rearrange("p (s t) -> p s t", t=LS), AF.Copy)
            yield

        def gelu_gen():
            self.V(D_, "tensor_scalar", tq, [tq], out=tq.ap[:, 0:T], in0=tq.ap[:, 0:T], scalar1=0.044715, scalar2=1.0, op0=ALU.mult, op1=ALU.add)
            yield
            self.V(D_, "tensor_tensor", tq, [tq, gbs], out=tq.ap[:, 0:T], in0=tq.ap[:, 0:T], in1=gbs.ap[:, 0:T], op=ALU.mult)
            yield
            yield
            self.A(tq, tq.ap[:, 0:T], tq, tq.ap[:, 0:T], AF.Tanh, scale=0.7978845608028654)
            yield
            yield
            self.V(D_, "scalar_tensor_tensor", tq, [tq, gbs], out=tq.ap[:, 0:T], in0=tq.ap[:, 0:T], scalar=1.0, in1=gbs.ap[:, 0:T], op0=ALU.add, op1=ALU.mult)
            yield

        half = []

        def xb_gen():
            cw = lambda j: self.pvc("convw", n * 4 + j)
            self.V(D_, "tensor_scalar", xc, [XP, self.pv], out=xc.ap[:, 0:TP], in0=XP.ap[:, 0:TP], scalar1=cw(0), scalar2=self.pvc("convb", n), op0=ALU.mult, op1=ALU.add)
            yield
            for j in range(1, 4):
                self.V(D_, "scalar_tensor_tensor", xc, [XP, self.pv, xc], out=xc.ap[:, 0:TP], in0=XP.ap[:, j:j + TP], scalar=cw(j), in1=xc.ap[:, 0:TP], op0=ALU.mult, op1=ALU.add)
                yield
            if pi == 1:
                self.V(D_, "tensor_scalar", xc, [XP, self.pv], out=xcs, in0=XPs[:, :, 0:LS], scalar1=cw(0), scalar2=self.pvc("convb", n), op0=ALU.mult, op1=ALU.add)
                for j in range(1, 4):
                    self.V(D_, "scalar_tensor_tensor", xc, [XP, self.pv, xc], out=xcs, in0=XPs[:, :, j:j + LS], scalar=cw(j), in1=xcs, op0=ALU.mult, op1=ALU.add)
                yield
            self.V(D_, "tensor_copy", self.convc, [XP], out=self.convc.ap[:, n, :], in_=XP.ap[:, TP:TP + 3])
            if pi == 1:
                self.V(D_, "tensor_copy", self.sconv, [XP], out=self.sconv.ap[:, n, :, :], in_=XPs[:, :, 16:19])
            self.A(xcb, xcb.ap[:, 0:T], xc, xc.ap[:, 0:T], AF.Copy)
            yield
            wg = self.get(("c_gate", n))
            for gi, dst in ((0, tr), (1, ti)):
                for (c0, c1) in cbs:
                    ps = self.next_ps()
                    self.mm(ps, ps.ap[:, 0:c1 - c0], wg, wg.ap[:, gi * 128:(gi + 1) * 128], xcb, xcb.ap[:, c0:c1], start=True, stop=True)
                    self.A(dst, dst.ap[:, c0:c1], ps, ps.ap[:, 0:c1 - c0], AF.Tanh, bias=small.ap[:, 8 + n * 2 + gi:9 + n * 2 + gi], scale=0.5, extra=[small])
                yield
            half.append(1)
            c1h = small.ap[:, n:n + 1]
            self.A(av, av.ap[:, 0:T], tr, tr.ap[:, 0:T], AF.Exp, bias=c1h, scale=c1h, extra=[small])
            yield
            self.V(D_, "scalar_tensor_tensor", tr, [av], out=tr.ap[:, 0:T], in0=av.ap[:, 0:T], scalar=0.99999994, in1=av.ap[:, 0:T], op0=ALU.min, op1=ALU.mult)
            yield
            self.A(tr, tr.ap[:, 0:T], tr, tr.ap[:, 0:T], AF.Sqrt, bias=self.cst.ap[:, 1:2], scale=-1.0, extra=[self.cst])
            self.V(D_, "scalar_tensor_tensor", ti, [ti, xc], out=ti.ap[:, 0:T], in0=ti.ap[:, 0:T], scalar=1.0, in1=xc.ap[:, 0:T], op0=ALU.add, op1=ALU.mult)
            yield
            self.V(D_, "scalar_tensor_tensor", ti, [ti, tr], out=ti.ap[:, 0:T], in0=ti.ap[:, 0:T], scalar=0.5, in1=tr.ap[:, 0:T], op0=ALU.mult, op1=ALU.mult)
            yield
            self.V(D_, "tensor_tensor_scan", hs, [av, ti, self.hc], out=hs.ap[:, 0:TP], data0=av.ap[:, 0:TP], data1=ti.ap[:, 0:TP],
                   initial=self.hc.ap[:, n:n + 1], op0=ALU.mult, op1=ALU.add)
            self.V(D_, "tensor_copy", self.hc, [hs], out=self.hc.ap[:, n:n + 1], in_=hs.ap[:, TP - 1:TP])
            if pi == 1:
                for sq in range(NSQ):
                    cs = TP + LS * sq
                    self.V(D_, "tensor_tensor_scan", hs, [av, ti, self.sv], out=hs.ap[:, cs:cs + LS], data0=av.ap[:, cs:cs + LS], data1=ti.ap[:, cs:cs + LS],
                           initial=self.svc("ch", n * 4 + sq), op0=ALU.mult, op1=ALU.add)
                self.V(D_, "tensor_copy", self.sch, [hs], out=self.sch.ap[:, n, :], in_=hs.ap[:, TP:TP + TS].rearrange("p (s t) -> p s t", t=LS)[:, :, LS - 1])
            yield
        sent = False
        for _ in self.interleave_g(xb_gen(), gelu_gen()):
            if half and not sent:
                sent = True
                yield "HALF"
            else:
                yield
        self.V(D_, "scalar_tensor_tensor", self.a[n], [tq, hs], out=self.a[n].ap[:, 0:T], in0=tq.ap[:, 0:T], scalar=0.5, in1=hs.ap[:, 0:T], op0=ALU.mult, op1=ALU.mult)
        yield

    def mixer1(self, pi):
        l, m = 1, 1
        self.norm_mod(l, m, pi)
        F = self.F
        V_ = self.av32
        sets = [(F[0], F[1], F[2], F[3], F[4], F[5], self.a[8]),
                (V_[5], V_[6], V_[7], V_[8], V_[9], V_[10], self.a[9])]
        gens = [self.mixer1_chunk(pi, n, sets[n % 2]) for n in range(NCH)]
        active = [gens[0]]
        nxt = 1
        want = False
        while active:
            for g in list(active):
                try:
                    tok = next(g)
                except StopIteration:
                    active.remove(g)
                    continue
                if tok == "HALF":
                    want = True
            if want and nxt < NCH and len(active) < 2:
                active.append(gens[nxt])
                nxt += 1
                want = False
            if not active and nxt < NCH:
                active.append(gens[nxt])
                nxt += 1
        self.out_proj(l, pi, "c_out")

    def out_proj(self, l, pi, kind):
        cbs = self.cbs(pi)
        for i in range(NCH):
            w = self.get((kind, i))
            for (c0, c1) in cbs:
                ps = self.next_ps()
                for k in range(NCH):
                    self.mm(ps, ps.ap[:, 0:c1 - c0], w, w.ap[:, k * 128:(k + 1) * 128], self.a[k], self.a[k].ap[:, c0:c1], start=(k == 0), stop=(k == 7))
                self.resid(l, 1, pi, i, ps, c0, c1)

    def proj_fm(self, w, pi, evac):
        for (c0, c1) in self.cbs(pi):
            ps = self.next_ps()
            for k in range(NCH):
                self.mm(ps, ps.ap[:, 0:c1 - c0], w, w.ap[:, k * 128:(k + 1) * 128], self.h[k].parts[0 if c0 < 512 else (1 if c0 < 1024 else 2)], self.h[k].ap[:, c0:c1], start=(k == 0), stop=(k == 7))
            evac(ps, c0, c1)

    def proj_tm(self, w, pi, evac, evac_s):
        for g in range(2):
            ps = self.next_ps()
            for t4 in range(4):
                tt_ = g * 4 + t4
                for k in range(NCH):
                    self.mm(ps, ps.ap[:, t4 * 128:(t4 + 1) * 128], self.h[k].parts[tt_ // 4], self.h[k].ap[:, tt_ * 128:(tt_ + 1) * 128], w, w.ap[:, k * 128:(k + 1) * 128],
                            start=(k == 0), stop=(k == 7))
            evac(ps, g)
        if pi == 1:
            ps = self.next_ps()
            for sq in range(NSQ):
                cs = TP + LS * sq
                for k in range(NCH):
                    self.mm(ps, ps.ap[0:LS, sq * 128:(sq + 1) * 128], self.h[k].parts[2], self.h[k].ap[:, cs:cs + LS], w, w.ap[:, k * 128:(k + 1) * 128],
                            start=(k == 0), stop=(k == 7))
            evac_s(ps)

    def mixer0(self, pi):
        l, m = 0, 1
        P = self.pool
        self.norm_mod(l, m, pi)
        w = self.get(("ab_gate",))
        self.V(P, "tensor_copy", self.gw, [w], out=self.gw.ap[:, 0:64], in_=w.ap[:, 0:64])
        for _ in self.mlstm_P(pi, 0):
            pass
        for hd in range(4):
            if pi == 0:
                self.pump(3)
            cg = self.mlstm_C(pi, hd)
            pg = self.mlstm_P(pi, hd + 1) if hd < 3 else None
            started = False
            for tok in cg:
                if tok == "P_OK":
                    started = True
                if started and pg is not None:
                    try:
                        next(pg)
                    except StopIteration:
                        pg = None
            if pg is not None:
                for _ in pg:
                    pass
        for _ in self.attn_pro(pi, 0):
            pass
        for hp in range(4):
            if pi == 0:
                self.pump(3)
            gens = [self.attn_loop(pi, hp)]
            if hp < 3:
                gens.append(self.attn_pro(pi, hp + 1))
            self.interleave(*gens)
            self.attn_sample(pi, hp)
        if pi == 0:
            self.pump(100)
            self.ada_derive(0, (2,))
            self.ada_derive(1)
        self.out_proj(l, pi, "ab_out")

    def mlstm_chunk(self, c0, L, k_t, k_ap, v_t, v_ap, Mp_t, Mp_ap, C, Cb, nr, nrb, qT, kT, ig, M, emt, hT):
        D_, P = self.dve, self.pool
        cols = self.cols
        m0, m1, m2, _ = self.m128
        Pb, qs, kw = self.b128
        acol = cols.ap[0:L, 1:2]
        self.V(D_, "scalar_tensor_tensor", [m0, cols], [ig, self.ident], out=m0.ap[0:L, 0:L], in0=ig.ap[0:L, c0:c0 + L], scalar=1.0,
               in1=self.ident.ap[0:L, 0:L], op0=ALU.mult, op1=ALU.mult, accum_out=acol)
        self.V(D_, "tensor_scalar", m0, [M, cols], out=m0.ap[0:L, 0:L], in0=M.ap[0:L, c0:c0 + L], scalar1=-1.0, scalar2=acol, op0=ALU.mult, op1=ALU.add)
        self.V(P, "affine_select", m0, [m0], out=m0.ap[0:L, 0:L], in_=m0.ap[0:L, 0:L], pattern=[[1, L]], compare_op=ALU.is_ge, fill=self.reg_neg,
               base=0, channel_multiplier=-1)
        self.A(m0, m0.ap[0:L, 0:L], m0, m0.ap[0:L, 0:L], AF.Exp)
        psS = self.next_ps()
        self.mm(psS, psS.ap[0:L, 0:L], kT, kT.ap[:, c0:c0 + L], qT, qT.ap[:, c0:c0 + L], start=True, stop=True)
        self.V(D_, "tensor_tensor", Pb, [psS, m0], out=Pb.ap[0:L, 0:L], in0=psS.ap[0:L, 0:L], in1=m0.ap[0:L, 0:L], op=ALU.mult)
        self.A(m1, m1.ap[:, 0:L], M, M.ap[:, c0:c0 + L], AF.Exp, bias=Mp_ap, scale=-1.0, extra=[Mp_t])
        self.V(D_, "tensor_tensor", qs, [qT, m1], out=qs.ap[:, 0:L], in0=qT.ap[:, c0:c0 + L], in1=m1.ap[:, 0:L], op=ALU.mult)
        psN = self.next_ps()
        self.mm(psN, psN.ap[:, 0:L], v_t, v_ap, Pb, Pb.ap[0:L, 0:L], start=True, stop=False, inc=True)
        self.mm(psN, psN.ap[:, 0:L], Cb, Cb.ap[:, :], qs, qs.ap[:, 0:L], start=False, stop=True)
        psD = self.next_ps()
        self.mm(psD, psD.ap[:, 0:L], self.ones_bf, self.ones_bf.ap[0:L, :], Pb, Pb.ap[0:L, 0:L], start=True, stop=False, inc=True)
        self.mm(psD, psD.ap[:, 0:L], nrb, nrb.ap[:, :], qs, qs.ap[:, 0:L], start=False, stop=True)
        self.A(m2, m2.ap[:, 0:L], psD, psD.ap[:, 0:L], AF.Abs)
        self.V(D_, "tensor_tensor", m2, [m2, emt], out=m2.ap[:, 0:L], in0=m2.ap[:, 0:L], in1=emt.ap[:, c0:c0 + L], op=ALU.max)
        self.V(D_, "reciprocal", m2, [m2], out=m2.ap[:, 0:L], in_=m2.ap[:, 0:L])
        self.V(D_, "tensor_tensor", hT, [psN, m2], out=hT.ap[:, c0:c0 + L], in0=psN.ap[:, 0:L], in1=m2.ap[:, 0:L], op=ALU.mult)
        self.V(D_, "tensor_tensor", cols, [M, cols], out=cols.ap[0:L, 2:3], in0=M.ap[0:L, c0 + L - 1:c0 + L], in1=acol, op=ALU.subtract)
        self.A(cols, cols.ap[0:L, 3:4], cols, cols.ap[0:L, 2:3], AF.Exp, scale=-1.0)
        self.A(cols, cols.ap[:, 4:5], M, M.ap[:, c0 + L - 1:c0 + L], AF.Exp, bias=Mp_ap, scale=-1.0, extra=[Mp_t])
        self.V(D_, "tensor_scalar", kw, [k_t, cols], out=kw.ap[0:L, :], in0=k_ap, scalar1=cols.ap[0:L, 3:4], scalar2=None, op0=ALU.mult)
        psC = self.next_ps()
        self.mm(psC, psC.ap[:, 0:128], kw, kw.ap[0:L, :], v_t, v_ap, start=True, stop=True)
        self.V(D_, "scalar_tensor_tensor", C, [C, cols, psC], out=C.ap[:, :], in0=C.ap[:, :], scalar=cols.ap[:, 4:5], in1=psC.ap[:, 0:128], op0=ALU.mult, op1=ALU.add)
        self.A(Cb, Cb.ap[:, :], C, C.ap[:, :], AF.Copy)
        psNn = self.next_ps()
        self.mm(psNn, psNn.ap[:, 0:128], kw, kw.ap[0:L, :], self.ones_bf, self.ones_bf.ap[0:L, :], start=True, stop=True)
        self.V(D_, "scalar_tensor_tensor", nr, [nr, cols, psNn], out=nr.ap[:, :], in0=nr.ap[:, :], scalar=cols.ap[:, 4:5], in1=psNn.ap[:, 0:128], op0=ALU.mult, op1=ALU.add)
        self.A(nrb, nrb.ap[:, :], nr, nr.ap[:, :], AF.Copy)

    def mlstm_sample(self, hd, ig_t, G_t, M_t, emt_t, hT_t, qT_t, kT_t, tkb, tvb, ig, G, M, emt, hT, qT, kT):
        D_ = self.dve
        outs = self.outs
        cs4 = self.cs4
        c = cs4.ap
        acol4, tcol4, wcol4, dec4, mend4, nnew4 = c[0:LS, 0:4], c[0:LS, 4:8], c[0:LS, 8:12], c[:, 12:16], c[:, 16:20], c[:, 20:24]
        W16 = self.m128[0]
        iw = self.m128[1]
        dn = self.m128[2]
        P16, qs, _ = self.b128
        Cs4, Cs4b, nr4b, kw4 = self.Cs4, self.Cs4b, self.nr4b, self.kw4
        S0 = TP
        sl = slice(S0, S0 + TS)
        m0c = self.svc("am", 0, 16).rearrange("p (s h) -> p s h", h=4)[:, :, hd]
        n0c = self.svc("an", 0, 16).rearrange("p (s h) -> p s h", h=4)[:, :, hd]
        self.dma(Cs4.ap[:, :, :], self.d_aC.rearrange("(s h) p d -> h p s d", h=4)[hd], writes=[Cs4], queue=self.act)
        self.A(Cs4b, Cs4b.ap[:, :, :], Cs4, Cs4.ap[:, :, :], AF.Copy)
        self.V(D_, "tensor_copy", nr4b, [self.sv], out=nr4b.ap[:, :, :], in_=n0c.unsqueeze(2).to_broadcast([128, NSQ, 128]))
        for sq in range(NSQ):
            cs = S0 + LS * sq
            self.V(D_, "scalar_tensor_tensor", [W16, cs4], [ig_t, self.ident], out=W16.ap[0:LS, 0:LS], in0=ig[0:LS, cs:cs + LS], scalar=1.0,
                   in1=self.ident.ap[0:LS, 0:LS], op0=ALU.mult, op1=ALU.mult, accum_out=c[0:LS, sq:sq + 1])
        Mv16 = M[0:LS, sl].rearrange("p (s l) -> p s l", l=LS)
        Wv = W16.ap[0:LS, 0:TS].rearrange("p (s l) -> p s l", l=LS)
        self.V(D_, "tensor_tensor", W16, [cs4, M_t], out=Wv, in0=acol4.unsqueeze(2).to_broadcast([LS, NSQ, LS]), in1=Mv16, op=ALU.subtract)
        self.V(D_, "tensor_tensor", W16, [W16, self.maskb], out=Wv, in0=Wv, in1=self.maskb.ap[0:LS, 0:LS].unsqueeze(1).to_broadcast([LS, NSQ, LS]), op=ALU.add)
        self.A(W16, W16.ap[0:LS, 0:TS], W16, W16.ap[0:LS, 0:TS], AF.Exp)
        psS = self.next_ps()
        for sq in range(NSQ):
            cs = S0 + LS * sq
            self.mm(psS, psS.ap[0:LS, sq * LS:(sq + 1) * LS], kT_t, kT[:, cs:cs + LS], qT_t, qT[:, cs:cs + LS], start=True, stop=True)
        self.V(D_, "tensor_tensor", P16, [psS, W16], out=P16.ap[0:LS, 0:TS], in0=psS.ap[0:LS, 0:TS], in1=W16.ap[0:LS, 0:TS], op=ALU.mult)
        Mv = M[:, sl].rearrange("p (s l) -> p s l", l=LS)
        iwv = iw.ap[:, 0:TS].rearrange("p (s l) -> p s l", l=LS)
        self.V(D_, "tensor_tensor", iw, [self.sv, M_t], out=iwv, in0=m0c.unsqueeze(2).to_broadcast([128, NSQ, LS]), in1=Mv, op=ALU.subtract)
        self.A(iw, iw.ap[:, 0:TS], iw, iw.ap[:, 0:TS], AF.Exp)
        self.V(D_, "tensor_tensor", qs, [qT_t, iw], out=qs.ap[:, 0:TS], in0=qT[:, sl], in1=iw.ap[:, 0:TS], op=ALU.mult)
        psN, psD = self.next_ps(), self.next_ps()
        for sq in range(NSQ):
            o = slice(sq * LS, (sq + 1) * LS)
            self.mm(psN, psN.ap[:, o], tvb, tvb.ap[0:LS, sq * 128:(sq + 1) * 128], P16, P16.ap[0:LS, o], start=True, stop=False, inc=True)
            self.mm(psN, psN.ap[:, o], Cs4b, Cs4b.ap[:, sq, :], qs, qs.ap[:, o], start=False, stop=True)
        for sq in range(NSQ):
            o = slice(sq * LS, (sq + 1) * LS)
            self.mm(psD, psD.ap[:, o], self.ones_bf, self.ones_bf.ap[0:LS, :], P16, P16.ap[0:LS, o], start=True, stop=False, inc=True)
            self.mm(psD, psD.ap[:, o], nr4b, nr4b.ap[:, sq, :], qs, qs.ap[:, o], start=False, stop=True)
        self.A(dn, dn.ap[:, 0:TS], psD, psD.ap[:, 0:TS], AF.Abs)
        self.V(D_, "tensor_tensor", dn, [dn, emt_t], out=dn.ap[:, 0:TS], in0=dn.ap[:, 0:TS], in1=emt[:, sl], op=ALU.max)
        self.A(dn, dn.ap[:, 0:TS], dn, dn.ap[:, 0:TS], AF.Ln)
        self.A(dn, dn.ap[:, 0:TS], dn, dn.ap[:, 0:TS], AF.Exp, scale=-1.0)
        self.V(D_, "tensor_tensor", hT_t, [psN, dn], out=hT[:, sl], in0=psN.ap[:, 0:TS], in1=dn.ap[:, 0:TS], op=ALU.mult)
        self.V(D_, "tensor_copy", cs4, [M_t], out=mend4, in_=Mv[:, :, LS - 1])
        self.V(D_, "tensor_tensor", cs4, [cs4], out=tcol4, in0=c[0:LS, 16:20], in1=acol4, op=ALU.subtract)
        self.A(cs4, wcol4, cs4, tcol4, AF.Exp, scale=-1.0)
        self.V(D_, "tensor_tensor", cs4, [self.sv, cs4], out=c[:, 24:28], in0=m0c, in1=mend4, op=ALU.subtract)
        self.A(cs4, dec4, cs4, c[:, 24:28], AF.Exp)
        self.V(D_, "tensor_tensor", kw4, [tkb, cs4], out=kw4.ap[0:LS, :].rearrange("p (s d) -> p s d", d=128), in0=tkb.ap[0:LS, 0:NSQ * 128].rearrange("p (s d) -> p s d", d=128),
               in1=wcol4.unsqueeze(2).to_broadcast([LS, NSQ, 128]), op=ALU.mult)
        psC = self.next_ps()
        psn = self.next_ps()
        for sq in range(NSQ):
            self.mm(psC, psC.ap[:, sq * 128:(sq + 1) * 128], kw4, kw4.ap[0:LS, sq * 128:(sq + 1) * 128], tvb, tvb.ap[0:LS, sq * 128:(sq + 1) * 128], start=True, stop=True)
        for sq in range(NSQ):
            self.mm(psn, psn.ap[:, sq:sq + 1], kw4, kw4.ap[0:LS, sq * 128:(sq + 1) * 128], self.ones_bf, self.ones_bf.ap[0:LS, 0:1], start=True, stop=True)
        for sq in range(NSQ):
            self.V(D_, "scalar_tensor_tensor", Cs4, [Cs4, cs4, psC], out=Cs4.ap[:, sq, :], in0=Cs4.ap[:, sq, :], scalar=c[:, 12 + sq:13 + sq], in1=psC.ap[:, sq * 128:(sq + 1) * 128],
                   op0=ALU.mult, op1=ALU.add)
        self.dma(self.o_saC.rearrange("(s h) p d -> h p s d", h=4)[hd], Cs4.ap[:, :, :], reads=[Cs4], queue=self.act)
        ov = outs.ap[:, 0:16].rearrange("p (s h) -> p s h", h=4)[:, :, hd]
        self.V(D_, "tensor_tensor", cs4, [self.sv, cs4], out=nnew4, in0=n0c, in1=dec4, op=ALU.mult)
        self.V(D_, "tensor_tensor", outs, [cs4, psn], out=ov, in0=nnew4, in1=psn.ap[:, 0:NSQ], op=ALU.add)
        mv = outs.ap[:, 16:32].rearrange("p (s h) -> p s h", h=4)[:, :, hd]
        self.V(D_, "tensor_tensor", outs, [cs4, G_t], out=mv, in0=mend4, in1=G[:, sl].rearrange("p (s l) -> p s l", l=LS)[:, :, LS - 1], op=ALU.subtract)

    def mlstm_bufs(self, hd):
        a = self.a
        if hd % 2 == 0:
            return a[8], a[9], a[10], a[11], self.tokb[0], self.tokb[1]
        return a[18], a[19], a[20], a[21], self.tokb[2], self.tokb[3]

    def mlstm_P(self, pi, hd):
        qT, kT, ktok, vtok, tkb, tvb = self.mlstm_bufs(hd)
        KS = 128 ** -0.5

        def pidx(c0):
            return 0 if c0 < 512 else (1 if c0 < 1024 else 2)
        w = self.get(("ab_in", "qa", hd))
        yield from self.proj_fm_g(w, pi, lambda ps, c0, c1: self.A(qT.parts[pidx(c0)], qT.ap[:, c0:c1], ps, ps.ap[:, 0:c1 - c0], AF.Copy))
        w = self.get(("ab_in", "ka", hd))
        yield from self.proj_fm_g(w, pi, lambda ps, c0, c1: self.A(kT.parts[pidx(c0)], kT.ap[:, c0:c1], ps, ps.ap[:, 0:c1 - c0], AF.Copy, scale=KS))
        yield from self.proj_tm_g(w, pi, lambda ps, g: self.A(ktok.parts[g], ktok.ap[:, g * 512:(g + 1) * 512], ps, ps.ap[:, 0:512], AF.Copy, scale=KS),
                                  lambda ps: self.A(tkb, tkb.ap[0:LS, 0:512], ps, ps.ap[0:LS, 0:512], AF.Copy, scale=KS))
        w = self.get(("ab_in", "va", hd))
        yield from self.proj_tm_g(w, pi, lambda ps, g: self.A(vtok.parts[g], vtok.ap[:, g * 512:(g + 1) * 512], ps, ps.ap[:, 0:512], AF.Copy),
                                  lambda ps: self.A(tvb, tvb.ap[0:LS, 0:512], ps, ps.ap[0:LS, 0:512], AF.Copy))

    def mlstm_C(self, pi, hd):
        D_, P = self.dve, self.pool
        yield
        T = TP + (TS if pi == 1 else 0)
        yield
        cbs = self.cbs(pi)
        yield
        H2 = ((0, 512), (512, 1024))
        yield
        F = self.F
        yield
        small = self.small
        yield
        ig, G, M, emt, tho, hT, W_ = F[0], F[1], F[2], F[3], F[4], F[5], F[6]
        yield
        qT, kT, ktok, vtok, tkb, tvb = self.mlstm_bufs(hd)
        yield
        sqh = self.a[12]
        yield
        P_all, qs_all, kw_all, Cb_all, nrb_all = self.a[13], self.a[14], self.a[15], self.a[16], self.a[17]
        yield
        KS = 128 ** -0.5
        yield
        ow = self.ones_w
        yield
        c8 = self.c8
        yield
        c8a = c8.ap
        yield
        m0 = self.m128[0]
        yield
        Cst, Cbst, nrst, nrbst = self.C[hd], self.Cb[hd], self.nr[hd], self.nrb[hd]
        yield

        def pidx(c0):
            return 0 if c0 < 512 else (1 if c0 < 1024 else 2)
        for gi, gcol in enumerate((hd, 4 + hd)):
            wr_t = self.wrep if gi == 0 else sqh
            wr_ap = wr_t.ap[:, 0:1024]
            self.V(D_, "tensor_copy", wr_t, [self.gw], out=wr_ap.rearrange("p (k c) -> p k c", c=128),
                   in_=self.gw.ap[:, 0:64].rearrange("p (k g) -> p k g", g=8)[:, :, gcol].unsqueeze(2).to_broadcast([128, 8, 128]))
            yield

            def ev(ps, c0, c1, gi=gi):
                g = pidx(c0)
                if gi == 0:
                    self.A(ig.parts[g], ig.ap[:, c0:c1], ps, ps.ap[:, 0:c1 - c0], AF.Identity, bias=self.pvc("gbias", hd), extra=[self.pv])
                else:
                    self.A(G.parts[g], G.ap[:, c0:c1], ps, ps.ap[:, 0:c1 - c0], AF.Exp, bias=small.ap[:, 34 + hd:35 + hd], scale=-1.0, extra=[small])
            self.proj_fm(wr_t, pi, ev)
            yield
        for (c0, c1) in cbs:
            g = pidx(c0)
            yield
            self.A(G.parts[g], G.ap[:, c0:c1], G.parts[g], G.ap[:, c0:c1], AF.Ln, bias=self.cst.ap[:, 1:2], extra=[self.cst])
            yield
        w = self.get(("ab_in", "oa", hd))
        yield
        self.proj_fm(w, pi, lambda ps, c0, c1: self.A(tho.parts[pidx(c0)], tho.ap[:, c0:c1], ps, ps.ap[:, 0:c1 - c0], AF.Tanh, scale=0.5))
        yield
        yield "P_OK"
        self.V(D_, "tensor_copy", self.cols, [self.Ml], out=self.cols.ap[:, 0:1], in_=self.Ml.ap[:, hd:hd + 1])
        yield
        for g, (c0, c1) in enumerate(H2):
            gi_t, gi_ap = (self.Gl, self.Gl.ap[:, hd:hd + 1]) if g == 0 else (G.parts[0], G.ap[:, 511:512])
            yield
            mi_t, mi_ap = (self.Ml, self.Ml.ap[:, hd:hd + 1]) if g == 0 else (M.parts[0], M.ap[:, 511:512])
            yield
            self.V(D_, "tensor_tensor_scan", G.parts[g], [ow, G.parts[g], gi_t], out=G.ap[:, c0:c1], data0=ow.ap[:, 0:512], data1=G.ap[:, c0:c1],
                   initial=gi_ap, op0=ALU.mult, op1=ALU.add)
            yield
            self.V(D_, "tensor_tensor", ig.parts[g], [ig.parts[g], G.parts[g]], out=ig.ap[:, c0:c1], in0=ig.ap[:, c0:c1], in1=G.ap[:, c0:c1], op=ALU.add)
            yield
            self.V(D_, "tensor_tensor_scan", M.parts[g], [ow, ig.parts[g], mi_t], out=M.ap[:, c0:c1], data0=ow.ap[:, 0:512], data1=ig.ap[:, c0:c1],
                   initial=mi_ap, op0=ALU.mult, op1=ALU.max)
            yield
            self.V(D_, "tensor_tensor", emt.parts[g], [G.parts[g], M.parts[g]], out=emt.ap[:, c0:c1], in0=G.ap[:, c0:c1], in1=M.ap[:, c0:c1], op=ALU.subtract)
            yield
            self.A(emt.parts[g], emt.ap[:, c0:c1], emt.parts[g], emt.ap[:, c0:c1], AF.Exp)
            yield
        if pi == 1:
            g = 2
            yield
            for sq in range(NSQ):
                cs = TP + LS * sq
                yield
                self.V(D_, "tensor_tensor_scan", G.parts[g], [ow, G.parts[g]], out=G.ap[:, cs:cs + LS], data0=ow.ap[:, 0:LS], data1=G.ap[:, cs:cs + LS],
                       initial=0.0, op0=ALU.mult, op1=ALU.add)
                yield
            self.V(D_, "tensor_tensor", ig.parts[g], [ig.parts[g], G.parts[g]], out=ig.ap[:, TP:T], in0=ig.ap[:, TP:T], in1=G.ap[:, TP:T], op=ALU.add)
            yield
            for sq in range(NSQ):
                cs = TP + LS * sq
                yield
                self.V(D_, "tensor_tensor_scan", M.parts[g], [ow, ig.parts[g], self.sv], out=M.ap[:, cs:cs + LS], data0=ow.ap[:, 0:LS], data1=ig.ap[:, cs:cs + LS],
                       initial=self.svc("am", sq * 4 + hd), op0=ALU.mult, op1=ALU.max)
                yield
            self.V(D_, "tensor_tensor", emt.parts[g], [G.parts[g], M.parts[g]], out=emt.ap[:, TP:T], in0=G.ap[:, TP:T], in1=M.ap[:, TP:T], op=ALU.subtract)
            yield
            self.A(emt.parts[g], emt.ap[:, TP:T], emt.parts[g], emt.ap[:, TP:T], AF.Exp)
            yield
        self.V(D_, "tensor_copy", self.Gl, [G.parts[1]], out=self.Gl.ap[:, hd:hd + 1], in_=G.ap[:, TP - 1:TP])
        yield
        Mv = M.ap[:, 0:TP].rearrange("p (t l) -> p t l", l=128)
        yield
        Wv = W_.ap[:, 0:TP].rearrange("p (t l) -> p t l", l=128)
        yield
        for g in range(2):
            cg = c8.parts[g]
            yield
            s4 = slice(4 * g, 4 * g + 4)
            yield
            for tt_ in range(4 * g, 4 * g + 4):
                self.V(D_, "scalar_tensor_tensor", [m0, cg], [ig.parts[g], self.ident], out=m0.ap[:, :], in0=ig.ap[:, tt_ * 128:(tt_ + 1) * 128], scalar=1.0,
                       in1=self.ident.ap[:, :], op0=ALU.mult, op1=ALU.mult, accum_out=c8a[:, tt_:tt_ + 1])
                yield
            if g == 0:
                self.V(D_, "tensor_copy", cg, [self.cols], out=c8a[:, 8:9], in_=self.cols.ap[:, 0:1])
                yield
                self.V(D_, "tensor_copy", cg, [M.parts[0]], out=c8a[:, 9:12], in_=Mv[:, 0:3, 127])
                yield
            else:
                self.V(D_, "tensor_copy", cg, [M.parts[0], M.parts[1]], out=c8a[:, 12:16], in_=Mv[:, 3:7, 127])
                yield
            self.V(D_, "tensor_copy", cg, [M.parts[g]], out=c8a[:, 16 + 4 * g:20 + 4 * g], in_=Mv[:, s4, 127])
            yield
            self.V(D_, "tensor_tensor", cg, [cg], out=c8a[:, 48 + 4 * g:52 + 4 * g], in0=c8a[:, 16 + 4 * g:20 + 4 * g], in1=c8a[:, 4 * g:4 * g + 4], op=ALU.subtract)
            yield
            self.A(cg, c8a[:, 24 + 4 * g:28 + 4 * g], cg, c8a[:, 48 + 4 * g:52 + 4 * g], AF.Exp, scale=-1.0)
            yield
            self.V(D_, "tensor_tensor", cg, [cg], out=c8a[:, 48 + 4 * g:52 + 4 * g], in0=c8a[:, 8 + 4 * g:12 + 4 * g], in1=c8a[:, 16 + 4 * g:20 + 4 * g], op=ALU.subtract)
            yield
            self.A(cg, c8a[:, 32 + 4 * g:36 + 4 * g], cg, c8a[:, 48 + 4 * g:52 + 4 * g], AF.Exp)
            yield
            self.V(D_, "tensor_tensor", W_.parts[g], [cg, M.parts[g]], out=Wv[:, s4, :], in0=c8a[:, s4].unsqueeze(2).to_broadcast([128, 4, 128]), in1=Mv[:, s4, :], op=ALU.subtract)
            yield
            self.V(D_, "tensor_tensor", W_.parts[g], [W_.parts[g], self.maskb], out=Wv[:, s4, :], in0=Wv[:, s4, :],
                   in1=self.maskb.ap[:, :].unsqueeze(1).to_broadcast([128, 4, 128]), op=ALU.add)
            yield
            self.A(W_.parts[g], W_.ap[:, g * 512:(g + 1) * 512], W_.parts[g], W_.ap[:, g * 512:(g + 1) * 512], AF.Exp)
            yield
        Wi, Wd = self.av32[2], self.av32[3]
        Wip = [[self.a[4].parts[0], self.a[4].parts[1]], [self.a[4].parts[2], self.a[5].parts[0], self.a[5].parts[1]]]
        Wiv = Wi.ap[:, 0:TP].rearrange("p (t l) -> p t l", l=128)
        Wdv = Wd.ap[:, 0:TP].rearrange("p (d t) -> p d t", t=8)
        self.V(D_, "tensor_copy", Wd, [c8.parts[0], c8.parts[1]], out=Wdv, in_=c8a[:, 32:40].unsqueeze(1).to_broadcast([128, 128, 8]))
        yield
        self.V(D_, "memset", Wd, [], ap=Wdv[:, :, 0], constant=0.0)
        yield
        CN = G
        yield
        CNtd = CN.ap[:, 0:TP].rearrange("p (d t) -> p t d", t=8)
        yield
        CNp = [CN.parts[0], CN.parts[1]]
        yield
        for g, (c0h, c1h) in enumerate(H2):
            cg = c8.parts[g]
            yield
            s4 = slice(4 * g, 4 * g + 4)
            yield
            hs = slice(c0h, c1h)
            yield
            self.V(D_, "tensor_tensor", Wip[g], [cg, M.parts[g]], out=Wiv[:, s4, :], in0=c8a[:, 8 + 4 * g:12 + 4 * g].unsqueeze(2).to_broadcast([128, 4, 128]),
                   in1=Mv[:, s4, :], op=ALU.subtract)
            yield
            self.A(Wip[g], Wi.ap[:, hs], Wip[g], Wi.ap[:, hs], AF.Exp)
            yield
            ps = self.next_ps()
            yield
            for t4 in range(4):
                c0 = (g * 4 + t4) * 128
                yield
                self.mm(ps, ps.ap[:, t4 * 128:(t4 + 1) * 128], kT.parts[g], kT.ap[:, c0:c0 + 128], qT.parts[g], qT.ap[:, c0:c0 + 128], start=True, stop=True)
                yield
            self.V(D_, "tensor_tensor", P_all.parts[g], [ps, W_.parts[g]], out=P_all.ap[:, hs], in0=ps.ap[:, 0:512], in1=W_.ap[:, hs], op=ALU.mult)
            yield
            self.V(D_, "tensor_tensor", qs_all.parts[g], [qT.parts[g], Wip[g]], out=qs_all.ap[:, hs], in0=qT.ap[:, hs], in1=Wi.ap[:, hs], op=ALU.mult)
            yield
            self.V(D_, "tensor_tensor", kw_all.parts[g], [ktok.parts[g], cg], out=kw_all.ap[:, hs].rearrange("p (t d) -> p t d", d=128),
                   in0=ktok.ap[:, hs].rearrange("p (t d) -> p t d", d=128), in1=c8a[:, 24 + 4 * g:28 + 4 * g].unsqueeze(2).to_broadcast([128, 4, 128]), op=ALU.mult)
            yield
            ps = self.next_ps()
            yield
            for t4 in range(4):
                c0 = (g * 4 + t4) * 128
                yield
                self.mm(ps, ps.ap[:, t4 * 128:(t4 + 1) * 128], kw_all.parts[g], kw_all.ap[:, c0:c0 + 128], vtok.parts[g], vtok.ap[:, c0:c0 + 128], start=True, stop=True)
                yield
            self.op(self.act, lambda ps=ps, g=g: self.nc.scalar.activation(out=CNtd[:, g * 4:(g + 1) * 4, :], in_=ps.ap[:, 0:512].rearrange("p (t d) -> p t d", d=128), func=AF.Copy),
                    reads=[ps], writes=CNp)
            yield
            psn = self.next_ps()
            yield
            for t4 in range(4):
                c0 = (g * 4 + t4) * 128
                yield
                self.mm(psn, psn.ap[:, t4:t4 + 1], kw_all.parts[g], kw_all.ap[:, c0:c0 + 128], self.ones_bf, self.ones_bf.ap[:, 0:1], start=True, stop=True)
                yield
            self.V(D_, "tensor_copy", cg, [psn], out=c8a[:, 40 + 4 * g:44 + 4 * g], in_=psn.ap[:, 0:4])
            yield
        c80, c81 = c8.parts[0], c8.parts[1]
        yield
        self.V(D_, "scalar_tensor_tensor", CNp, [Cst, c80] + CNp, out=CNtd[:, 0, :], in0=Cst.ap[:, :], scalar=c8a[:, 32:33], in1=CNtd[:, 0, :], op0=ALU.mult, op1=ALU.add)
        yield
        self.V(D_, "scalar_tensor_tensor", c80, [nrst, c80], out=c8a[:, 40:41], in0=nrst.ap[:, 0:1], scalar=c8a[:, 32:33], in1=c8a[:, 40:41], op0=ALU.mult, op1=ALU.add)
        yield
        self.V(D_, "memset", c80, [], ap=c8a[:, 32:33], constant=0.0)
        yield
        Wp = [W_.parts[0], W_.parts[1]]
        yield
        self.V(D_, "tensor_tensor_scan", CNp, CNp + [Wd], out=CN.ap[:, 0:TP], data0=Wd.ap[:, 0:TP], data1=CN.ap[:, 0:TP], initial=0.0, op0=ALU.mult, op1=ALU.add)
        yield
        self.V(D_, "tensor_tensor_scan", [c80, c81], [c80, c81], out=c8a[:, 40:48], data0=c8a[:, 32:40], data1=c8a[:, 40:48], initial=0.0, op0=ALU.mult, op1=ALU.add)
        yield
        Cbp = [Cb_all.parts[0], Cb_all.parts[1]]
        yield
        self.op(self.act, lambda: self.nc.scalar.activation(out=Cb_all.ap[:, 0:TP].rearrange("p (t d) -> p t d", d=128), in_=CNtd, func=AF.Copy), reads=CNp, writes=Cbp)
        yield
        nbp = [nrb_all.parts[0], nrb_all.parts[1]]
        yield
        self.V(D_, "tensor_copy", nbp, [c80, c81], out=nrb_all.ap[:, 0:TP].rearrange("p (t d) -> p t d", d=128), in_=c8a[:, 40:48].unsqueeze(2).to_broadcast([128, 8, 128]))
        yield
        for g, (c0h, c1h) in enumerate(H2):
            hs = slice(c0h, c1h)
            yield
            psN, psD = self.next_ps(), self.next_ps()
            yield
            for (psX, l_first_t, l_first, l_all) in ((psN, None, None, Cb_all), (psD, self.ones_bf, self.ones_bf.ap[:, :], nrb_all)):
                for t4 in range(4):
                    tt_ = g * 4 + t4
                    yield
                    c0 = tt_ * 128
                    yield
                    o = psX.ap[:, t4 * 128:(t4 + 1) * 128]
                    yield
                    if psX is psN:
                        self.mm(psX, o, vtok.parts[g], vtok.ap[:, c0:c0 + 128], P_all.parts[g], P_all.ap[:, c0:c0 + 128], start=True, stop=False, inc=True)
                        yield
                        st_t, st_ap = (Cbst, Cbst.ap[:, :]) if tt_ == 0 else (Cbp[(tt_ - 1) // 4], Cb_all.ap[:, c0 - 128:c0])
                        yield
                    else:
                        self.mm(psX, o, self.ones_bf, self.ones_bf.ap[:, :], P_all.parts[g], P_all.ap[:, c0:c0 + 128], start=True, stop=False, inc=True)
                        yield
                        st_t, st_ap = (nrbst, nrbst.ap[:, :]) if tt_ == 0 else (nbp[(tt_ - 1) // 4], nrb_all.ap[:, c0 - 128:c0])
                        yield
                    self.mm(psX, o, st_t, st_ap, qs_all.parts[g], qs_all.ap[:, c0:c0 + 128], start=False, stop=True)
                    yield
            Wg = W_.parts[g]
            yield
            self.A(Wg, W_.ap[:, hs], psD, psD.ap[:, 0:512], AF.Abs)
            yield
            self.V(D_, "tensor_tensor", Wg, [Wg, emt.parts[g]], out=W_.ap[:, hs], in0=W_.ap[:, hs], in1=emt.ap[:, hs], op=ALU.max)
            yield
            self.A(Wg, W_.ap[:, hs], Wg, W_.ap[:, hs], AF.Ln)
            yield
            self.A(Wg, W_.ap[:, hs], Wg, W_.ap[:, hs], AF.Exp, scale=-1.0)
            yield
            self.V(D_, "tensor_tensor", hT.parts[g], [psN, Wg], out=hT.ap[:, hs], in0=psN.ap[:, 0:512], in1=W_.ap[:, hs], op=ALU.mult)
            yield
        self.V(D_, "tensor_copy", Cst, CNp, out=Cst.ap[:, :], in_=CNtd[:, 7, :])
        yield
        self.op(self.act, lambda: self.nc.scalar.activation(out=Cbst.ap[:, :], in_=CNtd[:, 7, :], func=AF.Copy), reads=CNp, writes=[Cbst])
        yield
        self.V(D_, "tensor_copy", nrst, [c81], out=nrst.ap[:, :], in_=c8a[:, 47:48].to_broadcast([128, 128]))
        yield
        self.V(D_, "tensor_copy", nrbst, [c81], out=nrbst.ap[:, :], in_=c8a[:, 47:48].to_broadcast([128, 128]))
        yield
        self.V(D_, "tensor_copy", self.Ml, [M.parts[1]], out=self.Ml.ap[:, hd:hd + 1], in_=M.ap[:, TP - 1:TP])
        yield
        if pi == 1:
            outs = self.outs
            yield
            self.V(D_, "tensor_copy", outs, [self.nr[hd]], out=outs.ap[:, 32 + hd:33 + hd], in_=self.nr[hd].ap[:, 0:1])
            yield
            self.V(D_, "tensor_tensor", outs, [self.Ml, self.Gl], out=outs.ap[:, 36 + hd:37 + hd], in0=self.Ml.ap[:, hd:hd + 1], in1=self.Gl.ap[:, hd:hd + 1], op=ALU.subtract)
            yield
            self.dma(self.o_paC[hd], self.C[hd].ap[:, :], reads=[self.C[hd]], queue=self.act)
            yield
            self.mlstm_sample(hd, ig.parts[2], G.parts[2], M.parts[2], emt.parts[2], hT.parts[2], qT.parts[2], kT.parts[2], tkb, tvb,
                              ig.ap, G.ap, M.ap, emt.ap, hT.ap, qT.ap, kT.ap)
            yield
        lnt = ig
        yield
        for (c0, c1) in cbs:
            g = pidx(c0)
            yield
            cs_ = slice(c0, c1)
            yield
            self.A(sqh.parts[g], sqh.ap[:, cs_], hT.parts[g], hT.ap[:, cs_], AF.Square)
            yield
            ps = self.next_ps()
            yield
            self.mm(ps, ps.ap[:, 0:c1 - c0], self.ones_bf, self.ones_bf.ap[:, :], sqh.parts[g], sqh.ap[:, cs_], start=True, stop=True)
            yield
            self.A(lnt.parts[g], lnt.ap[:, cs_], ps, ps.ap[:, 0:c1 - c0], AF.Ln, bias=self.cst.ap[:, 0:1], scale=1.0 / 128, extra=[self.cst])
            yield
            self.A(lnt.parts[g], lnt.ap[:, cs_], lnt.parts[g], lnt.ap[:, cs_], AF.Exp, scale=-0.5)
            yield
            self.V(D_, "tensor_tensor", hT.parts[g], [hT.parts[g], lnt.parts[g]], out=hT.ap[:, cs_], in0=hT.ap[:, cs_], in1=lnt.ap[:, cs_], op=ALU.mult)
            yield
            self.V(D_, "scalar_tensor_tensor", hT.parts[g], [tho.parts[g], hT.parts[g]], out=hT.ap[:, cs_], in0=tho.ap[:, cs_], scalar=1.0, in1=hT.ap[:, cs_], op0=ALU.add, op1=ALU.mult)
            yield
            self.V(D_, "tensor_scalar", self.a[hd].parts[g], [hT.parts[g], small], out=self.a[hd].ap[:, cs_], in0=hT.ap[:, cs_], scalar1=small.ap[:, 24 + hd:25 + hd], scalar2=None, op0=ALU.mult)
            yield

    def proj_fm_g(self, w, pi, evac):
        for (c0, c1) in self.cbs(pi):
            ps = self.next_ps()
            for k in range(NCH):
                self.mm(ps, ps.ap[:, 0:c1 - c0], w, w.ap[:, k * 128:(k + 1) * 128], self.h[k].parts[0 if c0 < 512 else (1 if c0 < 1024 else 2)], self.h[k].ap[:, c0:c1], start=(k == 0), stop=(k == 7))
            evac(ps, c0, c1)
            yield

    def proj_tm_g(self, w, pi, evac, evac_s):
        for g in range(2):
            ps = self.next_ps()
            for t4 in range(4):
                tt_ = g * 4 + t4
                for k in range(NCH):
                    self.mm(ps, ps.ap[:, t4 * 128:(t4 + 1) * 128], self.h[k].parts[tt_ // 4], self.h[k].ap[:, tt_ * 128:(tt_ + 1) * 128], w, w.ap[:, k * 128:(k + 1) * 128],
                            start=(k == 0), stop=(k == 7))
                if t4 == 1:
                    yield
            evac(ps, g)
            yield
        if pi == 1:
            ps = self.next_ps()
            for sq in range(NSQ):
                cs = TP + LS * sq
                for k in range(NCH):
                    self.mm(ps, ps.ap[0:LS, sq * 128:(sq + 1) * 128], self.h[k].parts[2], self.h[k].ap[:, cs:cs + LS], w, w.ap[:, k * 128:(k + 1) * 128],
                            start=(k == 0), stop=(k == 7))
            evac_s(ps)
            yield

    def attn_bufs(self, hp):
        a = self.a
        if hp % 2 == 0:
            return a[8], a[9], a[10], self.tokb[0]
        return a[18], a[19], a[20], self.tokb[1]

    def attn_pro(self, pi, hp):
        D_ = self.dve
        T = TP + (TS if pi == 1 else 0)
        cbs = self.cbs(pi)
        F = self.F
        small = self.small
        Fq, Fk, rs = F[0], F[1], F[2]
        sqb = self.a[12]
        qn, kn, vcur, tvb = self.attn_bufs(hp)

        def qk_norm(name, raw):
            w = self.get(("ab_in", name, hp))

            def ev(ps, c0, c1):
                self.A(raw, raw.ap[:, c0:c1], ps, ps.ap[:, 0:c1 - c0], AF.Copy)
                self.A(sqb, sqb.ap[:, c0:c1], ps, ps.ap[:, 0:c1 - c0], AF.Square)
            yield from self.proj_fm_g(w, pi, ev)
            for (c0, c1) in cbs:
                ps = self.next_ps()
                self.mm(ps, ps.ap[:, 0:c1 - c0], self.blk_bf, self.blk_bf.ap[:, :], sqb, sqb.ap[:, c0:c1], start=True, stop=True)
                self.A(rs, rs.ap[:, c0:c1], ps, ps.ap[:, 0:c1 - c0], AF.Ln, bias=self.cst.ap[:, 0:1], scale=1.0 / 64, extra=[self.cst])
            yield
            self.A(rs, rs.ap[:, 0:T], rs, rs.ap[:, 0:T], AF.Exp, scale=-0.5)
            yield
        yield from qk_norm("qb", Fq)
        self.V(D_, "scalar_tensor_tensor", qn, [Fq, small, rs], out=qn.ap[:, 0:T], in0=Fq.ap[:, 0:T], scalar=small.ap[:, 28:29], in1=rs.ap[:, 0:T], op0=ALU.mult, op1=ALU.mult)
        yield
        yield from qk_norm("kb", Fk)
        self.V(D_, "scalar_tensor_tensor", Fk, [Fk, self.pv, rs], out=Fk.ap[:, 0:T], in0=Fk.ap[:, 0:T], scalar=self.pvc("gk"), in1=rs.ap[:, 0:T], op0=ALU.mult, op1=ALU.mult)
        yield
        self.A(kn, kn.ap[:, 0:T], Fk, Fk.ap[:, 0:T], AF.Copy)
        if pi == 1:
            self.dma(self.o_pbk[:, hp, :], Fk.ap[:, 512:1024], reads=[Fk], queue=self.act)
            self.dma(self.o_sbk[:, hp, :], Fk.ap[:, TP:T], reads=[Fk], queue=self.act)
        yield
        w = self.get(("ab_in", "vb", hp))

        def ev_v(ps, g):
            self.A(vcur, vcur.ap[:, g * 512:(g + 1) * 512], ps, ps.ap[:, 0:512], AF.Copy)
            if pi == 1 and g == 1:
                vf = F[5]
                self.A(vf, vf.ap[:, 0:512], ps, ps.ap[:, 0:512], AF.Copy)
                self.dma(self.o_pbv[:, :, hp * 128:(hp + 1) * 128].rearrange("t p f -> p t f"), vf.ap[:, 0:512].rearrange("p (t f) -> p t f", f=128), reads=[vf], queue=self.act)

        def ev_vs(ps):
            self.A(tvb, tvb.ap[0:LS, 0:512], ps, ps.ap[0:LS, 0:512], AF.Copy)
            self.A(self.tokf, self.tokf.ap[0:LS, 0:512], ps, ps.ap[0:LS, 0:512], AF.Copy)
            self.dma(self.o_sbv[:, :, hp * 128:(hp + 1) * 128].rearrange("s t f -> t s f"), self.tokf.ap[0:LS, 0:512].rearrange("p (s f) -> p s f", f=128), reads=[self.tokf], queue=self.act)
        yield from self.proj_tm_g(w, pi, ev_v, ev_vs)

    def attn_loop(self, pi, hp):
        D_, P = self.dve, self.pool
        F = self.F
        qn, kn, vcur, tvb = self.attn_bufs(hp)
        tmpSs = (F[3], F[4], F[6])
        Pbfs = (self.a[11], self.a[13], self.a[14])
        bT = self.biasT
        mix = self.a[4 + hp]
        self.dma(bT.ap[:, :, :], self.d_bias[hp], writes=[bT], queue=self.act)
        self.V(P, "memset", bT, [], ap=bT.ap[64:128, :, 512:576], constant=NEG)
        self.V(P, "memset", bT, [], ap=bT.ap[0:64, :, 64:128], constant=NEG)
        iters = [(qt, hh) for qt in range(8) for hh in range(2)]

        def srcs(qt, o):
            aq = 8 * pi + qt
            ka = aq - 4 + o
            if ka >= 8 * pi:
                c = (ka - 8 * pi) * 128
                return kn, kn.ap[:, c:c + 128], vcur, vcur.ap[:, c:c + 128]
            c = (ka - 4) * 128
            return self.kband[hp], self.kband[hp].ap[:, c:c + 128], self.vband, self.vband.ap[:, ka - 4, hp * 128:(hp + 1) * 128]

        def stageA(i):
            qt, hh = iters[i]
            r0 = 64 * hh
            offs = [o for o in range(5) if 8 * pi + qt - 4 + o >= 0]
            tmpS, Pbf = tmpSs[i % 3], Pbfs[i % 3]
            t0, t1, pS = self.next_ps2()
            for o in offs:
                k_t, k_ap, _, _ = srcs(qt, o)
                self.mm(t0 if o < 4 else t1, pS[:, o * 128:(o + 1) * 128], k_t, k_ap[r0:r0 + 64, :], qn, qn.ap[r0:r0 + 64, qt * 128:(qt + 1) * 128], start=True, stop=True)
            n0, n1 = offs[0] * 128, 640
            self.V(D_, "tensor_tensor", tmpS, [t0, t1, bT], out=tmpS.ap[:, n0:n1], in0=pS[:, n0:n1], in1=bT.ap[:, hh, n0:n1], op=ALU.add)
            self.A(Pbf, Pbf.ap[:, n0:n1], tmpS, tmpS.ap[:, n0:n1], AF.Exp)
        pend = {}

        def stageB1(i):
            qt, hh = iters[i]
            r0 = 64 * hh
            offs = [o for o in range(5) if 8 * pi + qt - 4 + o >= 0]
            Pbf = Pbfs[i % 3]
            psB = self.next_ps()
            for idx, o in enumerate(offs):
                _, _, v_t, v_ap = srcs(qt, o)
                self.mm(psB, psB.ap[:, 0:128], v_t, v_ap, Pbf, Pbf.ap[:, o * 128:(o + 1) * 128], start=(idx == 0), stop=(idx == len(offs) - 1))
            for idx, o in enumerate(offs):
                self.mm(psB, psB.ap[:, 128:256], self.ones_bf, self.ones_bf.ap[:, :], Pbf, Pbf.ap[:, o * 128:(o + 1) * 128], start=(idx == 0), stop=(idx == len(offs) - 1))
            rc = self.m128[2 + hh]
            self.A(rc, rc.ap[r0:r0 + 64, :], psB, psB.ap[r0:r0 + 64, 128:256], AF.Ln)
            self.A(rc, rc.ap[r0:r0 + 64, :], rc, rc.ap[r0:r0 + 64, :], AF.Exp, scale=-1.0)
            pend[i] = psB

        def stageB2(i):
            qt, hh = iters[i]
            r0 = 64 * hh
            psB = pend.pop(i)
            rc = self.m128[2 + hh]
            self.V(D_, "tensor_tensor", mix, [psB, rc], out=mix.ap[r0:r0 + 64, qt * 128:(qt + 1) * 128], in0=psB.ap[r0:r0 + 64, 0:128], in1=rc.ap[r0:r0 + 64, :], op=ALU.mult)
        stageA(0)
        stageA(1)
        yield
        for i in range(len(iters)):
            if i + 2 < len(iters):
                stageA(i + 2)
            stageB1(i)
            if i >= 1:
                stageB2(i - 1)
            yield
        stageB2(len(iters) - 1)

    def attn_sample(self, pi, hp):
        D_, P = self.dve, self.pool
        F = self.F
        qn, kn, vcur, tvb = self.attn_bufs(hp)
        tmpSs = (F[3], F[4], F[6])
        Pbfs = (self.a[11], self.a[13], self.a[14])
        bT = self.biasT
        mix = self.a[4 + hp]
        if pi == 1:
            tS4 = (self.m128[0], self.m128[1], F[3], F[4])
            Pb4 = (self.b128[0], self.b128[1], self.b128[2], self.a[11])

            def sA(sq):
                cs = TP + LS * sq
                kc = self.get(("kc", sq, hp))
                for hh in range(2):
                    r0 = 64 * hh
                    tmpS, Pbf = tS4[2 * (sq % 2) + hh], Pb4[2 * (sq % 2) + hh]
                    psS = self.next_ps()
                    qa = qn.ap[r0:r0 + 64, cs:cs + LS]
                    for o in range(4):
                        self.mm(psS, psS.ap[:, o * LS:(o + 1) * LS], kc, kc.ap[r0:r0 + 64, o * 128:(o + 1) * 128], qn, qa, start=True, stop=True)
                    self.mm(psS, psS.ap[0:LS, 64:64 + LS], kn, kn.ap[r0:r0 + 64, cs:cs + LS], qn, qa, start=True, stop=True)
                    self.V(D_, "tensor_tensor", tmpS, [psS, bT], out=tmpS.ap[:, 0:64].rearrange("p (k i) -> p k i", i=LS),
                           in0=psS.ap[:, 0:64].rearrange("p (k i) -> p k i", i=LS),
                           in1=bT.ap[:, hh, 0:512].rearrange("p (k i) -> p k i", i=128)[:, :, 0:LS], op=ALU.add)
                    self.V(D_, "tensor_tensor", tmpS, [psS, bT], out=tmpS.ap[0:LS, 64:64 + LS], in0=psS.ap[0:LS, 64:64 + LS], in1=bT.ap[0:LS, hh, 512:512 + LS], op=ALU.add)
                    self.A(Pbf, Pbf.ap[:, 0:64], tmpS, tmpS.ap[:, 0:64], AF.Exp)
                    self.A(Pbf, Pbf.ap[0:LS, 64:64 + LS], tmpS, tmpS.ap[0:LS, 64:64 + LS], AF.Exp)

            def sB(sq):
                cs = TP + LS * sq
                vc = self.get(("vc", sq, hp))
                for hh in range(2):
                    r0 = 64 * hh
                    Pbf = Pb4[2 * (sq % 2) + hh]
                    psO = self.next_ps()
                    for o in range(4):
                        self.mm(psO, psO.ap[:, 0:LS], vc, vc.ap[:, o * 128:(o + 1) * 128], Pbf, Pbf.ap[:, o * LS:(o + 1) * LS], start=(o == 0), stop=False, inc=(o == 3))
                    self.mm(psO, psO.ap[:, 0:LS], tvb, tvb.ap[0:LS, sq * 128:(sq + 1) * 128], Pbf, Pbf.ap[0:LS, 64:64 + LS], start=False, stop=True)
                    for o in range(4):
                        self.mm(psO, psO.ap[:, 128:128 + LS], self.ones_bf, self.ones_bf.ap[:, :], Pbf, Pbf.ap[:, o * LS:(o + 1) * LS], start=(o == 0), stop=False, inc=(o == 3))
                    self.mm(psO, psO.ap[:, 128:128 + LS], self.ones_bf, self.ones_bf.ap[0:LS, :], Pbf, Pbf.ap[0:LS, 64:64 + LS], start=False, stop=True)
                    rc = self.m128[2 + hh]
                    self.A(rc, rc.ap[r0:r0 + 64, 0:LS], psO, psO.ap[r0:r0 + 64, 128:128 + LS], AF.Ln)
                    self.A(rc, rc.ap[r0:r0 + 64, 0:LS], rc, rc.ap[r0:r0 + 64, 0:LS], AF.Exp, scale=-1.0)
                    self.V(D_, "tensor_tensor", mix, [psO, rc], out=mix.ap[r0:r0 + 64, cs:cs + LS], in0=psO.ap[r0:r0 + 64, 0:LS], in1=rc.ap[r0:r0 + 64, 0:LS], op=ALU.mult)
            sA(0)
            for sq in range(NSQ):
                if sq + 1 < NSQ:
                    sA(sq + 1)
                sB(sq)
        if pi == 0:
            self.V(P, "tensor_copy", self.kband[hp], [kn], out=self.kband[hp].ap[:, :], in_=kn.ap[:, 512:1024])
            self.V(P, "tensor_copy", self.vband, [vcur], out=self.vband.ap[:, :, hp * 128:(hp + 1) * 128], in_=vcur.ap[:, 512:1024].rearrange("p (t f) -> p t f", f=128))

    def final_outputs(self):
        q = self.act
        o = self.outs
        self.dma(self.o_pan, o.ap[:, 32:36], reads=[o], queue=q)
        self.dma(self.o_pam, o.ap[0:1, 36:40], reads=[o], queue=q)
        self.dma(self.o_san, o.ap[:, 0:16], reads=[o], queue=q)
        self.dma(self.o_sam, o.ap[0:1, 16:32], reads=[o], queue=q)
        self.dma(self.o_pconv, self.convc.ap, reads=[self.convc], queue=q)
        self.dma(self.o_pch, self.hc.ap, reads=[self.hc], queue=q)
        self.dma(self.o_sconv, self.sconv.ap, reads=[self.sconv], queue=q)
        self.dma(self.o_sch, self.sch.ap, reads=[self.sch], queue=q)

    def build(self):
        self.build_body()
        return self.nc

    def build_body(self):
        stop = DEBUG.get("stop")
        self.setup()
        done = False
        for pi in range(2):
            self.load_x(pi)
            if stop == "setup":
                done = True
                self.store_x(pi)
                continue
            if pi == 0:
                self.ada_pending = [(0, c) for c in range(4, 18)] + [(1, c) for c in range(18)]
                for cbk in range(4):
                    self.ada_block(0, cbk)
                self.ada_derive(0, (0,), gate=False)
            if stop == "ada":
                done = True
                self.store_x(pi)
                continue
            for l in range(2):
                self.ffn(l, 1, pi)
                if stop == "l%df1" % l:
                    done = True
                    break
                if l == 0:
                    self.mixer0(pi)
                else:
                    self.mixer1(pi)
                if DEBUG.get("dump") == "mixed" and stop == "l%dmix" % l:
                    T_ = TP + (TS if pi == 1 else 0)
                    for c in range(NCH):
                        self.V(self.dve, "tensor_copy", self.x[c], [self.a[c]], out=self.x[c].ap[:, 0:T_], in_=self.a[c].ap[:, 0:T_])
                if stop == "l%dmix" % l:
                    done = True
                    break
                self.ffn(l, 2, pi)
                if stop == "l%df2" % l:
                    done = True
                    break
            self.store_x(pi)
            if done and DEBUG.get("one_pass"):
                break
        if not done:
            self.final_outputs()
        self.finish()


def _unit_w(key, W):
    kind = key[0]

    def colblk(M, c0, n=128):
        return np.ascontiguousarray(M[:, c0:c0 + n].reshape(8, 128, n).transpose(1, 0, 2)).reshape(128, 8 * n)
    if kind == "ada":
        _, l, cbk, kq = key
        M = W["ada_w"][l][kq * 256:(kq + 1) * 256, cbk * 512:(cbk + 1) * 512]
        return np.ascontiguousarray(M.reshape(2, 128, 512).transpose(1, 0, 2)).reshape(128, 1024)
    if kind == "f_in":
        _, l, f, j, g = key
        Wi = W["ffn1_w_in"] if f == 1 else W["ffn2_w_in"]
        return colblk(Wi[l], g * DFF + j * 128)
    if kind == "f_out":
        _, l, f, i, hf = key
        j0, j1 = FO_SPLIT[hf]
        Wo = W["ffn1_w_out"] if f == 1 else W["ffn2_w_out"]
        M = Wo[l][j0 * 128:j1 * 128, i * 128:(i + 1) * 128]
        return np.ascontiguousarray(M.reshape(j1 - j0, 128, 128).transpose(1, 0, 2)).reshape(128, (j1 - j0) * 128)
    if kind == "ab_gate":
        return colblk(W["ab_w_in"][0], 2048, 8)
    if kind == "ab_in":
        _, nm, idx = key
        base = dict(qa=0, ka=512, va=1024, oa=1536, qb=2056, kb=2568, vb=3080)[nm]
        return colblk(W["ab_w_in"][0], base + idx * 128)
    if kind == "ab_out":
        return colblk(W["ab_w_out"][0], key[1] * 128)
    if kind == "c_in":
        _, which, n = key
        return colblk(W["c_w_in"][0], which * 1024 + n * 128)
    if kind == "c_gate":
        return np.ascontiguousarray(W["c_gate_w"][0][key[1]])
    if kind == "c_out":
        return colblk(W["c_w_out"][0], key[1] * 128)
    raise KeyError(key)


def _unit_c(key, ck, cv):
    kind, sq, hp = key
    if kind == "kc":
        return np.ascontiguousarray(ck[sq][:, 2 * hp:2 * hp + 2, :].reshape(512, 128).T)
    if kind == "vc":
        M = cv[sq][:, 2 * hp:2 * hp + 2, :].reshape(4, 128, 128)
        return np.ascontiguousarray(M.transpose(1, 0, 2)).reshape(128, 512)
    raise KeyError(key)


_CACHE = {}


def _get_prog():
    key = repr(sorted(DEBUG.items()))
    if key not in _CACHE:
        p = Prog()
        p.build()
        _CACHE[key] = p
    return _CACHE[key]


def kernel(**inp):
    inp = {k: np.asarray(v) for k, v in inp.items()}
    p = _get_prog()
    f32 = np.float32
    wst = np.zeros((128, max(p.wtotal, 1)), f32)
    for (src, key), off in p.offs.items():
        if src == "w":
            u = _unit_w(key, inp)
            wst[:, off:off + u.shape[1]] = u
    pv = np.zeros((128, p.NPV), f32)

    def fm(v):
        return np.asarray(v, f32).reshape(8, 128).T
    for nm, src in (("g0", "ffn1_norm"), ("g1", "mix_norm"), ("g2", "ffn2_norm")):
        for l in range(2):
            pv[:, p.PV[nm] + l * 8:p.PV[nm] + l * 8 + 8] = fm(inp[src][l])
    pv[:, p.PV["aon"]:p.PV["aon"] + 4] = inp["a_out_norm"][0].T
    pv[:, p.PV["gq"]] = np.tile(inp["b_q_norm"][0], 2)
    pv[:, p.PV["gk"]] = np.tile(inp["b_k_norm"][0], 2)
    pv[:, p.PV["gbias"]:p.PV["gbias"] + 8] = np.broadcast_to(inp["ab_gate_bias"][0][None, :], (128, 8))
    pv[:, p.PV["convw"]:p.PV["convw"] + 32] = inp["c_conv_w"][0].reshape(4, 8, 128).transpose(2, 1, 0).reshape(128, 32)
    pv[:, p.PV["convb"]:p.PV["convb"] + 8] = fm(inp["c_conv_b"][0])
    pv[:, p.PV["gateb"]:p.PV["gateb"] + 16] = inp["c_gate_b"][0].reshape(2, 8, 128).transpose(2, 1, 0).reshape(128, 16)
    pv[:, p.PV["lam"]:p.PV["lam"] + 8] = fm(inp["c_lambda"][0])
    adab = np.ascontiguousarray(inp["ada_b"].reshape(2, 72, 128).transpose(2, 0, 1), dtype=f32)
    tb = inp["b_rel_bias"][0]
    j = np.arange(640)[:, None]
    i = np.arange(128)[None, :]
    idx = np.clip(j - 512 - i, -128, 128) + 128
    bt = tb[:, idx]
    bt = bt.reshape(4, 2, 5, 128, 128).transpose(0, 3, 1, 2, 4).reshape(4, 128, 2, 640)
    biasT = np.ascontiguousarray(bt, f32)
    in_maps = []
    for c in range(8):
        sl = slice(4 * c, 4 * c + 4)
        xT = np.concatenate([inp["x_prompt"][c], inp["x_sample"][sl].reshape(TS, D)], axis=0)
        xT = np.ascontiguousarray(xT.T.reshape(8, 128, SEQ + TS).transpose(1, 0, 2))
        cc = np.concatenate([inp["c_prompt"][c:c + 1], inp["c_sample"][sl]], axis=0)
        cT = np.ascontiguousarray(cc.T.reshape(8, 128, 5).transpose(1, 0, 2))
        cst = np.zeros((128, max(p.ctotal, 1)), f32)
        ck, cv = inp["cache_b_k"][0][sl], inp["cache_b_v"][0][sl]
        for (src, key), off in p.offs.items():
            if src == "c":
                u = _unit_c(key, ck, cv)
                cst[:, off:off + u.shape[1]] = u
        sv = np.zeros((128, p.NSV), f32)
        sv[:, 0:16] = inp["state_a_n"][0][sl].reshape(16, 128).T
        sv[:, 16:32] = np.broadcast_to(inp["state_a_m"][0][sl].reshape(1, 16), (128, 16))
        sv[:, 32:128] = inp["state_c_conv"][0][sl].reshape(4, 3, 8, 128).transpose(3, 2, 0, 1).reshape(128, 96)
        sv[:, 128:160] = inp["state_c_h"][0][sl].reshape(4, 8, 128).transpose(2, 1, 0).reshape(128, 32)
        aC = np.ascontiguousarray(inp["state_a_C"][0][sl].reshape(16, 128, 128))
        in_maps.append(dict(xT=xT, cT=cT, wst=wst, cst=cst, pv=pv, sv=sv, adab=adab, biasT=biasT, aC=aC))
    res = run_bass_kernel_spmd(p.nc, in_maps, core_ids=list(range(8)))
    R = res.results
    if DEBUG.get("raw"):
        return R
    B = 8

    def cat(fn):
        return np.stack([fn(R[c]) for c in range(B)], axis=0)
    yT = cat(lambda r: r["o_yT"])
    y_all = yT.transpose(0, 3, 2, 1).reshape(B, SEQ + TS, D)
    y_prompt = np.ascontiguousarray(y_all[:, :SEQ])
    y_sample = np.ascontiguousarray(y_all[:, SEQ:].reshape(B * NSQ, LS, D))
    p_a_C = cat(lambda r: r["o_paC"])[None]
    p_a_n = cat(lambda r: r["o_pan"].T)[None]
    p_a_m = cat(lambda r: r["o_pam"][0])[None]
    p_b_k = cat(lambda r: r["o_pbk"].transpose(2, 1, 0).reshape(512, 8, 64))[None]
    p_b_v = cat(lambda r: r["o_pbv"].reshape(512, 8, 64))[None]
    p_c_conv = cat(lambda r: r["o_pconv"].transpose(2, 1, 0).reshape(3, D))[None]
    p_c_h = cat(lambda r: r["o_pch"].T.reshape(D))[None]
    s_a_C = cat(lambda r: r["o_saC"].reshape(4, 4, 128, 128)).reshape(1, 32, 4, 128, 128)
    s_a_n = cat(lambda r: r["o_san"].T.reshape(4, 4, 128)).reshape(1, 32, 4, 128)
    s_a_m = cat(lambda r: r["o_sam"][0].reshape(4, 4)).reshape(1, 32, 4)
    s_b_k = cat(lambda r: r["o_sbk"].reshape(128, 4, NSQ, LS).transpose(2, 3, 1, 0).reshape(NSQ, LS, 8, 64)).reshape(1, 32, LS, 8, 64)
    s_b_v = cat(lambda r: r["o_sbv"].reshape(NSQ, LS, 8, 64)).reshape(1, 32, LS, 8, 64)
    s_c_conv = cat(lambda r: r["o_sconv"].transpose(2, 3, 1, 0).reshape(NSQ, 3, D)).reshape(1, 32, 3, D)
    s_c_h = cat(lambda r: r["o_sch"].transpose(2, 1, 0).reshape(NSQ, D)).reshape(1, 32, D)
    outs = (y_prompt, y_sample, p_a_C, p_a_n, p_a_m, p_b_k, p_b_v, p_c_conv, p_c_h,
            s_a_C, s_a_n, s_a_m, s_b_k, s_b_v, s_c_conv, s_c_h)
    return tuple(np.ascontiguousarray(o, dtype=np.float32) for o in outs)
```

```python
import numpy as np
from contextlib import ExitStack
import concourse.bass as bass
import concourse.mybir as mybir
from concourse.bass_utils import run_bass_kernel_spmd

F32 = mybir.dt.float32
BF16 = mybir.dt.bfloat16
AF = mybir.ActivationFunctionType
ALU = mybir.AluOpType

D = 1024
NCH = 8
DFF = 2816
NJ = 22
SEQ = 2048
TP = 1024
TS = 64
NSQ = 4
LS = 16
TMAX = TP + TS
EPS = 1e-6
SLOT = 1024
NST = 4
NBF = 4
LOOK_D = 3
LOOK_C = 2
FW = 1104
FO_SPLIT = ((0, 8), (8, 15), (15, 22))
NEG = -30000.0

DEBUG = {}


class Tile:
    __slots__ = ("ap", "w", "r", "name")

    def __init__(self, ap, name=""):
        self.ap = ap
        self.w = None
        self.r = {}
        self.name = name


class TG:
    def __init__(self, ap, name=""):
        self.ap = ap
        self.name = name
        self.parts = [Tile(ap, name + "_p%d" % i) for i in range(3)]


class View(TG):
    def __init__(self, ap, parts, name=""):
        self.ap = ap
        self.name = name
        self.parts = list(parts)


def _flat(items):
    out = []
    for t in items:
        if isinstance(t, TG):
            out.extend(t.parts)
        elif isinstance(t, (list, tuple)):
            out.extend(_flat(t))
        else:
            out.append(t)
    return out


class Eng:
    def __init__(self, name, e, sem, step=1):
        self.name = name
        self.e = e
        self.sem = sem
        self.cnt = 0
        self.seen = {}
        self.step = step


class Builder:
    def __init__(self):
        self.nc = bass.Bass("TRN2", target_bir_lowering=False)
        self.es = ExitStack()
        nc = self.nc
        self.pe = Eng("pe", nc.tensor, self.sem("s_pe"))
        self.act = Eng("act", nc.scalar, self.sem("s_act"))
        self.dve = Eng("dve", nc.vector, self.sem("s_dve"))
        self.pool = Eng("pool", nc.gpsimd, self.sem("s_pool"))
        self.sp = Eng("sp", nc.sync, None)
        self.chans = [Eng("ch%d" % i, None, self.sem("s_ch%d" % i), 16) for i in range(12)]
        self.chan_i = 0
        self.units = []
        self.unit_keys = {}
        self.n_sb = 0
        self.dry = False

    def sem(self, name):
        return self.es.enter_context(self.nc.semaphore(name))

    def sb(self, shape, dtype, name=None):
        self.n_sb += 1
        t = self.es.enter_context(self.nc.sbuf_tensor("t_" + (name or ("sb%d" % self.n_sb)), list(shape), dtype))
        return t

    def tgroup(self, shape, dtype, name=None):
        t = self.sb(shape, dtype, name)
        return TG(t[tuple(slice(None) for _ in shape)], name or "")

    def tile(self, shape, dtype, name=None):
        t = self.sb(shape, dtype, name)
        return Tile(t[tuple(slice(None) for _ in shape)], name or "")

    def _waits(self, eng, reads, writes):
        need = {}

        def req(wc):
            if wc is None:
                return
            who, c = wc
            if need.get(who, 0) < c:
                need[who] = c
        inorder = eng is self.act or eng is self.dve
        for t in reads:
            req(t.w)
        for t in writes:
            if not (inorder and t.w is not None and t.w[0] is eng):
                req(t.w)
            for who, c in t.r.items():
                if inorder and who is eng:
                    continue
                req((who, c))
        for who, c in need.items():
            if who is eng and eng is self.pe:
                continue
            if eng.seen.get(who, 0) >= c:
                continue
            assert c <= who.cnt, ("wait on a not-yet-emitted increment", eng.name, who.name, c, who.cnt)
            eng.e.wait_ge(who.sem, c)
            eng.seen[who] = c

    def op(self, eng, fn, reads=(), writes=(), inc=True):
        if self.dry:
            return None
        reads = _flat(reads)
        writes = _flat(writes)
        self._waits(eng, reads, writes)
        ins = fn()
        if inc:
            eng.cnt += 1
            ins.then_inc(eng.sem, 1)
            mark = eng.cnt
        else:
            mark = eng.cnt + 1
        for t in writes:
            t.w = (eng, mark)
            t.r = {}
        for t in reads:
            if t.r.get(eng, 0) < mark:
                t.r[eng] = mark
        return ins

    def dma(self, out_ap, in_ap, reads=(), writes=(), queue=None, chan=None):
        q = queue or self.sp
        if self.dry:
            return None
        reads = _flat(reads)
        writes = _flat(writes)
        if chan is None:
            chan = self.chans[self.chan_i % len(self.chans)]
            self.chan_i += 1
        if chan.cnt > 0 and q.seen.get(chan, 0) < chan.cnt:
            q.e.wait_ge(chan.sem, chan.cnt)
            q.seen[chan] = chan.cnt
        self._waits(q, reads, writes)
        ins = q.e.dma_start(out=out_ap, in_=in_ap)
        chan.cnt += 16
        ins.then_inc(chan.sem, 16)
        for t in writes:
            t.w = (chan, chan.cnt)
            t.r = {}
        for t in reads:
            t.r[chan] = chan.cnt
        return chan

    def finish(self):
        if self.dry:
            return
        for ch in self.chans + list(self.stream_ch):
            if ch.cnt > 0:
                self.sp.e.wait_ge(ch.sem, ch.cnt)
        for e in (self.pe, self.act, self.dve, self.pool):
            if e.cnt > 0:
                self.sp.e.wait_ge(e.sem, e.cnt)

    def mm(self, out_t, out_ap, l_t, l_ap, r_t, r_ap, start, stop, inc=None):
        if inc is None:
            inc = stop
        return self.op(self.pe, lambda: self.nc.tensor.matmul(out_ap, lhsT=l_ap, rhs=r_ap, start=start, stop=stop),
                       reads=[l_t, r_t], writes=[out_t], inc=inc)

    def A(self, out_t, out_ap, in_t, in_ap, func, bias=0.0, scale=1.0, extra=()):
        return self.op(self.act, lambda: self.nc.scalar.activation(out=out_ap, in_=in_ap, func=func, bias=bias, scale=scale),
                       reads=[in_t] + list(extra), writes=[out_t])

    def V(self, eng, meth, out_t, reads, **kw):
        e = eng.e
        outs = list(out_t) if isinstance(out_t, (list, tuple)) else [out_t]
        return self.op(eng, lambda: getattr(e, meth)(**kw), reads=list(reads), writes=outs)

    def plan_unit(self, key, n, cast=True, src="w"):
        assert n <= SLOT, (key, n)
        self.units.append(dict(key=key, n=n, cast=cast, src=src))

    def stream_setup(self, wdram, cdram, offs):
        self.stream_ch = [Eng("st%d" % i, None, self.sem("s_st%d" % i), 16) for i in range(NST)]
        self.st_tiles = [self.tile([128, SLOT], F32, "stg%d" % i) for i in range(NST)]
        self.bf_tiles = [self.tile([128, SLOT], BF16, "wbf%d" % i) for i in range(NBF)]
        self.wdram = wdram
        self.cdram = cdram
        self.offs = offs
        self.s_dma = 0
        self.s_cast = 0
        self.s_next = 0

    def _issue_dma(self, k):
        u = self.units[k]
        st = self.st_tiles[k % NST]
        src = self.wdram if u["src"] == "w" else self.cdram
        off = self.offs[(u["src"], u["key"])]
        self.dma(st.ap[:, 0:u["n"]], src[:, off:off + u["n"]], writes=[st], chan=self.stream_ch[k % NST])

    def _issue_cast(self, k):
        u = self.units[k]
        if not u["cast"]:
            return
        st = self.st_tiles[k % NST]
        bf = self.bf_tiles[k % NBF]
        n = u["n"]
        if u["key"][0] in ("f_in", "f_out", "ada"):
            self.n_cast = getattr(self, "n_cast", 0) + 1
            if self.n_cast % 2:
                self.op(self.act, lambda: self.nc.scalar.activation(out=bf.ap[:, 0:n], in_=st.ap[:, 0:n], func=AF.Copy), reads=[st], writes=[bf])
            else:
                self.op(self.dve, lambda: self.nc.vector.tensor_copy(out=bf.ap[:, 0:n], in_=st.ap[:, 0:n]), reads=[st], writes=[bf])
        else:
            self.op(self.act, lambda: self.nc.scalar.activation(out=bf.ap[:, 0:n], in_=st.ap[:, 0:n], func=AF.Copy), reads=[st], writes=[bf])

    def get(self, key):
        if self.dry:
            kind = key[0]
            n = {"ada": 1024, "f_in": 1024, "ab_gate": 64, "ab_in": 1024, "ab_out": 1024, "c_in": 1024, "c_gate": 256, "c_out": 1024, "kc": 512, "vc": 512}.get(kind)
            if kind == "f_out":
                j0, j1 = FO_SPLIT[key[4]]
                n = (j1 - j0) * 128
            self.plan_unit(key, n, src=("c" if kind in ("kc", "vc") else "w"))
            return self.bf_tiles[0]
        k = self.s_next
        u = self.units[k]
        assert u["key"] == key, (u["key"], key)
        last = len(self.units) - 1
        while self.s_dma <= min(k + LOOK_D, last):
            self._issue_dma(self.s_dma)
            self.s_dma += 1
        while self.s_cast <= min(k + LOOK_C, last):
            self._issue_cast(self.s_cast)
            self.s_cast += 1
        self.s_next += 1
        t = self.bf_tiles[k % NBF] if u["cast"] else self.st_tiles[k % NST]
        return t


class Prog(Builder):
    def __init__(self):
        super().__init__()
        self.declare_dram()
        self.alloc()
        self.dry = True
        self.build_body()
        self.dry = False
        self.ps_i = 0
        self.n_cast = 0
        self.offs = {}
        self.wtotal = 0
        self.ctotal = 0
        for u in self.units:
            k = (u["src"], u["key"])
            if k in self.offs:
                continue
            if u["src"] == "w":
                self.offs[k] = self.wtotal
                self.wtotal += u["n"]
            else:
                self.offs[k] = self.ctotal
                self.ctotal += u["n"]
        self.d_w = self.nc.dram_tensor("wst", [128, max(self.wtotal, 1)], F32, kind="ExternalInput").ap()
        self.d_c = self.nc.dram_tensor("cst", [128, max(self.ctotal, 1)], F32, kind="ExternalInput").ap()
        self.wdram, self.cdram = self.d_w, self.d_c

    def pump(self, n):
        lst = getattr(self, "ada_pending", None)
        for _ in range(n):
            if not lst:
                return
            l, cbk = lst.pop(0)
            self.ada_block(l, cbk)

    PV = dict(g0=0, g1=16, g2=32, aon=48, gq=52, gk=53, gbias=54, convw=62, convb=94, gateb=102, lam=118)
    NPV = 126
    SV = dict(an=0, am=16, conv=32, ch=128)
    NSV = 160

    def declare_dram(self):
        nc = self.nc
        di = lambda n, s: nc.dram_tensor(n, list(s), F32, kind="ExternalInput").ap()
        do = lambda n, s: nc.dram_tensor(n, list(s), F32, kind="ExternalOutput").ap()
        self.d_xT = di("xT", [128, NCH, SEQ + TS])
        self.d_cT = di("cT", [128, NCH, 5])
        self.d_pv = di("pv", [128, self.NPV])
        self.d_sv = di("sv", [128, self.NSV])
        self.d_adab = di("adab", [128, 2, 72])
        self.d_bias = di("biasT", [4, 128, 2, 640])
        self.d_aC = di("aC", [16, 128, 128])
        self.o_yT = do("o_yT", [128, NCH, SEQ + TS])
        self.o_paC = do("o_paC", [4, 128, 128])
        self.o_pan = do("o_pan", [128, 4])
        self.o_pam = do("o_pam", [1, 4])
        self.o_pbk = do("o_pbk", [128, 4, 512])
        self.o_pbv = do("o_pbv", [4, 128, 512])
        self.o_pconv = do("o_pconv", [128, NCH, 3])
        self.o_pch = do("o_pch", [128, NCH])
        self.o_saC = do("o_saC", [16, 128, 128])
        self.o_san = do("o_san", [128, 16])
        self.o_sam = do("o_sam", [1, 16])
        self.o_sbk = do("o_sbk", [128, 4, TS])
        self.o_sbv = do("o_sbv", [NSQ, LS, 512])
        self.o_sconv = do("o_sconv", [128, NCH, NSQ, 3])
        self.o_sch = do("o_sch", [128, NCH, NSQ])
        if DEBUG.get("dbg"):
            self.o_dbg = do("o_dbg", [128, NCH, TMAX])

    def alloc(self):
        nc = self.nc
        self.x = [self.tile([128, TMAX], F32, "x%d" % c) for c in range(NCH)]
        self.h = [self.tgroup([128, TMAX], BF16, "h%d" % c) for c in range(NCH)]
        self.a = []
        self.av32 = []
        for j in range(NJ // 2):
            t = self.sb([128, 2, TMAX], BF16, "apair%d" % j)
            g0, g1 = TG(t[:, 0, :], "a%d" % (2 * j)), TG(t[:, 1, :], "a%d" % (2 * j + 1))
            self.a += [g0, g1]
            self.av32.append(View(t[:, :, :].rearrange("p a b -> p (a b)").bitcast(F32), g0.parts + g1.parts, "av%d" % j))
        self.F = [self.tgroup([128, FW], F32, "F%d" % i) for i in range(7)]
        self.sg = [self.tile([128, 512], F32, "sg%d" % i) for i in range(2)]
        self.ps2 = [self.es.enter_context(nc.psum_tensor("ps%d" % i, [128, 1024], F32)) for i in range(4)]
        self.ps = []
        for i in range(4):
            self.ps.append(Tile(self.ps2[i][:, 0:512], "psb%d" % (2 * i)))
            self.ps.append(Tile(self.ps2[i][:, 512:1024], "psb%d" % (2 * i + 1)))
        self.ps_i = 0
        self.stream_setup(None, None, None)
        self.pv = self.tile([128, self.NPV], F32, "pv")
        self.sv = self.tile([128, self.NSV], F32, "sv")
        self.cT = self.tile([128, NCH, 5], F32, "cT")
        self.cTb = self.tile([128, NCH, 5], BF16, "cTb")
        self.cst = self.tile([128, 8], F32, "cst")
        self.ones_bf = self.tile([128, 128], BF16, "ones_bf")
        self.blk_bf = self.tile([128, 128], BF16, "blk_bf")
        self.ones_w = self.tile([128, TP], BF16, "ones_w")
        self.ident = self.tile([128, 128], F32, "ident")
        self.adaT = [self.tile([128, 72, 5], F32, "adaT%d" % l) for l in range(2)]
        self.adatok = [self.F[4], self.F[5]]
        self.adabT = self.tile([128, 2, 72], F32, "adabT")
        self.gs = [[self.tile([128, NCH, 5], F32, "gs%d_%d" % (l, m)) for m in range(3)] for l in range(2)]
        self.gt = [[self.tile([128, NCH, 5], F32, "gt%d_%d" % (l, m)) for m in range(3)] for l in range(2)]
        self.GS_tok = self.tile([128, NCH, TS], F32, "GS_tok")
        self.SH_tok = self.tile([128, NCH, TS], F32, "SH_tok")
        self.GT_tok = self.tile([128, NCH, TS], F32, "GT_tok")
        self.small = self.tile([128, 64], F32, "small")
        self.C = [self.tile([128, 128], F32, "C%d" % i) for i in range(4)]
        self.Cb = [self.tile([128, 128], BF16, "Cb%d" % i) for i in range(4)]
        self.nr = [self.tile([128, 128], F32, "nr%d" % i) for i in range(4)]
        self.nrb = [self.tile([128, 128], BF16, "nrb%d" % i) for i in range(4)]
        self.Gl = self.tile([128, 4], F32, "Gl")
        self.Ml = self.tile([128, 4], F32, "Ml")
        self.Cs4 = self.tile([128, NSQ, 128], F32, "Cs4")
        self.Cs4b = self.tile([128, NSQ, 128], BF16, "Cs4b")
        self.nr4b = self.tile([128, NSQ, 128], BF16, "nr4b")
        self.kw4 = self.tile([16, NSQ * 128], BF16, "kw4")
        self.cs4 = self.tile([128, 32], F32, "cs4")
        self.gw = self.tile([128, 64], BF16, "gw")
        self.wrep = self.tile([128, 1024], BF16, "wrep")
        self.cols = self.tile([128, 16], F32, "cols")
        self.m128 = [self.tile([128, 128], F32, "m128_%d" % i) for i in range(4)]
        self.b128 = [self.tile([128, 128], BF16, "b128_%d" % i) for i in range(3)]
        self.tokb = [self.tile([16, 512], BF16, "tokb%d" % i) for i in range(4)]
        self.tokf = self.sg[1]
        self.maskb = self.tile([128, 128], F32, "maskb")
        self.c8 = self.tgroup([128, 64], F32, "c8")
        self.outs = self.tile([128, 64], F32, "outs")
        self.kband = [self.tile([128, 512], BF16, "kband%d" % i) for i in range(4)]
        self.vband = self.tile([128, 4, 512], BF16, "vband")
        self.biasT = self.tile([128, 2, 640], F32, "biasTs")
        self.convc = self.tile([128, NCH, 3], F32, "convc")
        self.hc = self.tile([128, NCH], F32, "hc")
        self.XP = self.F[6]
        self.sconv = self.tile([128, NCH, NSQ, 3], F32, "sconv")
        self.sch = self.tile([128, NCH, NSQ], F32, "sch")

    def next_ps(self):
        if self.ps_i == 0 and DEBUG.get("verbose"):
            print("sbuf bytes remaining", self.nc.sbuf_bytes_remaining())
        t = self.ps[self.ps_i % 8]
        self.ps_i += 1
        return t

    def next_ps2(self):
        if self.ps_i % 2:
            self.ps_i += 1
        i = (self.ps_i % 8) // 2
        self.ps_i += 2
        return self.ps[2 * i], self.ps[2 * i + 1], self.ps2[i]

    def pvc(self, name, i=0, n=1):
        o = self.PV[name] + i
        return self.pv.ap[:, o:o + n]

    def svc(self, name, i=0, n=1):
        o = self.SV[name] + i
        return self.sv.ap[:, o:o + n]

    def cbs(self, pi):
        r = [(0, 512), (512, 1024)]
        if pi == 1:
            r.append((TP, TP + TS))
        return r

    def setup(self):
        nc = self.nc
        q = self.act
        self.dma(self.pv.ap[:, :], self.d_pv, writes=[self.pv], queue=q)
        self.dma(self.sv.ap[:, :], self.d_sv, writes=[self.sv], queue=q)
        self.dma(self.cT.ap[:, :, :], self.d_cT, writes=[self.cT], queue=q)
        self.dma(self.adabT.ap[:, :, :], self.d_adab, writes=[self.adabT], queue=q)
        P, D_ = self.pool, self.dve
        self.A(self.cTb, self.cTb.ap[:, :, :], self.cT, self.cT.ap[:, :, :], AF.Copy)
        c = self.cst
        for i, v in enumerate([EPS, 1.0, 0.5, 0.0]):
            self.V(P, "memset", c, [], ap=c.ap[:, i:i + 1], constant=v)
        self.V(P, "memset", self.ones_bf, [], ap=self.ones_bf.ap[:, :], constant=1.0)
        self.V(P, "memset", self.ones_w, [], ap=self.ones_w.ap[:, :], constant=1.0)
        self.V(P, "memset", self.blk_bf, [], ap=self.blk_bf.ap[:, :], constant=0.0)
        self.V(P, "memset", self.blk_bf, [], ap=self.blk_bf.ap[0:64, 0:64], constant=1.0)
        self.V(P, "memset", self.blk_bf, [], ap=self.blk_bf.ap[64:128, 64:128], constant=1.0)
        self.reg_neg = None if self.dry else nc.gpsimd.to_reg(NEG)
        self.V(P, "memset", self.ident, [], ap=self.ident.ap[:, :], constant=1.0)
        self.V(P, "affine_select", self.ident, [self.ident], out=self.ident.ap[:, :], in_=self.ident.ap[:, :],
               pattern=[[1, 128]], compare_op=ALU.is_equal, fill=0.0, base=0, channel_multiplier=-1)
        self.V(P, "memset", self.maskb, [], ap=self.maskb.ap[:, :], constant=0.0)
        self.V(P, "affine_select", self.maskb, [self.maskb], out=self.maskb.ap[:, :], in_=self.maskb.ap[:, :],
               pattern=[[1, 128]], compare_op=ALU.is_ge, fill=self.reg_neg, base=0, channel_multiplier=-1)
        for t in (self.convc, self.hc, self.Gl, self.Ml):
            self.V(P, "memset", t, [], ap=t.ap, constant=0.0)
        for hd in range(4):
            for t in (self.C[hd], self.nr[hd]):
                self.V(P, "memset", t, [], ap=t.ap[:, :], constant=0.0)
            for t in (self.Cb[hd], self.nrb[hd]):
                self.V(P, "memset", t, [], ap=t.ap[:, :], constant=0.0)
        s = self.small
        self.A(s, s.ap[:, 0:8], self.pv, self.pvc("lam", 0, 8), AF.Exp, scale=-1.0)
        self.A(s, s.ap[:, 0:8], s, s.ap[:, 0:8], AF.Ln, bias=self.cst.ap[:, 1:2], extra=[self.cst])
        self.V(D_, "tensor_scalar", s, [s], out=s.ap[:, 0:8], in0=s.ap[:, 0:8], scalar1=-4.0, scalar2=None, op0=ALU.mult)
        self.V(D_, "tensor_scalar", s, [self.pv], out=s.ap[:, 8:24], in0=self.pvc("gateb", 0, 16), scalar1=0.5, scalar2=None, op0=ALU.mult)
        self.V(D_, "tensor_scalar", s, [self.pv], out=s.ap[:, 24:28], in0=self.pvc("aon", 0, 4), scalar1=0.5, scalar2=None, op0=ALU.mult)
        self.V(D_, "tensor_scalar", s, [self.pv], out=s.ap[:, 28:29], in0=self.pvc("gq"), scalar1=0.125, scalar2=None, op0=ALU.mult)
        self.V(D_, "tensor_scalar", s, [self.pv], out=s.ap[:, 30:38], in0=self.pvc("gbias", 0, 8), scalar1=-1.0, scalar2=None, op0=ALU.mult)

    def ada_block(self, l, cbk):
        nc = self.nc
        D_ = self.dve
        aT = self.adaT[l]
        ps = self.next_ps()
        for kq in range(4):
            w = self.get(("ada", l, cbk, kq))
            for i in range(2):
                k = 2 * kq + i
                self.mm(ps, ps.ap[0:5, 0:512], self.cTb, self.cTb.ap[:, k, :], w, w.ap[:, i * 512:(i + 1) * 512],
                        start=(k == 0), stop=(k == 7), inc=(i == 1))
        tok = self.adatok[cbk % 2]
        self.A(tok, tok.ap[0:5, 0:512], ps, ps.ap[0:5, 0:512], AF.Copy)
        ps2 = self.next_ps()
        for i in range(4):
            self.op(self.pe, lambda i=i: nc.tensor.transpose(ps2.ap[:, i * 8:i * 8 + 5], tok.ap[0:5, i * 128:(i + 1) * 128], self.ident.ap[0:5, 0:5]),
                    reads=[tok, self.ident], writes=[ps2], inc=(i == 3))
        self.V(D_, "tensor_tensor", aT, [ps2, self.adabT], out=aT.ap[:, cbk * 4:(cbk + 1) * 4, :],
               in0=ps2.ap[:, 0:32].rearrange("p (a b) -> p a b", b=8)[:, :, 0:5],
               in1=self.adabT.ap[:, l, cbk * 4:(cbk + 1) * 4].unsqueeze(2).to_broadcast([128, 4, 5]), op=ALU.add)

    def ada_derive(self, l, ms=(0, 1, 2), scale=True, gate=True):
        D_ = self.dve
        aT = self.adaT[l]
        for m in ms:
            gname = ("g0", "g1", "g2")[m]
            gs, gt = self.gs[l][m], self.gt[l][m]
            if scale:
                self.V(D_, "tensor_scalar", gs, [aT], out=gs.ap[:, :, :], in0=aT.ap[:, (3 * m + 1) * 8:(3 * m + 2) * 8, :],
                       scalar1=1.0, scalar2=None, op0=ALU.add)
                self.V(D_, "tensor_tensor", gs, [gs, self.pv], out=gs.ap[:, :, :], in0=gs.ap[:, :, :],
                       in1=self.pvc(gname, l * 8, 8).unsqueeze(2).to_broadcast([128, NCH, 5]), op=ALU.mult)
            if gate:
                self.V(D_, "tensor_scalar", gt, [aT], out=gt.ap[:, :, :], in0=aT.ap[:, (3 * m + 2) * 8:(3 * m + 3) * 8, :],
                       scalar1=(1.0 if m == 1 else 0.5), scalar2=None, op0=ALU.mult)

    def shift_col(self, l, m, c, r=0):
        return self.adaT[l].ap[:, 3 * m * 8 + c, r:r + 1]

    def expand_tok(self, l, m):
        D_ = self.dve
        aT = self.adaT[l]
        for dst, src_t, src in ((self.GS_tok, self.gs[l][m], self.gs[l][m].ap[:, :, 1:5]),
                                (self.SH_tok, aT, aT.ap[:, 3 * m * 8:(3 * m + 1) * 8, 1:5]),
                                (self.GT_tok, self.gt[l][m], self.gt[l][m].ap[:, :, 1:5])):
            for c in range(NCH):
                self.V(D_, "tensor_copy", dst, [src_t], out=dst.ap[:, c, :].rearrange("p (s t) -> p s t", t=LS),
                       in_=src[:, c, :].unsqueeze(2).to_broadcast([128, NSQ, LS]))

    def norm_mod(self, l, m, pi):
        D_ = self.dve
        T = TP + (TS if pi == 1 else 0)
        if pi == 1:
            self.expand_tok(l, m)
        sq = [self.a[14 + c] for c in range(NCH)]
        for c in range(NCH):
            self.A(sq[c], sq[c].ap[:, 0:T], self.x[c], self.x[c].ap[:, 0:T], AF.Square)
        lnt, rstd = self.F[0], self.F[1]
        for gi, (c0, c1) in enumerate(self.cbs(pi)):
            ps = self.next_ps()
            for c in range(NCH):
                self.mm(ps, ps.ap[:, 0:c1 - c0], self.ones_bf, self.ones_bf.ap[:, :], sq[c], sq[c].ap[:, c0:c1], start=(c == 0), stop=(c == 7))
            self.A(lnt.parts[gi], lnt.ap[:, c0:c1], ps, ps.ap[:, 0:c1 - c0], AF.Ln, bias=self.cst.ap[:, 0:1], scale=1.0 / D, extra=[self.cst])
            self.A(rstd.parts[gi], rstd.ap[:, c0:c1], lnt.parts[gi], lnt.ap[:, c0:c1], AF.Exp, scale=-0.5)
        for gi, (c0, c1) in enumerate(self.cbs(pi)):
            for c in range(NCH):
                tmp = self.F[2 + (c % 2)]
                tp, hp_, rp = tmp.parts[gi], self.h[c].parts[gi], rstd.parts[gi]
                if c0 < TP:
                    self.V(D_, "scalar_tensor_tensor", tp, [self.x[c], self.gs[l][m], rp], out=tmp.ap[:, c0:c1], in0=self.x[c].ap[:, c0:c1],
                           scalar=self.gs[l][m].ap[:, c, 0:1], in1=rstd.ap[:, c0:c1], op0=ALU.mult, op1=ALU.mult)
                    self.A(hp_, self.h[c].ap[:, c0:c1], tp, tmp.ap[:, c0:c1], AF.Identity, bias=self.shift_col(l, m, c), extra=[self.adaT[l]])
                else:
                    self.V(D_, "tensor_tensor", tp, [self.x[c], rp], out=tmp.ap[:, TP:T], in0=self.x[c].ap[:, TP:T], in1=rstd.ap[:, TP:T], op=ALU.mult)
                    self.V(D_, "tensor_tensor", tp, [tp, self.GS_tok], out=tmp.ap[:, TP:T], in0=tmp.ap[:, TP:T], in1=self.GS_tok.ap[:, c, :], op=ALU.mult)
                    self.V(D_, "tensor_tensor", hp_, [tp, self.SH_tok], out=self.h[c].ap[:, TP:T], in0=tmp.ap[:, TP:T], in1=self.SH_tok.ap[:, c, :], op=ALU.add)

    def resid(self, l, m, pi, i, ps, c0, c1):
        D_ = self.dve
        x = self.x[i]
        if c0 < TP:
            e = min(c1, TP)
            self.V(D_, "scalar_tensor_tensor", x, [ps, self.gt[l][m], x], out=x.ap[:, c0:e], in0=ps.ap[:, 0:e - c0],
                   scalar=self.gt[l][m].ap[:, i, 0:1], in1=x.ap[:, c0:e], op0=ALU.mult, op1=ALU.add)
        if c1 > TP:
            b = max(c0, TP)
            tmp = self.sg[0]
            self.V(D_, "tensor_tensor", tmp, [ps, self.GT_tok], out=tmp.ap[:, 0:c1 - b], in0=ps.ap[:, b - c0:c1 - c0], in1=self.GT_tok.ap[:, i, b - TP:c1 - TP], op=ALU.mult)
            self.V(D_, "tensor_tensor", x, [tmp, x], out=x.ap[:, b:c1], in0=tmp.ap[:, 0:c1 - b], in1=x.ap[:, b:c1], op=ALU.add)

    def parts_of(self, tg, c0, c1):
        ps = []
        if c0 < 512:
            ps.append(tg.parts[0])
        if c1 > 512 and c0 < 1024:
            ps.append(tg.parts[1])
        if c1 > 1024:
            ps.append(tg.parts[2])
        return ps

    def ffn(self, l, f, pi):
        D_ = self.dve
        m = 0 if f == 1 else 2
        cbs = self.cbs(pi) if pi == 0 else [(0, 363), (363, 726), (726, TMAX)]
        self.norm_mod(l, m, pi)
        sgi = 0
        step = 0
        pumping = (pi == 0 and l == 0 and f == 1)
        for j in range(NJ):
            wg = self.get(("f_in", l, f, j, 0))
            pg = [self.next_ps() for _ in cbs]
            for ci, (c0, c1) in enumerate(cbs):
                for k in range(NCH):
                    self.mm(pg[ci], pg[ci].ap[:, 0:c1 - c0], wg, wg.ap[:, k * 128:(k + 1) * 128], self.parts_of(self.h[k], c0, c1), self.h[k].ap[:, c0:c1], start=(k == 0), stop=(k == 7))
            wu = self.get(("f_in", l, f, j, 1))
            pu = [self.next_ps() for _ in cbs]
            for ci, (c0, c1) in enumerate(cbs):
                for k in range(NCH):
                    self.mm(pu[ci], pu[ci].ap[:, 0:c1 - c0], wu, wu.ap[:, k * 128:(k + 1) * 128], self.parts_of(self.h[k], c0, c1), self.h[k].ap[:, c0:c1], start=(k == 0), stop=(k == 7))
            for ci, (c0, c1) in enumerate(cbs):
                sg = self.sg[sgi % 2]
                sgi += 1
                n = c1 - c0
                self.A(sg, sg.ap[:, 0:n], pg[ci], pg[ci].ap[:, 0:n], AF.Silu)
                self.V(D_, "tensor_tensor", self.parts_of(self.a[j], c0, c1), [sg, pu[ci]], out=self.a[j].ap[:, c0:c1], in0=sg.ap[:, 0:n], in1=pu[ci].ap[:, 0:n], op=ALU.mult)
            if pumping and step % 4 == 0:
                self.pump(1)
            step += 1
        if pumping:
            self.ada_derive(0, (0,), scale=False)
        for i in range(NCH):
            ps = [self.next_ps() for _ in cbs]
            for hf, (j0, j1) in enumerate(FO_SPLIT):
                w = self.get(("f_out", l, f, i, hf))
                for ci, (c0, c1) in enumerate(cbs):
                    for j in range(j0, j1):
                        self.mm(ps[ci], ps[ci].ap[:, 0:c1 - c0], w, w.ap[:, (j - j0) * 128:(j - j0 + 1) * 128], self.parts_of(self.a[j], c0, c1), self.a[j].ap[:, c0:c1],
                                start=(j == 0), stop=(j == NJ - 1), inc=(j == j1 - 1))
            for ci, (c0, c1) in enumerate(cbs):
                self.resid(l, m, pi, i, ps[ci], c0, c1)
            if pumping and step % 4 == 0:
                self.pump(1)
            step += 1
        if pumping:
            self.ada_derive(0, (1,))

    def load_x(self, pi):
        for c in range(NCH):
            self.dma(self.x[c].ap[:, 0:TP], self.d_xT[:, c, pi * TP:(pi + 1) * TP], writes=[self.x[c]], queue=self.act)
            if pi == 1:
                self.dma(self.x[c].ap[:, TP:TMAX], self.d_xT[:, c, SEQ:SEQ + TS], writes=[self.x[c]], queue=self.act)

    def store_x(self, pi, dst=None):
        dst = dst if dst is not None else self.o_yT
        for c in range(NCH):
            self.dma(dst[:, c, pi * TP:(pi + 1) * TP], self.x[c].ap[:, 0:TP], reads=[self.x[c]])
            if pi == 1:
                self.dma(dst[:, c, SEQ:SEQ + TS], self.x[c].ap[:, TP:TMAX], reads=[self.x[c]])

    def interleave(self, *gens):
        gens = list(gens)
        while gens:
            for g in list(gens):
                try:
                    next(g)
                except StopIteration:
                    gens.remove(g)

    def interleave_g(self, *gens):
        gens = list(gens)
        while gens:
            for g in list(gens):
                try:
                    next(g)
                    yield
                except StopIteration:
                    gens.remove(g)

    def mixer1_chunk(self, pi, n, bufs):
        D_ = self.dve
        T = TP + (TS if pi == 1 else 0)
        cbs = self.cbs(pi)
        XP = self.XP
        xs0 = 3 + TP
        XPs = XP.ap[:, xs0:xs0 + 76].rearrange("p (s t) -> p s t", t=19)
        small = self.small
        xc, tr, ti, av, tq, gbs, xcb = bufs
        hs = ti
        xcs = xc.ap[:, TP:TP + TS].rearrange("p (s t) -> p s t", t=LS)
        self.V(D_, "tensor_copy", XP, [self.convc], out=XP.ap[:, 0:3], in_=self.convc.ap[:, n, :])
        if pi == 1:
            self.V(D_, "tensor_copy", XP, [self.sv], out=XPs[:, :, 0:3], in_=self.svc("conv", n * 12, 12).rearrange("p (s t) -> p s t", t=3))
        yield
        w = self.get(("c_in", 0, n))
        for (c0, c1) in cbs:
            ps = self.next_ps()
            for k in range(NCH):
                self.mm(ps, ps.ap[:, 0:c1 - c0], w, w.ap[:, k * 128:(k + 1) * 128], self.h[k].parts[0 if c0 < 512 else (1 if c0 < 1024 else 2)], self.h[k].ap[:, c0:c1], start=(k == 0), stop=(k == 7))
            self.A(gbs, gbs.ap[:, c0:c1], ps, ps.ap[:, 0:c1 - c0], AF.Copy)
            self.A(tq, tq.ap[:, c0:c1], ps, ps.ap[:, 0:c1 - c0], AF.Square)
            yield
        w = self.get(("c_in", 1, n))
        for (c0, c1) in cbs:
            ps = self.next_ps()
            for k in range(NCH):
                self.mm(ps, ps.ap[:, 0:c1 - c0], w, w.ap[:, k * 128:(k + 1) * 128], self.h[k].parts[0 if c0 < 512 else (1 if c0 < 1024 else 2)], self.h[k].ap[:, c0:c1], start=(k == 0), stop=(k == 7))
            if c0 < TP:
                self.A(XP, XP.ap[:, 3 + c0:3 + c1], ps, ps.ap[:, 0:c1 - c0], AF.Copy)
            else:
                self.A(XP, XPs[:, :, 3:19], ps, ps.ap[:, 0:TS].rearrange("p (s t) -> p s t", t=LS), AF.Copy)
            yield

        def gelu_gen():
            self.V(D_, "tensor_scalar", tq, [tq], out=tq.ap[:, 0:T], in0=tq.ap[:, 0:T], scalar1=0.044715, scalar2=1.0, op0=ALU.mult, op1=ALU.add)
            yield
            self.V(D_, "tensor_tensor", tq, [tq, gbs], out=tq.ap[:, 0:T], in0=tq.ap[:, 0:T], in1=gbs.ap[:, 0:T], op=ALU.mult)
            yield
            yield
            self.A(tq, tq.ap[:, 0:T], tq, tq.ap[:, 0:T], AF.Tanh, scale=0.7978845608028654)
            yield
            yield
            self.V(D_, "scalar_tensor_tensor", tq, [tq, gbs], out=tq.ap[:, 0:T], in0=tq.ap[:, 0:T], scalar=1.0, in1=gbs.ap[:, 0:T], op0=ALU.add, op1=ALU.mult)
            yield

        half = []

        def xb_gen():
            cw = lambda j: self.pvc("convw", n * 4 + j)
            self.V(D_, "tensor_scalar", xc, [XP, self.pv], out=xc.ap[:, 0:TP], in0=XP.ap[:, 0:TP], scalar1=cw(0), scalar2=self.pvc("convb", n), op0=ALU.mult, op1=ALU.add)
            yield
            for j in range(1, 4):
                self.V(D_, "scalar_tensor_tensor", xc, [XP, self.pv, xc], out=xc.ap[:, 0:TP], in0=XP.ap[:, j:j + TP], scalar=cw(j), in1=xc.ap[:, 0:TP], op0=ALU.mult, op1=ALU.add)
                yield
            if pi == 1:
                self.V(D_, "tensor_scalar", xc, [XP, self.pv], out=xcs, in0=XPs[:, :, 0:LS], scalar1=cw(0), scalar2=self.pvc("convb", n), op0=ALU.mult, op1=ALU.add)
                for j in range(1, 4):
                    self.V(D_, "scalar_tensor_tensor", xc, [XP, self.pv, xc], out=xcs, in0=XPs[:, :, j:j + LS], scalar=cw(j), in1=xcs, op0=ALU.mult, op1=ALU.add)
                yield
            self.V(D_, "tensor_copy", self.convc, [XP], out=self.convc.ap[:, n, :], in_=XP.ap[:, TP:TP + 3])
            if pi == 1:
                self.V(D_, "tensor_copy", self.sconv, [XP], out=self.sconv.ap[:, n, :, :], in_=XPs[:, :, 16:19])
            self.A(xcb, xcb.ap[:, 0:T], xc, xc.ap[:, 0:T], AF.Copy)
            yield
            wg = self.get(("c_gate", n))
            for gi, dst in ((0, tr), (1, ti)):
                for (c0, c1) in cbs:
                    ps = self.next_ps()
                    self.mm(ps, ps.ap[:, 0:c1 - c0], wg, wg.ap[:, gi * 128:(gi + 1) * 128], xcb, xcb.ap[:, c0:c1], start=True, stop=True)
                    self.A(dst, dst.ap[:, c0:c1], ps, ps.ap[:, 0:c1 - c0], AF.Tanh, bias=small.ap[:, 8 + n * 2 + gi:9 + n * 2 + gi], scale=0.5, extra=[small])
                yield
            half.append(1)
            c1h = small.ap[:, n:n + 1]
            self.A(av, av.ap[:, 0:T], tr, tr.ap[:, 0:T], AF.Exp, bias=c1h, scale=c1h, extra=[small])
            yield
            self.V(D_, "scalar_tensor_tensor", tr, [av], out=tr.ap[:, 0:T], in0=av.ap[:, 0:T], scalar=0.99999994, in1=av.ap[:, 0:T], op0=ALU.min, op1=ALU.mult)
            yield
            self.A(tr, tr.ap[:, 0:T], tr, tr.ap[:, 0:T], AF.Sqrt, bias=self.cst.ap[:, 1:2], scale=-1.0, extra=[self.cst])
            self.V(D_, "scalar_tensor_tensor", ti, [ti, xc], out=ti.ap[:, 0:T], in0=ti.ap[:, 0:T], scalar=1.0, in1=xc.ap[:, 0:T], op0=ALU.add, op1=ALU.mult)
            yield
            self.V(D_, "scalar_tensor_tensor", ti, [ti, tr], out=ti.ap[:, 0:T], in0=ti.ap[:, 0:T], scalar=0.5, in1=tr.ap[:, 0:T], op0=ALU.mult, op1=ALU.mult)
            yield
            self.V(D_, "tensor_tensor_scan", hs, [av, ti, self.hc], out=hs.ap[:, 0:TP], data0=av.ap[:, 0:TP], data1=ti.ap[:, 0:TP],
                   initial=self.hc.ap[:, n:n + 1], op0=ALU.mult, op1=ALU.add)
            self.V(D_, "tensor_copy", self.hc, [hs], out=self.hc.ap[:, n:n + 1], in_=hs.ap[:, TP - 1:TP])
            if pi == 1:
                for sq in range(NSQ):
                    cs = TP + LS * sq
                    self.V(D_, "tensor_tensor_scan", hs, [av, ti, self.sv], out=hs.ap[:, cs:cs + LS], data0=av.ap[:, cs:cs + LS], data1=ti.ap[:, cs:cs + LS],
                           initial=self.svc("ch", n * 4 + sq), op0=ALU.mult, op1=ALU.add)
                self.V(D_, "tensor_copy", self.sch, [hs], out=self.sch.ap[:, n, :], in_=hs.ap[:, TP:TP + TS].rearrange("p (s t) -> p s t", t=LS)[:, :, LS - 1])
            yield
        sent = False
        for _ in self.interleave_g(xb_gen(), gelu_gen()):
            if half and not sent:
                sent = True
                yield "HALF"
            else:
                yield
        self.V(D_, "scalar_tensor_tensor", self.a[n], [tq, hs], out=self.a[n].ap[:, 0:T], in0=tq.ap[:, 0:T], scalar=0.5, in1=hs.ap[:, 0:T], op0=ALU.mult, op1=ALU.mult)
        yield

    def mixer1(self, pi):
        l, m = 1, 1
        self.norm_mod(l, m, pi)
        F = self.F
        V_ = self.av32
        sets = [(F[0], F[1], F[2], F[3], F[4], F[5], self.a[8]),
                (V_[5], V_[6], V_[7], V_[8], V_[9], V_[10], self.a[9])]
        gens = [self.mixer1_chunk(pi, n, sets[n % 2]) for n in range(NCH)]
        active = [gens[0]]
        nxt = 1
        want = False
        while active:
            for g in list(active):
                try:
                    tok = next(g)
                except StopIteration:
                    active.remove(g)
                    continue
                if tok == "HALF":
                    want = True
            if want and nxt < NCH and len(active) < 2:
                active.append(gens[nxt])
                nxt += 1
                want = False
            if not active and nxt < NCH:
                active.append(gens[nxt])
                nxt += 1
        self.out_proj(l, pi, "c_out")

    def out_proj(self, l, pi, kind):
        cbs = self.cbs(pi)
        for i in range(NCH):
            w = self.get((kind, i))
            for (c0, c1) in cbs:
                ps = self.next_ps()
                for k in range(NCH):
                    self.mm(ps, ps.ap[:, 0:c1 - c0], w, w.ap[:, k * 128:(k + 1) * 128], self.a[k], self.a[k].ap[:, c0:c1], start=(k == 0), stop=(k == 7))
                self.resid(l, 1, pi, i, ps, c0, c1)

    def proj_fm(self, w, pi, evac):
        for (c0, c1) in self.cbs(pi):
            ps = self.next_ps()
            for k in range(NCH):
                self.mm(ps, ps.ap[:, 0:c1 - c0], w, w.ap[:, k * 128:(k + 1) * 128], self.h[k].parts[0 if c0 < 512 else (1 if c0 < 1024 else 2)], self.h[k].ap[:, c0:c1], start=(k == 0), stop=(k == 7))
            evac(ps, c0, c1)

    def proj_tm(self, w, pi, evac, evac_s):
        for g in range(2):
            ps = self.next_ps()
            for t4 in range(4):
                tt_ = g * 4 + t4
                for k in range(NCH):
                    self.mm(ps, ps.ap[:, t4 * 128:(t4 + 1) * 128], self.h[k].parts[tt_ // 4], self.h[k].ap[:, tt_ * 128:(tt_ + 1) * 128], w, w.ap[:, k * 128:(k + 1) * 128],
                            start=(k == 0), stop=(k == 7))
            evac(ps, g)
        if pi == 1:
            ps = self.next_ps()
            for sq in range(NSQ):
                cs = TP + LS * sq
                for k in range(NCH):
                    self.mm(ps, ps.ap[0:LS, sq * 128:(sq + 1) * 128], self.h[k].parts[2], self.h[k].ap[:, cs:cs + LS], w, w.ap[:, k * 128:(k + 1) * 128],
                            start=(k == 0), stop=(k == 7))
            evac_s(ps)

    def mixer0(self, pi):
        l, m = 0, 1
        P = self.pool
        self.norm_mod(l, m, pi)
        w = self.get(("ab_gate",))
        self.V(P, "tensor_copy", self.gw, [w], out=self.gw.ap[:, 0:64], in_=w.ap[:, 0:64])
        for _ in self.mlstm_P(pi, 0):
            pass
        for hd in range(4):
            if pi == 0:
                self.pump(3)
            cg = self.mlstm_C(pi, hd)
            pg = self.mlstm_P(pi, hd + 1) if hd < 3 else None
            started = False
            for tok in cg:
                if tok == "P_OK":
                    started = True
                if started and pg is not None:
                    try:
                        next(pg)
                    except StopIteration:
                        pg = None
            if pg is not None:
                for _ in pg:
                    pass
        for _ in self.attn_pro(pi, 0):
            pass
        for hp in range(4):
            if pi == 0:
                self.pump(3)
            gens = [self.attn_loop(pi, hp)]
            if hp < 3:
                gens.append(self.attn_pro(pi, hp + 1))
            self.interleave(*gens)
            self.attn_sample(pi, hp)
        if pi == 0:
            self.pump(100)
            self.ada_derive(0, (2,))
            self.ada_derive(1)
        self.out_proj(l, pi, "ab_out")

    def mlstm_chunk(self, c0, L, k_t, k_ap, v_t, v_ap, Mp_t, Mp_ap, C, Cb, nr, nrb, qT, kT, ig, M, emt, hT):
        D_, P = self.dve, self.pool
        cols = self.cols
        m0, m1, m2, _ = self.m128
        Pb, qs, kw = self.b128
        acol = cols.ap[0:L, 1:2]
        self.V(D_, "scalar_tensor_tensor", [m0, cols], [ig, self.ident], out=m0.ap[0:L, 0:L], in0=ig.ap[0:L, c0:c0 + L], scalar=1.0,
               in1=self.ident.ap[0:L, 0:L], op0=ALU.mult, op1=ALU.mult, accum_out=acol)
        self.V(D_, "tensor_scalar", m0, [M, cols], out=m0.ap[0:L, 0:L], in0=M.ap[0:L, c0:c0 + L], scalar1=-1.0, scalar2=acol, op0=ALU.mult, op1=ALU.add)
        self.V(P, "affine_select", m0, [m0], out=m0.ap[0:L, 0:L], in_=m0.ap[0:L, 0:L], pattern=[[1, L]], compare_op=ALU.is_ge, fill=self.reg_neg,
               base=0, channel_multiplier=-1)
        self.A(m0, m0.ap[0:L, 0:L], m0, m0.ap[0:L, 0:L], AF.Exp)
        psS = self.next_ps()
        self.mm(psS, psS.ap[0:L, 0:L], kT, kT.ap[:, c0:c0 + L], qT, qT.ap[:, c0:c0 + L], start=True, stop=True)
        self.V(D_, "tensor_tensor", Pb, [psS, m0], out=Pb.ap[0:L, 0:L], in0=psS.ap[0:L, 0:L], in1=m0.ap[0:L, 0:L], op=ALU.mult)
        self.A(m1, m1.ap[:, 0:L], M, M.ap[:, c0:c0 + L], AF.Exp, bias=Mp_ap, scale=-1.0, extra=[Mp_t])
        self.V(D_, "tensor_tensor", qs, [qT, m1], out=qs.ap[:, 0:L], in0=qT.ap[:, c0:c0 + L], in1=m1.ap[:, 0:L], op=ALU.mult)
        psN = self.next_ps()
        self.mm(psN, psN.ap[:, 0:L], v_t, v_ap, Pb, Pb.ap[0:L, 0:L], start=True, stop=False, inc=True)
        self.mm(psN, psN.ap[:, 0:L], Cb, Cb.ap[:, :], qs, qs.ap[:, 0:L], start=False, stop=True)
        psD = self.next_ps()
        self.mm(psD, psD.ap[:, 0:L], self.ones_bf, self.ones_bf.ap[0:L, :], Pb, Pb.ap[0:L, 0:L], start=True, stop=False, inc=True)
        self.mm(psD, psD.ap[:, 0:L], nrb, nrb.ap[:, :], qs, qs.ap[:, 0:L], start=False, stop=True)
        self.A(m2, m2.ap[:, 0:L], psD, psD.ap[:, 0:L], AF.Abs)
        self.V(D_, "tensor_tensor", m2, [m2, emt], out=m2.ap[:, 0:L], in0=m2.ap[:, 0:L], in1=emt.ap[:, c0:c0 + L], op=ALU.max)
        self.V(D_, "reciprocal", m2, [m2], out=m2.ap[:, 0:L], in_=m2.ap[:, 0:L])
        self.V(D_, "tensor_tensor", hT, [psN, m2], out=hT.ap[:, c0:c0 + L], in0=psN.ap[:, 0:L], in1=m2.ap[:, 0:L], op=ALU.mult)
        self.V(D_, "tensor_tensor", cols, [M, cols], out=cols.ap[0:L, 2:3], in0=M.ap[0:L, c0 + L - 1:c0 + L], in1=acol, op=ALU.subtract)
        self.A(cols, cols.ap[0:L, 3:4], cols, cols.ap[0:L, 2:3], AF.Exp, scale=-1.0)
        self.A(cols, cols.ap[:, 4:5], M, M.ap[:, c0 + L - 1:c0 + L], AF.Exp, bias=Mp_ap, scale=-1.0, extra=[Mp_t])
        self.V(D_, "tensor_scalar", kw, [k_t, cols], out=kw.ap[0:L, :], in0=k_ap, scalar1=cols.ap[0:L, 3:4], scalar2=None, op0=ALU.mult)
        psC = self.next_ps()
        self.mm(psC, psC.ap[:, 0:128], kw, kw.ap[0:L, :], v_t, v_ap, start=True, stop=True)
        self.V(D_, "scalar_tensor_tensor", C, [C, cols, psC], out=C.ap[:, :], in0=C.ap[:, :], scalar=cols.ap[:, 4:5], in1=psC.ap[:, 0:128], op0=ALU.mult, op1=ALU.add)
        self.A(Cb, Cb.ap[:, :], C, C.ap[:, :], AF.Copy)
        psNn = self.next_ps()
        self.mm(psNn, psNn.ap[:, 0:128], kw, kw.ap[0:L, :], self.ones_bf, self.ones_bf.ap[0:L, :], start=True, stop=True)
        self.V(D_, "scalar_tensor_tensor", nr, [nr, cols, psNn], out=nr.ap[:, :], in0=nr.ap[:, :], scalar=cols.ap[:, 4:5], in1=psNn.ap[:, 0:128], op0=ALU.mult, op1=ALU.add)
        self.A(nrb, nrb.ap[:, :], nr, nr.ap[:, :], AF.Copy)

    def mlstm_sample(self, hd, ig_t, G_t, M_t, emt_t, hT_t, qT_t, kT_t, tkb, tvb, ig, G, M, emt, hT, qT, kT):
        D_ = self.dve
        outs = self.outs
        cs4 = self.cs4
        c = cs4.ap
        acol4, tcol4, wcol4, dec4, mend4, nnew4 = c[0:LS, 0:4], c[0:LS, 4:8], c[0:LS, 8:12], c[:, 12:16], c[:, 16:20], c[:, 20:24]
        W16 = self.m128[0]
        iw = self.m128[1]
        dn = self.m128[2]
        P16, qs, _ = self.b128
        Cs4, Cs4b, nr4b, kw4 = self.Cs4, self.Cs4b, self.nr4b, self.kw4
        S0 = TP
        sl = slice(S0, S0 + TS)
        m0c = self.svc("am", 0, 16).rearrange("p (s h) -> p s h", h=4)[:, :, hd]
        n0c = self.svc("an", 0, 16).rearrange("p (s h) -> p s h", h=4)[:, :, hd]
        self.dma(Cs4.ap[:, :, :], self.d_aC.rearrange("(s h) p d -> h p s d", h=4)[hd], writes=[Cs4], queue=self.act)
        self.A(Cs4b, Cs4b.ap[:, :, :], Cs4, Cs4.ap[:, :, :], AF.Copy)
        self.V(D_, "tensor_copy", nr4b, [self.sv], out=nr4b.ap[:, :, :], in_=n0c.unsqueeze(2).to_broadcast([128, NSQ, 128]))
        for sq in range(NSQ):
            cs = S0 + LS * sq
            self.V(D_, "scalar_tensor_tensor", [W16, cs4], [ig_t, self.ident], out=W16.ap[0:LS, 0:LS], in0=ig[0:LS, cs:cs + LS], scalar=1.0,
                   in1=self.ident.ap[0:LS, 0:LS], op0=ALU.mult, op1=ALU.mult, accum_out=c[0:LS, sq:sq + 1])
        Mv16 = M[0:LS, sl].rearrange("p (s l) -> p s l", l=LS)
        Wv = W16.ap[0:LS, 0:TS].rearrange("p (s l) -> p s l", l=LS)
        self.V(D_, "tensor_tensor", W16, [cs4, M_t], out=Wv, in0=acol4.unsqueeze(2).to_broadcast([LS, NSQ, LS]), in1=Mv16, op=ALU.subtract)
        self.V(D_, "tensor_tensor", W16, [W16, self.maskb], out=Wv, in0=Wv, in1=self.maskb.ap[0:LS, 0:LS].unsqueeze(1).to_broadcast([LS, NSQ, LS]), op=ALU.add)
        self.A(W16, W16.ap[0:LS, 0:TS], W16, W16.ap[0:LS, 0:TS], AF.Exp)
        psS = self.next_ps()
        for sq in range(NSQ):
            cs = S0 + LS * sq
            self.mm(psS, psS.ap[0:LS, sq * LS:(sq + 1) * LS], kT_t, kT[:, cs:cs + LS], qT_t, qT[:, cs:cs + LS], start=True, stop=True)
        self.V(D_, "tensor_tensor", P16, [psS, W16], out=P16.ap[0:LS, 0:TS], in0=psS.ap[0:LS, 0:TS], in1=W16.ap[0:LS, 0:TS], op=ALU.mult)
        Mv = M[:, sl].rearrange("p (s l) -> p s l", l=LS)
        iwv = iw.ap[:, 0:TS].rearrange("p (s l) -> p s l", l=LS)
        self.V(D_, "tensor_tensor", iw, [self.sv, M_t], out=iwv, in0=m0c.unsqueeze(2).to_broadcast([128, NSQ, LS]), in1=Mv, op=ALU.subtract)
        self.A(iw, iw.ap[:, 0:TS], iw, iw.ap[:, 0:TS], AF.Exp)
        self.V(D_, "tensor_tensor", qs, [qT_t, iw], out=qs.ap[:, 0:TS], in0=qT[:, sl], in1=iw.ap[:, 0:TS], op=ALU.mult)
        psN, psD = self.next_ps(), self.next_ps()
        for sq in range(NSQ):
            o = slice(sq * LS, (sq + 1) * LS)
            self.mm(psN, psN.ap[:, o], tvb, tvb.ap[0:LS, sq * 128:(sq + 1) * 128], P16, P16.ap[0:LS, o], start=True, stop=False, inc=True)
            self.mm(psN, psN.ap[:, o], Cs4b, Cs4b.ap[:, sq, :], qs, qs.ap[:, o], start=False, stop=True)
        for sq in range(NSQ):
            o = slice(sq * LS, (sq + 1) * LS)
            self.mm(psD, psD.ap[:, o], self.ones_bf, self.ones_bf.ap[0:LS, :], P16, P16.ap[0:LS, o], start=True, stop=False, inc=True)
            self.mm(psD, psD.ap[:, o], nr4b, nr4b.ap[:, sq, :], qs, qs.ap[:, o], start=False, stop=True)
        self.A(dn, dn.ap[:, 0:TS], psD, psD.ap[:, 0:TS], AF.Abs)
        self.V(D_, "tensor_tensor", dn, [dn, emt_t], out=dn.ap[:, 0:TS], in0=dn.ap[:, 0:TS], in1=emt[:, sl], op=ALU.max)
        self.A(dn, dn.ap[:, 0:TS], dn, dn.ap[:, 0:TS], AF.Ln)
        self.A(dn, dn.ap[:, 0:TS], dn, dn.ap[:, 0:TS], AF.Exp, scale=-1.0)
        self.V(D_, "tensor_tensor", hT_t, [psN, dn], out=hT[:, sl], in0=psN.ap[:, 0:TS], in1=dn.ap[:, 0:TS], op=ALU.mult)
        self.V(D_, "tensor_copy", cs4, [M_t], out=mend4, in_=Mv[:, :, LS - 1])
        self.V(D_, "tensor_tensor", cs4, [cs4], out=tcol4, in0=c[0:LS, 16:20], in1=acol4, op=ALU.subtract)
        self.A(cs4, wcol4, cs4, tcol4, AF.Exp, scale=-1.0)
        self.V(D_, "tensor_tensor", cs4, [self.sv, cs4], out=c[:, 24:28], in0=m0c, in1=mend4, op=ALU.subtract)
        self.A(cs4, dec4, cs4, c[:, 24:28], AF.Exp)
        self.V(D_, "tensor_tensor", kw4, [tkb, cs4], out=kw4.ap[0:LS, :].rearrange("p (s d) -> p s d", d=128), in0=tkb.ap[0:LS, 0:NSQ * 128].rearrange("p (s d) -> p s d", d=128),
               in1=wcol4.unsqueeze(2).to_broadcast([LS, NSQ, 128]), op=ALU.mult)
        psC = self.next_ps()
        psn = self.next_ps()
        for sq in range(NSQ):
            self.mm(psC, psC.ap[:, sq * 128:(sq + 1) * 128], kw4, kw4.ap[0:LS, sq * 128:(sq + 1) * 128], tvb, tvb.ap[0:LS, sq * 128:(sq + 1) * 128], start=True, stop=True)
        for sq in range(NSQ):
            self.mm(psn, psn.ap[:, sq:sq + 1], kw4, kw4.ap[0:LS, sq * 128:(sq + 1) * 128], self.ones_bf, self.ones_bf.ap[0:LS, 0:1], start=True, stop=True)
        for sq in range(NSQ):
            self.V(D_, "scalar_tensor_tensor", Cs4, [Cs4, cs4, psC], out=Cs4.ap[:, sq, :], in0=Cs4.ap[:, sq, :], scalar=c[:, 12 + sq:13 + sq], in1=psC.ap[:, sq * 128:(sq + 1) * 128],
                   op0=ALU.mult, op1=ALU.add)
        self.dma(self.o_saC.rearrange("(s h) p d -> h p s d", h=4)[hd], Cs4.ap[:, :, :], reads=[Cs4], queue=self.act)
        ov = outs.ap[:, 0:16].rearrange("p (s h) -> p s h", h=4)[:, :, hd]
        self.V(D_, "tensor_tensor", cs4, [self.sv, cs4], out=nnew4, in0=n0c, in1=dec4, op=ALU.mult)
        self.V(D_, "tensor_tensor", outs, [cs4, psn], out=ov, in0=nnew4, in1=psn.ap[:, 0:NSQ], op=ALU.add)
        mv = outs.ap[:, 16:32].rearrange("p (s h) -> p s h", h=4)[:, :, hd]
        self.V(D_, "tensor_tensor", outs, [cs4, G_t], out=mv, in0=mend4, in1=G[:, sl].rearrange("p (s l) -> p s l", l=LS)[:, :, LS - 1], op=ALU.subtract)

    def mlstm_bufs(self, hd):
        a = self.a
        if hd % 2 == 0:
            return a[8], a[9], a[10], a[11], self.tokb[0], self.tokb[1]
        return a[18], a[19], a[20], a[21], self.tokb[2], self.tokb[3]

    def mlstm_P(self, pi, hd):
        qT, kT, ktok, vtok, tkb, tvb = self.mlstm_bufs(hd)
        KS = 128 ** -0.5

        def pidx(c0):
            return 0 if c0 < 512 else (1 if c0 < 1024 else 2)
        w = self.get(("ab_in", "qa", hd))
        yield from self.proj_fm_g(w, pi, lambda ps, c0, c1: self.A(qT.parts[pidx(c0)], qT.ap[:, c0:c1], ps, ps.ap[:, 0:c1 - c0], AF.Copy))
        w = self.get(("ab_in", "ka", hd))
        yield from self.proj_fm_g(w, pi, lambda ps, c0, c1: self.A(kT.parts[pidx(c0)], kT.ap[:, c0:c1], ps, ps.ap[:, 0:c1 - c0], AF.Copy, scale=KS))
        yield from self.proj_tm_g(w, pi, lambda ps, g: self.A(ktok.parts[g], ktok.ap[:, g * 512:(g + 1) * 512], ps, ps.ap[:, 0:512], AF.Copy, scale=KS),
                                  lambda ps: self.A(tkb, tkb.ap[0:LS, 0:512], ps, ps.ap[0:LS, 0:512], AF.Copy, scale=KS))
        w = self.get(("ab_in", "va", hd))
        yield from self.proj_tm_g(w, pi, lambda ps, g: self.A(vtok.parts[g], vtok.ap[:, g * 512:(g + 1) * 512], ps, ps.ap[:, 0:512], AF.Copy),
                                  lambda ps: self.A(tvb, tvb.ap[0:LS, 0:512], ps, ps.ap[0:LS, 0:512], AF.Copy))

    def mlstm_C(self, pi, hd):
        D_, P = self.dve, self.pool
        yield
        T = TP + (TS if pi == 1 else 0)
        yield
        cbs = self.cbs(pi)
        yield
        H2 = ((0, 512), (512, 1024))
        yield
        F = self.F
        yield
        small = self.small
        yield
        ig, G, M, emt, tho, hT, W_ = F[0], F[1], F[2], F[3], F[4], F[5], F[6]
        yield
        qT, kT, ktok, vtok, tkb, tvb = self.mlstm_bufs(hd)
        yield
        sqh = self.a[12]
        yield
        P_all, qs_all, kw_all, Cb_all, nrb_all = self.a[13], self.a[14], self.a[15], self.a[16], self.a[17]
        yield
        KS = 128 ** -0.5
        yield
        ow = self.ones_w
        yield
        c8 = self.c8
        yield
        c8a = c8.ap
        yield
        m0 = self.m128[0]
        yield
        Cst, Cbst, nrst, nrbst = self.C[hd], self.Cb[hd], self.nr[hd], self.nrb[hd]
        yield

        def pidx(c0):
            return 0 if c0 < 512 else (1 if c0 < 1024 else 2)
        for gi, gcol in enumerate((hd, 4 + hd)):
            wr_t = self.wrep if gi == 0 else sqh
            wr_ap = wr_t.ap[:, 0:1024]
            self.V(D_, "tensor_copy", wr_t, [self.gw], out=wr_ap.rearrange("p (k c) -> p k c", c=128),
                   in_=self.gw.ap[:, 0:64].rearrange("p (k g) -> p k g", g=8)[:, :, gcol].unsqueeze(2).to_broadcast([128, 8, 128]))
            yield

            def ev(ps, c0, c1, gi=gi):
                g = pidx(c0)
                if gi == 0:
                    self.A(ig.parts[g], ig.ap[:, c0:c1], ps, ps.ap[:, 0:c1 - c0], AF.Identity, bias=self.pvc("gbias", hd), extra=[self.pv])
                else:
                    self.A(G.parts[g], G.ap[:, c0:c1], ps, ps.ap[:, 0:c1 - c0], AF.Exp, bias=small.ap[:, 34 + hd:35 + hd], scale=-1.0, extra=[small])
            self.proj_fm(wr_t, pi, ev)
            yield
        for (c0, c1) in cbs:
            g = pidx(c0)
            yield
            self.A(G.parts[g], G.ap[:, c0:c1], G.parts[g], G.ap[:, c0:c1], AF.Ln, bias=self.cst.ap[:, 1:2], extra=[self.cst])
            yield
        w = self.get(("ab_in", "oa", hd))
        yield
        self.proj_fm(w, pi, lambda ps, c0, c1: self.A(tho.parts[pidx(c0)], tho.ap[:, c0:c1], ps, ps.ap[:, 0:c1 - c0], AF.Tanh, scale=0.5))
        yield
        yield "P_OK"
        self.V(D_, "tensor_copy", self.cols, [self.Ml], out=self.cols.ap[:, 0:1], in_=self.Ml.ap[:, hd:hd + 1])
        yield
        for g, (c0, c1) in enumerate(H2):
            gi_t, gi_ap = (self.Gl, self.Gl.ap[:, hd:hd + 1]) if g == 0 else (G.parts[0], G.ap[:, 511:512])
            yield
            mi_t, mi_ap = (self.Ml, self.Ml.ap[:, hd:hd + 1]) if g == 0 else (M.parts[0], M.ap[:, 511:512])
            yield
            self.V(D_, "tensor_tensor_scan", G.parts[g], [ow, G.parts[g], gi_t], out=G.ap[:, c0:c1], data0=ow.ap[:, 0:512], data1=G.ap[:, c0:c1],
                   initial=gi_ap, op0=ALU.mult, op1=ALU.add)
            yield
            self.V(D_, "tensor_tensor", ig.parts[g], [ig.parts[g], G.parts[g]], out=ig.ap[:, c0:c1], in0=ig.ap[:, c0:c1], in1=G.ap[:, c0:c1], op=ALU.add)
            yield
            self.V(D_, "tensor_tensor_scan", M.parts[g], [ow, ig.parts[g], mi_t], out=M.ap[:, c0:c1], data0=ow.ap[:, 0:512], data1=ig.ap[:, c0:c1],
                   initial=mi_ap, op0=ALU.mult, op1=ALU.max)
            yield
            self.V(D_, "tensor_tensor", emt.parts[g], [G.parts[g], M.parts[g]], out=emt.ap[:, c0:c1], in0=G.ap[:, c0:c1], in1=M.ap[:, c0:c1], op=ALU.subtract)
            yield
            self.A(emt.parts[g], emt.ap[:, c0:c1], emt.parts[g], emt.ap[:, c0:c1], AF.Exp)
            yield
        if pi == 1:
            g = 2
            yield
            for sq in range(NSQ):
                cs = TP + LS * sq
                yield
                self.V(D_, "tensor_tensor_scan", G.parts[g], [ow, G.parts[g]], out=G.ap[:, cs:cs + LS], data0=ow.ap[:, 0:LS], data1=G.ap[:, cs:cs + LS],
                       initial=0.0, op0=ALU.mult, op1=ALU.add)
                yield
            self.V(D_, "tensor_tensor", ig.parts[g], [ig.parts[g], G.parts[g]], out=ig.ap[:, TP:T], in0=ig.ap[:, TP:T], in1=G.ap[:, TP:T], op=ALU.add)
            yield
            for sq in range(NSQ):
                cs = TP + LS * sq
                yield
                self.V(D_, "tensor_tensor_scan", M.parts[g], [ow, ig.parts[g], self.sv], out=M.ap[:, cs:cs + LS], data0=ow.ap[:, 0:LS], data1=ig.ap[:, cs:cs + LS],
                       initial=self.svc("am", sq * 4 + hd), op0=ALU.mult, op1=ALU.max)
                yield
            self.V(D_, "tensor_tensor", emt.parts[g], [G.parts[g], M.parts[g]], out=emt.ap[:, TP:T], in0=G.ap[:, TP:T], in1=M.ap[:, TP:T], op=ALU.subtract)
            yield
            self.A(emt.parts[g], emt.ap[:, TP:T], emt.parts[g], emt.ap[:, TP:T], AF.Exp)
            yield
        self.V(D_, "tensor_copy", self.Gl, [G.parts[1]], out=self.Gl.ap[:, hd:hd + 1], in_=G.ap[:, TP - 1:TP])
        yield
        Mv = M.ap[:, 0:TP].rearrange("p (t l) -> p t l", l=128)
        yield
        Wv = W_.ap[:, 0:TP].rearrange("p (t l) -> p t l", l=128)
        yield
        for g in range(2):
            cg = c8.parts[g]
            yield
            s4 = slice(4 * g, 4 * g + 4)
            yield
            for tt_ in range(4 * g, 4 * g + 4):
                self.V(D_, "scalar_tensor_tensor", [m0, cg], [ig.parts[g], self.ident], out=m0.ap[:, :], in0=ig.ap[:, tt_ * 128:(tt_ + 1) * 128], scalar=1.0,
                       in1=self.ident.ap[:, :], op0=ALU.mult, op1=ALU.mult, accum_out=c8a[:, tt_:tt_ + 1])
                yield
            if g == 0:
                self.V(D_, "tensor_copy", cg, [self.cols], out=c8a[:, 8:9], in_=self.cols.ap[:, 0:1])
                yield
                self.V(D_, "tensor_copy", cg, [M.parts[0]], out=c8a[:, 9:12], in_=Mv[:, 0:3, 127])
                yield
            else:
                self.V(D_, "tensor_copy", cg, [M.parts[0], M.parts[1]], out=c8a[:, 12:16], in_=Mv[:, 3:7, 127])
                yield
            self.V(D_, "tensor_copy", cg, [M.parts[g]], out=c8a[:, 16 + 4 * g:20 + 4 * g], in_=Mv[:, s4, 127])
            yield
            self.V(D_, "tensor_tensor", cg, [cg], out=c8a[:, 48 + 4 * g:52 + 4 * g], in0=c8a[:, 16 + 4 * g:20 + 4 * g], in1=c8a[:, 4 * g:4 * g + 4], op=ALU.subtract)
            yield
            self.A(cg, c8a[:, 24 + 4 * g:28 + 4 * g], cg, c8a[:, 48 + 4 * g:52 + 4 * g], AF.Exp, scale=-1.0)
            yield
            self.V(D_, "tensor_tensor", cg, [cg], out=c8a[:, 48 + 4 * g:52 + 4 * g], in0=c8a[:, 8 + 4 * g:12 + 4 * g], in1=c8a[:, 16 + 4 * g:20 + 4 * g], op=ALU.subtract)
            yield
            self.A(cg, c8a[:, 32 + 4 * g:36 + 4 * g], cg, c8a[:, 48 + 4 * g:52 + 4 * g], AF.Exp)
            yield
            self.V(D_, "tensor_tensor", W_.parts[g], [cg, M.parts[g]], out=Wv[:, s4, :], in0=c8a[:, s4].unsqueeze(2).to_broadcast([128, 4, 128]), in1=Mv[:, s4, :], op=ALU.subtract)
            yield
            self.V(D_, "tensor_tensor", W_.parts[g], [W_.parts[g], self.maskb], out=Wv[:, s4, :], in0=Wv[:, s4, :],
                   in1=self.maskb.ap[:, :].unsqueeze(1).to_broadcast([128, 4, 128]), op=ALU.add)
            yield
            self.A(W_.parts[g], W_.ap[:, g * 512:(g + 1) * 512], W_.parts[g], W_.ap[:, g * 512:(g + 1) * 512], AF.Exp)
            yield
        CN = G
        yield
        CNtd = CN.ap[:, 0:TP].rearrange("p (d t) -> p t d", t=8)
        yield
        CNp = [CN.parts[0], CN.parts[1]]
        yield
        for g, (c0h, c1h) in enumerate(H2):
            cg = c8.parts[g]
            yield
            s4 = slice(4 * g, 4 * g + 4)
            yield
            hs = slice(c0h, c1h)
            yield
            ps = self.next_ps()
            yield
            for t4 in range(4):
                c0 = (g * 4 + t4) * 128
                yield
                self.mm(ps, ps.ap[:, t4 * 128:(t4 + 1) * 128], kT.parts[g], kT.ap[:, c0:c0 + 128], qT.parts[g], qT.ap[:, c0:c0 + 128], start=True, stop=True)
                yield
            self.V(D_, "tensor_tensor", P_all.parts[g], [ps, W_.parts[g]], out=P_all.ap[:, hs], in0=ps.ap[:, 0:512], in1=W_.ap[:, hs], op=ALU.mult)
            yield
            self.V(D_, "tensor_tensor", W_.parts[g], [cg, M.parts[g]], out=Wv[:, s4, :], in0=c8a[:, 8 + 4 * g:12 + 4 * g].unsqueeze(2).to_broadcast([128, 4, 128]),
                   in1=Mv[:, s4, :], op=ALU.subtract)
            yield
            self.A(W_.parts[g], W_.ap[:, hs], W_.parts[g], W_.ap[:, hs], AF.Exp)
            yield
            self.V(D_, "tensor_tensor", qs_all.parts[g], [qT.parts[g], W_.parts[g]], out=qs_all.ap[:, hs], in0=qT.ap[:, hs], in1=W_.ap[:, hs], op=ALU.mult)
            yield
            self.V(D_, "tensor_tensor", kw_all.parts[g], [ktok.parts[g], cg], out=kw_all.ap[:, hs].rearrange("p (t d) -> p t d", d=128),
                   in0=ktok.ap[:, hs].rearrange("p (t d) -> p t d", d=128), in1=c8a[:, 24 + 4 * g:28 + 4 * g].unsqueeze(2).to_broadcast([128, 4, 128]), op=ALU.mult)
            yield
            ps = self.next_ps()
            yield
            for t4 in range(4):
                c0 = (g * 4 + t4) * 128
                yield
                self.mm(ps, ps.ap[:, t4 * 128:(t4 + 1) * 128], kw_all.parts[g], kw_all.ap[:, c0:c0 + 128], vtok.parts[g], vtok.ap[:, c0:c0 + 128], start=True, stop=True)
                yield
            self.op(self.act, lambda ps=ps, g=g: self.nc.scalar.activation(out=CNtd[:, g * 4:(g + 1) * 4, :], in_=ps.ap[:, 0:512].rearrange("p (t d) -> p t d", d=128), func=AF.Copy),
                    reads=[ps], writes=CNp)
            yield
            psn = self.next_ps()
            yield
            for t4 in range(4):
                c0 = (g * 4 + t4) * 128
                yield
                self.mm(psn, psn.ap[:, t4:t4 + 1], kw_all.parts[g], kw_all.ap[:, c0:c0 + 128], self.ones_bf, self.ones_bf.ap[:, 0:1], start=True, stop=True)
                yield
            self.V(D_, "tensor_copy", cg, [psn], out=c8a[:, 40 + 4 * g:44 + 4 * g], in_=psn.ap[:, 0:4])
            yield
        c80, c81 = c8.parts[0], c8.parts[1]
        yield
        self.V(D_, "scalar_tensor_tensor", CNp, [Cst, c80] + CNp, out=CNtd[:, 0, :], in0=Cst.ap[:, :], scalar=c8a[:, 32:33], in1=CNtd[:, 0, :], op0=ALU.mult, op1=ALU.add)
        yield
        self.V(D_, "scalar_tensor_tensor", c80, [nrst, c80], out=c8a[:, 40:41], in0=nrst.ap[:, 0:1], scalar=c8a[:, 32:33], in1=c8a[:, 40:41], op0=ALU.mult, op1=ALU.add)
        yield
        self.V(D_, "memset", c80, [], ap=c8a[:, 32:33], constant=0.0)
        yield
        Wp = [W_.parts[0], W_.parts[1]]
        yield
        self.V(D_, "tensor_copy", Wp, [c80, c81], out=W_.ap[:, 0:TP].rearrange("p (d t) -> p d t", t=8), in_=c8a[:, 32:40].unsqueeze(1).to_broadcast([128, 128, 8]))
        yield
        self.V(D_, "tensor_tensor_scan", CNp, CNp + Wp, out=CN.ap[:, 0:TP], data0=W_.ap[:, 0:TP], data1=CN.ap[:, 0:TP], initial=0.0, op0=ALU.mult, op1=ALU.add)
        yield
        self.V(D_, "tensor_tensor_scan", [c80, c81], [c80, c81], out=c8a[:, 40:48], data0=c8a[:, 32:40], data1=c8a[:, 40:48], initial=0.0, op0=ALU.mult, op1=ALU.add)
        yield
        Cbp = [Cb_all.parts[0], Cb_all.parts[1]]
        yield
        self.op(self.act, lambda: self.nc.scalar.activation(out=Cb_all.ap[:, 0:TP].rearrange("p (t d) -> p t d", d=128), in_=CNtd, func=AF.Copy), reads=CNp, writes=Cbp)
        yield
        nbp = [nrb_all.parts[0], nrb_all.parts[1]]
        yield
        self.V(D_, "tensor_copy", nbp, [c80, c81], out=nrb_all.ap[:, 0:TP].rearrange("p (t d) -> p t d", d=128), in_=c8a[:, 40:48].unsqueeze(2).to_broadcast([128, 8, 128]))
        yield
        for g, (c0h, c1h) in enumerate(H2):
            hs = slice(c0h, c1h)
            yield
            psN, psD = self.next_ps(), self.next_ps()
            yield
            for (psX, l_first_t, l_first, l_all) in ((psN, None, None, Cb_all), (psD, self.ones_bf, self.ones_bf.ap[:, :], nrb_all)):
                for t4 in range(4):
                    tt_ = g * 4 + t4
                    yield
                    c0 = tt_ * 128
                    yield
                    o = psX.ap[:, t4 * 128:(t4 + 1) * 128]
                    yield
                    if psX is psN:
                        self.mm(psX, o, vtok.parts[g], vtok.ap[:, c0:c0 + 128], P_all.parts[g], P_all.ap[:, c0:c0 + 128], start=True, stop=False, inc=True)
                        yield
                        st_t, st_ap = (Cbst, Cbst.ap[:, :]) if tt_ == 0 else (Cbp[(tt_ - 1) // 4], Cb_all.ap[:, c0 - 128:c0])
                        yield
                    else:
                        self.mm(psX, o, self.ones_bf, self.ones_bf.ap[:, :], P_all.parts[g], P_all.ap[:, c0:c0 + 128], start=True, stop=False, inc=True)
                        yield
                        st_t, st_ap = (nrbst, nrbst.ap[:, :]) if tt_ == 0 else (nbp[(tt_ - 1) // 4], nrb_all.ap[:, c0 - 128:c0])
                        yield
                    self.mm(psX, o, st_t, st_ap, qs_all.parts[g], qs_all.ap[:, c0:c0 + 128], start=False, stop=True)
                    yield
            Wg = W_.parts[g]
            yield
            self.A(Wg, W_.ap[:, hs], psD, psD.ap[:, 0:512], AF.Abs)
            yield
            self.V(D_, "tensor_tensor", Wg, [Wg, emt.parts[g]], out=W_.ap[:, hs], in0=W_.ap[:, hs], in1=emt.ap[:, hs], op=ALU.max)
            yield
            self.A(Wg, W_.ap[:, hs], Wg, W_.ap[:, hs], AF.Ln)
            yield
            self.A(Wg, W_.ap[:, hs], Wg, W_.ap[:, hs], AF.Exp, scale=-1.0)
            yield
            self.V(D_, "tensor_tensor", hT.parts[g], [psN, Wg], out=hT.ap[:, hs], in0=psN.ap[:, 0:512], in1=W_.ap[:, hs], op=ALU.mult)
            yield
        self.V(D_, "tensor_copy", Cst, CNp, out=Cst.ap[:, :], in_=CNtd[:, 7, :])
        yield
        self.op(self.act, lambda: self.nc.scalar.activation(out=Cbst.ap[:, :], in_=CNtd[:, 7, :], func=AF.Copy), reads=CNp, writes=[Cbst])
        yield
        self.V(D_, "tensor_copy", nrst, [c81], out=nrst.ap[:, :], in_=c8a[:, 47:48].to_broadcast([128, 128]))
        yield
        self.V(D_, "tensor_copy", nrbst, [c81], out=nrbst.ap[:, :], in_=c8a[:, 47:48].to_broadcast([128, 128]))
        yield
        self.V(D_, "tensor_copy", self.Ml, [M.parts[1]], out=self.Ml.ap[:, hd:hd + 1], in_=M.ap[:, TP - 1:TP])
        yield
        if pi == 1:
            outs = self.outs
            yield
            self.V(D_, "tensor_copy", outs, [self.nr[hd]], out=outs.ap[:, 32 + hd:33 + hd], in_=self.nr[hd].ap[:, 0:1])
            yield
            self.V(D_, "tensor_tensor", outs, [self.Ml, self.Gl], out=outs.ap[:, 36 + hd:37 + hd], in0=self.Ml.ap[:, hd:hd + 1], in1=self.Gl.ap[:, hd:hd + 1], op=ALU.subtract)
            yield
            self.dma(self.o_paC[hd], self.C[hd].ap[:, :], reads=[self.C[hd]], queue=self.act)
            yield
            self.mlstm_sample(hd, ig.parts[2], G.parts[2], M.parts[2], emt.parts[2], hT.parts[2], qT.parts[2], kT.parts[2], tkb, tvb,
                              ig.ap, G.ap, M.ap, emt.ap, hT.ap, qT.ap, kT.ap)
            yield
        lnt = ig
        yield
        for (c0, c1) in cbs:
            g = pidx(c0)
            yield
            cs_ = slice(c0, c1)
            yield
            self.A(sqh.parts[g], sqh.ap[:, cs_], hT.parts[g], hT.ap[:, cs_], AF.Square)
            yield
            ps = self.next_ps()
            yield
            self.mm(ps, ps.ap[:, 0:c1 - c0], self.ones_bf, self.ones_bf.ap[:, :], sqh.parts[g], sqh.ap[:, cs_], start=True, stop=True)
            yield
            self.A(lnt.parts[g], lnt.ap[:, cs_], ps, ps.ap[:, 0:c1 - c0], AF.Ln, bias=self.cst.ap[:, 0:1], scale=1.0 / 128, extra=[self.cst])
            yield
            self.A(lnt.parts[g], lnt.ap[:, cs_], lnt.parts[g], lnt.ap[:, cs_], AF.Exp, scale=-0.5)
            yield
            self.V(D_, "tensor_tensor", hT.parts[g], [hT.parts[g], lnt.parts[g]], out=hT.ap[:, cs_], in0=hT.ap[:, cs_], in1=lnt.ap[:, cs_], op=ALU.mult)
            yield
            self.V(D_, "scalar_tensor_tensor", hT.parts[g], [tho.parts[g], hT.parts[g]], out=hT.ap[:, cs_], in0=tho.ap[:, cs_], scalar=1.0, in1=hT.ap[:, cs_], op0=ALU.add, op1=ALU.mult)
            yield
            self.V(D_, "tensor_scalar", self.a[hd].parts[g], [hT.parts[g], small], out=self.a[hd].ap[:, cs_], in0=hT.ap[:, cs_], scalar1=small.ap[:, 24 + hd:25 + hd], scalar2=None, op0=ALU.mult)
            yield

    def proj_fm_g(self, w, pi, evac):
        for (c0, c1) in self.cbs(pi):
            ps = self.next_ps()
            for k in range(NCH):
                self.mm(ps, ps.ap[:, 0:c1 - c0], w, w.ap[:, k * 128:(k + 1) * 128], self.h[k].parts[0 if c0 < 512 else (1 if c0 < 1024 else 2)], self.h[k].ap[:, c0:c1], start=(k == 0), stop=(k == 7))
            evac(ps, c0, c1)
            yield

    def proj_tm_g(self, w, pi, evac, evac_s):
        for g in range(2):
            ps = self.next_ps()
            for t4 in range(4):
                tt_ = g * 4 + t4
                for k in range(NCH):
                    self.mm(ps, ps.ap[:, t4 * 128:(t4 + 1) * 128], self.h[k].parts[tt_ // 4], self.h[k].ap[:, tt_ * 128:(tt_ + 1) * 128], w, w.ap[:, k * 128:(k + 1) * 128],
                            start=(k == 0), stop=(k == 7))
                if t4 == 1:
                    yield
            evac(ps, g)
            yield
        if pi == 1:
            ps = self.next_ps()
            for sq in range(NSQ):
                cs = TP + LS * sq
                for k in range(NCH):
                    self.mm(ps, ps.ap[0:LS, sq * 128:(sq + 1) * 128], self.h[k].parts[2], self.h[k].ap[:, cs:cs + LS], w, w.ap[:, k * 128:(k + 1) * 128],
                            start=(k == 0), stop=(k == 7))
            evac_s(ps)
            yield

    def attn_bufs(self, hp):
        a = self.a
        if hp % 2 == 0:
            return a[8], a[9], a[10], self.tokb[0]
        return a[18], a[19], a[20], self.tokb[1]

    def attn_pro(self, pi, hp):
        D_ = self.dve
        T = TP + (TS if pi == 1 else 0)
        cbs = self.cbs(pi)
        F = self.F
        small = self.small
        Fq, Fk, rs = F[0], F[1], F[2]
        sqb = self.a[12]
        qn, kn, vcur, tvb = self.attn_bufs(hp)

        def qk_norm(name, raw):
            w = self.get(("ab_in", name, hp))

            def ev(ps, c0, c1):
                self.A(raw, raw.ap[:, c0:c1], ps, ps.ap[:, 0:c1 - c0], AF.Copy)
                self.A(sqb, sqb.ap[:, c0:c1], ps, ps.ap[:, 0:c1 - c0], AF.Square)
            yield from self.proj_fm_g(w, pi, ev)
            for (c0, c1) in cbs:
                ps = self.next_ps()
                self.mm(ps, ps.ap[:, 0:c1 - c0], self.blk_bf, self.blk_bf.ap[:, :], sqb, sqb.ap[:, c0:c1], start=True, stop=True)
                self.A(rs, rs.ap[:, c0:c1], ps, ps.ap[:, 0:c1 - c0], AF.Ln, bias=self.cst.ap[:, 0:1], scale=1.0 / 64, extra=[self.cst])
            yield
            self.A(rs, rs.ap[:, 0:T], rs, rs.ap[:, 0:T], AF.Exp, scale=-0.5)
            yield
        yield from qk_norm("qb", Fq)
        self.V(D_, "scalar_tensor_tensor", qn, [Fq, small, rs], out=qn.ap[:, 0:T], in0=Fq.ap[:, 0:T], scalar=small.ap[:, 28:29], in1=rs.ap[:, 0:T], op0=ALU.mult, op1=ALU.mult)
        yield
        yield from qk_norm("kb", Fk)
        self.V(D_, "scalar_tensor_tensor", Fk, [Fk, self.pv, rs], out=Fk.ap[:, 0:T], in0=Fk.ap[:, 0:T], scalar=self.pvc("gk"), in1=rs.ap[:, 0:T], op0=ALU.mult, op1=ALU.mult)
        yield
        self.A(kn, kn.ap[:, 0:T], Fk, Fk.ap[:, 0:T], AF.Copy)
        if pi == 1:
            self.dma(self.o_pbk[:, hp, :], Fk.ap[:, 512:1024], reads=[Fk], queue=self.act)
            self.dma(self.o_sbk[:, hp, :], Fk.ap[:, TP:T], reads=[Fk], queue=self.act)
        yield
        w = self.get(("ab_in", "vb", hp))

        def ev_v(ps, g):
            self.A(vcur, vcur.ap[:, g * 512:(g + 1) * 512], ps, ps.ap[:, 0:512], AF.Copy)
            if pi == 1 and g == 1:
                vf = F[5]
                self.A(vf, vf.ap[:, 0:512], ps, ps.ap[:, 0:512], AF.Copy)
                self.dma(self.o_pbv[:, :, hp * 128:(hp + 1) * 128].rearrange("t p f -> p t f"), vf.ap[:, 0:512].rearrange("p (t f) -> p t f", f=128), reads=[vf], queue=self.act)

        def ev_vs(ps):
            self.A(tvb, tvb.ap[0:LS, 0:512], ps, ps.ap[0:LS, 0:512], AF.Copy)
            self.A(self.tokf, self.tokf.ap[0:LS, 0:512], ps, ps.ap[0:LS, 0:512], AF.Copy)
            self.dma(self.o_sbv[:, :, hp * 128:(hp + 1) * 128].rearrange("s t f -> t s f"), self.tokf.ap[0:LS, 0:512].rearrange("p (s f) -> p s f", f=128), reads=[self.tokf], queue=self.act)
        yield from self.proj_tm_g(w, pi, ev_v, ev_vs)

    def attn_loop(self, pi, hp):
        D_, P = self.dve, self.pool
        F = self.F
        qn, kn, vcur, tvb = self.attn_bufs(hp)
        tmpSs = (F[3], F[4], F[6])
        Pbfs = (self.a[11], self.a[13], self.a[14])
        bT = self.biasT
        mix = self.a[4 + hp]
        self.dma(bT.ap[:, :, :], self.d_bias[hp], writes=[bT], queue=self.act)
        self.V(P, "memset", bT, [], ap=bT.ap[64:128, :, 512:576], constant=NEG)
        self.V(P, "memset", bT, [], ap=bT.ap[0:64, :, 64:128], constant=NEG)
        iters = [(qt, hh) for qt in range(8) for hh in range(2)]

        def srcs(qt, o):
            aq = 8 * pi + qt
            ka = aq - 4 + o
            if ka >= 8 * pi:
                c = (ka - 8 * pi) * 128
                return kn, kn.ap[:, c:c + 128], vcur, vcur.ap[:, c:c + 128]
            c = (ka - 4) * 128
            return self.kband[hp], self.kband[hp].ap[:, c:c + 128], self.vband, self.vband.ap[:, ka - 4, hp * 128:(hp + 1) * 128]

        def stageA(i):
            qt, hh = iters[i]
            r0 = 64 * hh
            offs = [o for o in range(5) if 8 * pi + qt - 4 + o >= 0]
            tmpS, Pbf = tmpSs[i % 3], Pbfs[i % 3]
            t0, t1, pS = self.next_ps2()
            for o in offs:
                k_t, k_ap, _, _ = srcs(qt, o)
                self.mm(t0 if o < 4 else t1, pS[:, o * 128:(o + 1) * 128], k_t, k_ap[r0:r0 + 64, :], qn, qn.ap[r0:r0 + 64, qt * 128:(qt + 1) * 128], start=True, stop=True)
            n0, n1 = offs[0] * 128, 640
            self.V(D_, "tensor_tensor", tmpS, [t0, t1, bT], out=tmpS.ap[:, n0:n1], in0=pS[:, n0:n1], in1=bT.ap[:, hh, n0:n1], op=ALU.add)
            self.A(Pbf, Pbf.ap[:, n0:n1], tmpS, tmpS.ap[:, n0:n1], AF.Exp)
        pend = {}

        def stageB1(i):
            qt, hh = iters[i]
            r0 = 64 * hh
            offs = [o for o in range(5) if 8 * pi + qt - 4 + o >= 0]
            Pbf = Pbfs[i % 3]
            psB = self.next_ps()
            for idx, o in enumerate(offs):
                _, _, v_t, v_ap = srcs(qt, o)
                self.mm(psB, psB.ap[:, 0:128], v_t, v_ap, Pbf, Pbf.ap[:, o * 128:(o + 1) * 128], start=(idx == 0), stop=(idx == len(offs) - 1))
            for idx, o in enumerate(offs):
                self.mm(psB, psB.ap[:, 128:256], self.ones_bf, self.ones_bf.ap[:, :], Pbf, Pbf.ap[:, o * 128:(o + 1) * 128], start=(idx == 0), stop=(idx == len(offs) - 1))
            rc = self.m128[2 + hh]
            self.A(rc, rc.ap[r0:r0 + 64, :], psB, psB.ap[r0:r0 + 64, 128:256], AF.Ln)
            self.A(rc, rc.ap[r0:r0 + 64, :], rc, rc.ap[r0:r0 + 64, :], AF.Exp, scale=-1.0)
            pend[i] = psB

        def stageB2(i):
            qt, hh = iters[i]
            r0 = 64 * hh
            psB = pend.pop(i)
            rc = self.m128[2 + hh]
            self.V(D_, "tensor_tensor", mix, [psB, rc], out=mix.ap[r0:r0 + 64, qt * 128:(qt + 1) * 128], in0=psB.ap[r0:r0 + 64, 0:128], in1=rc.ap[r0:r0 + 64, :], op=ALU.mult)
        stageA(0)
        stageA(1)
        yield
        for i in range(len(iters)):
            if i + 2 < len(iters):
                stageA(i + 2)
            stageB1(i)
            if i >= 1:
                stageB2(i - 1)
            yield
        stageB2(len(iters) - 1)

    def attn_sample(self, pi, hp):
        D_, P = self.dve, self.pool
        F = self.F
        qn, kn, vcur, tvb = self.attn_bufs(hp)
        tmpSs = (F[3], F[4], F[6])
        Pbfs = (self.a[11], self.a[13], self.a[14])
        bT = self.biasT
        mix = self.a[4 + hp]
        if pi == 1:
            tS4 = (self.m128[0], self.m128[1], F[3], F[4])
            Pb4 = (self.b128[0], self.b128[1], self.b128[2], self.a[11])

            def sA(sq):
                cs = TP + LS * sq
                kc = self.get(("kc", sq, hp))
                for hh in range(2):
                    r0 = 64 * hh
                    tmpS, Pbf = tS4[2 * (sq % 2) + hh], Pb4[2 * (sq % 2) + hh]
                    psS = self.next_ps()
                    qa = qn.ap[r0:r0 + 64, cs:cs + LS]
                    for o in range(4):
                        self.mm(psS, psS.ap[:, o * LS:(o + 1) * LS], kc, kc.ap[r0:r0 + 64, o * 128:(o + 1) * 128], qn, qa, start=True, stop=True)
                    self.mm(psS, psS.ap[0:LS, 64:64 + LS], kn, kn.ap[r0:r0 + 64, cs:cs + LS], qn, qa, start=True, stop=True)
                    self.V(D_, "tensor_tensor", tmpS, [psS, bT], out=tmpS.ap[:, 0:64].rearrange("p (k i) -> p k i", i=LS),
                           in0=psS.ap[:, 0:64].rearrange("p (k i) -> p k i", i=LS),
                           in1=bT.ap[:, hh, 0:512].rearrange("p (k i) -> p k i", i=128)[:, :, 0:LS], op=ALU.add)
                    self.V(D_, "tensor_tensor", tmpS, [psS, bT], out=tmpS.ap[0:LS, 64:64 + LS], in0=psS.ap[0:LS, 64:64 + LS], in1=bT.ap[0:LS, hh, 512:512 + LS], op=ALU.add)
                    self.A(Pbf, Pbf.ap[:, 0:64], tmpS, tmpS.ap[:, 0:64], AF.Exp)
                    self.A(Pbf, Pbf.ap[0:LS, 64:64 + LS], tmpS, tmpS.ap[0:LS, 64:64 + LS], AF.Exp)

            def sB(sq):
                cs = TP + LS * sq
                vc = self.get(("vc", sq, hp))
                for hh in range(2):
                    r0 = 64 * hh
                    Pbf = Pb4[2 * (sq % 2) + hh]
                    psO = self.next_ps()
                    for o in range(4):
                        self.mm(psO, psO.ap[:, 0:LS], vc, vc.ap[:, o * 128:(o + 1) * 128], Pbf, Pbf.ap[:, o * LS:(o + 1) * LS], start=(o == 0), stop=False, inc=(o == 3))
                    self.mm(psO, psO.ap[:, 0:LS], tvb, tvb.ap[0:LS, sq * 128:(sq + 1) * 128], Pbf, Pbf.ap[0:LS, 64:64 + LS], start=False, stop=True)
                    for o in range(4):
                        self.mm(psO, psO.ap[:, 128:128 + LS], self.ones_bf, self.ones_bf.ap[:, :], Pbf, Pbf.ap[:, o * LS:(o + 1) * LS], start=(o == 0), stop=False, inc=(o == 3))
                    self.mm(psO, psO.ap[:, 128:128 + LS], self.ones_bf, self.ones_bf.ap[0:LS, :], Pbf, Pbf.ap[0:LS, 64:64 + LS], start=False, stop=True)
                    rc = self.m128[2 + hh]
                    self.A(rc, rc.ap[r0:r0 + 64, 0:LS], psO, psO.ap[r0:r0 + 64, 128:128 + LS], AF.Ln)
                    self.A(rc, rc.ap[r0:r0 + 64, 0:LS], rc, rc.ap[r0:r0 + 64, 0:LS], AF.Exp, scale=-1.0)
                    self.V(D_, "tensor_tensor", mix, [psO, rc], out=mix.ap[r0:r0 + 64, cs:cs + LS], in0=psO.ap[r0:r0 + 64, 0:LS], in1=rc.ap[r0:r0 + 64, 0:LS], op=ALU.mult)
            sA(0)
            for sq in range(NSQ):
                if sq + 1 < NSQ:
                    sA(sq + 1)
                sB(sq)
        if pi == 0:
            self.V(P, "tensor_copy", self.kband[hp], [kn], out=self.kband[hp].ap[:, :], in_=kn.ap[:, 512:1024])
            self.V(P, "tensor_copy", self.vband, [vcur], out=self.vband.ap[:, :, hp * 128:(hp + 1) * 128], in_=vcur.ap[:, 512:1024].rearrange("p (t f) -> p t f", f=128))

    def final_outputs(self):
        q = self.act
        o = self.outs
        self.dma(self.o_pan, o.ap[:, 32:36], reads=[o], queue=q)
        self.dma(self.o_pam, o.ap[0:1, 36:40], reads=[o], queue=q)
        self.dma(self.o_san, o.ap[:, 0:16], reads=[o], queue=q)
        self.dma(self.o_sam, o.ap[0:1, 16:32], reads=[o], queue=q)
        self.dma(self.o_pconv, self.convc.ap, reads=[self.convc], queue=q)
        self.dma(self.o_pch, self.hc.ap, reads=[self.hc], queue=q)
        self.dma(self.o_sconv, self.sconv.ap, reads=[self.sconv], queue=q)
        self.dma(self.o_sch, self.sch.ap, reads=[self.sch], queue=q)

    def build(self):
        self.build_body()
        return self.nc

    def build_body(self):
        stop = DEBUG.get("stop")
        self.setup()
        done = False
        for pi in range(2):
            self.load_x(pi)
            if stop == "setup":
                done = True
                self.store_x(pi)
                continue
            if pi == 0:
                self.ada_pending = [(0, c) for c in range(4, 18)] + [(1, c) for c in range(18)]
                for cbk in range(4):
                    self.ada_block(0, cbk)
                self.ada_derive(0, (0,), gate=False)
            if stop == "ada":
                done = True
                self.store_x(pi)
                continue
            for l in range(2):
                self.ffn(l, 1, pi)
                if stop == "l%df1" % l:
                    done = True
                    break
                if l == 0:
                    self.mixer0(pi)
                else:
                    self.mixer1(pi)
                if DEBUG.get("dump") == "mixed" and stop == "l%dmix" % l:
                    T_ = TP + (TS if pi == 1 else 0)
                    for c in range(NCH):
                        self.V(self.dve, "tensor_copy", self.x[c], [self.a[c]], out=self.x[c].ap[:, 0:T_], in_=self.a[c].ap[:, 0:T_])
                if stop == "l%dmix" % l:
                    done = True
                    break
                self.ffn(l, 2, pi)
                if stop == "l%df2" % l:
                    done = True
                    break
            self.store_x(pi)
            if done and DEBUG.get("one_pass"):
                break
        if not done:
            self.final_outputs()
        self.finish()


def _unit_w(key, W):
    kind = key[0]

    def colblk(M, c0, n=128):
        return np.ascontiguousarray(M[:, c0:c0 + n].reshape(8, 128, n).transpose(1, 0, 2)).reshape(128, 8 * n)
    if kind == "ada":
        _, l, cbk, kq = key
        M = W["ada_w"][l][kq * 256:(kq + 1) * 256, cbk * 512:(cbk + 1) * 512]
        return np.ascontiguousarray(M.reshape(2, 128, 512).transpose(1, 0, 2)).reshape(128, 1024)
    if kind == "f_in":
        _, l, f, j, g = key
        Wi = W["ffn1_w_in"] if f == 1 else W["ffn2_w_in"]
        return colblk(Wi[l], g * DFF + j * 128)
    if kind == "f_out":
        _, l, f, i, hf = key
        j0, j1 = FO_SPLIT[hf]
        Wo = W["ffn1_w_out"] if f == 1 else W["ffn2_w_out"]
        M = Wo[l][j0 * 128:j1 * 128, i * 128:(i + 1) * 128]
        return np.ascontiguousarray(M.reshape(j1 - j0, 128, 128).transpose(1, 0, 2)).reshape(128, (j1 - j0) * 128)
    if kind == "ab_gate":
        return colblk(W["ab_w_in"][0], 2048, 8)
    if kind == "ab_in":
        _, nm, idx = key
        base = dict(qa=0, ka=512, va=1024, oa=1536, qb=2056, kb=2568, vb=3080)[nm]
        return colblk(W["ab_w_in"][0], base + idx * 128)
    if kind == "ab_out":
        return colblk(W["ab_w_out"][0], key[1] * 128)
    if kind == "c_in":
        _, which, n = key
        return colblk(W["c_w_in"][0], which * 1024 + n * 128)
    if kind == "c_gate":
        return np.ascontiguousarray(W["c_gate_w"][0][key[1]])
    if kind == "c_out":
        return colblk(W["c_w_out"][0], key[1] * 128)
    raise KeyError(key)


def _unit_c(key, ck, cv):
    kind, sq, hp = key
    if kind == "kc":
        return np.ascontiguousarray(ck[sq][:, 2 * hp:2 * hp + 2, :].reshape(512, 128).T)
    if kind == "vc":
        M = cv[sq][:, 2 * hp:2 * hp + 2, :].reshape(4, 128, 128)
        return np.ascontiguousarray(M.transpose(1, 0, 2)).reshape(128, 512)
    raise KeyError(key)


_CACHE = {}


def _get_prog():
    key = repr(sorted(DEBUG.items()))
    if key not in _CACHE:
        p = Prog()
        p.build()
        _CACHE[key] = p
    return _CACHE[key]


def kernel(**inp):
    inp = {k: np.asarray(v) for k, v in inp.items()}
    p = _get_prog()
    f32 = np.float32
    wst = np.zeros((128, max(p.wtotal, 1)), f32)
    for (src, key), off in p.offs.items():
        if src == "w":
            u = _unit_w(key, inp)
            wst[:, off:off + u.shape[1]] = u
    pv = np.zeros((128, p.NPV), f32)

    def fm(v):
        return np.asarray(v, f32).reshape(8, 128).T
    for nm, src in (("g0", "ffn1_norm"), ("g1", "mix_norm"), ("g2", "ffn2_norm")):
        for l in range(2):
            pv[:, p.PV[nm] + l * 8:p.PV[nm] + l * 8 + 8] = fm(inp[src][l])
    pv[:, p.PV["aon"]:p.PV["aon"] + 4] = inp["a_out_norm"][0].T
    pv[:, p.PV["gq"]] = np.tile(inp["b_q_norm"][0], 2)
    pv[:, p.PV["gk"]] = np.tile(inp["b_k_norm"][0], 2)
    pv[:, p.PV["gbias"]:p.PV["gbias"] + 8] = np.broadcast_to(inp["ab_gate_bias"][0][None, :], (128, 8))
    pv[:, p.PV["convw"]:p.PV["convw"] + 32] = inp["c_conv_w"][0].reshape(4, 8, 128).transpose(2, 1, 0).reshape(128, 32)
    pv[:, p.PV["convb"]:p.PV["convb"] + 8] = fm(inp["c_conv_b"][0])
    pv[:, p.PV["gateb"]:p.PV["gateb"] + 16] = inp["c_gate_b"][0].reshape(2, 8, 128).transpose(2, 1, 0).reshape(128, 16)
    pv[:, p.PV["lam"]:p.PV["lam"] + 8] = fm(inp["c_lambda"][0])
    adab = np.ascontiguousarray(inp["ada_b"].reshape(2, 72, 128).transpose(2, 0, 1), dtype=f32)
    tb = inp["b_rel_bias"][0]
    j = np.arange(640)[:, None]
    i = np.arange(128)[None, :]
    idx = np.clip(j - 512 - i, -128, 128) + 128
    bt = tb[:, idx]
    bt = bt.reshape(4, 2, 5, 128, 128).transpose(0, 3, 1, 2, 4).reshape(4, 128, 2, 640)
    biasT = np.ascontiguousarray(bt, f32)
    in_maps = []
    for c in range(8):
        sl = slice(4 * c, 4 * c + 4)
        xT = np.concatenate([inp["x_prompt"][c], inp["x_sample"][sl].reshape(TS, D)], axis=0)
        xT = np.ascontiguousarray(xT.T.reshape(8, 128, SEQ + TS).transpose(1, 0, 2))
        cc = np.concatenate([inp["c_prompt"][c:c + 1], inp["c_sample"][sl]], axis=0)
        cT = np.ascontiguousarray(cc.T.reshape(8, 128, 5).transpose(1, 0, 2))
        cst = np.zeros((128, max(p.ctotal, 1)), f32)
        ck, cv = inp["cache_b_k"][0][sl], inp["cache_b_v"][0][sl]
        for (src, key), off in p.offs.items():
            if src == "c":
                u = _unit_c(key, ck, cv)
                cst[:, off:off + u.shape[1]] = u
        sv = np.zeros((128, p.NSV), f32)
        sv[:, 0:16] = inp["state_a_n"][0][sl].reshape(16, 128).T
        sv[:, 16:32] = np.broadcast_to(inp["state_a_m"][0][sl].reshape(1, 16), (128, 16))
        sv[:, 32:128] = inp["state_c_conv"][0][sl].reshape(4, 3, 8, 128).transpose(3, 2, 0, 1).reshape(128, 96)
        sv[:, 128:160] = inp["state_c_h"][0][sl].reshape(4, 8, 128).transpose(2, 1, 0).reshape(128, 32)
        aC = np.ascontiguousarray(inp["state_a_C"][0][sl].reshape(16, 128, 128))
        in_maps.append(dict(xT=xT, cT=cT, wst=wst, cst=cst, pv=pv, sv=sv, adab=adab, biasT=biasT, aC=aC))
    res = run_bass_kernel_spmd(p.nc, in_maps, core_ids=list(range(8)))
    R = res.results
    if DEBUG.get("raw"):
        return R
    B = 8

    def cat(fn):
        return np.stack([fn(R[c]) for c in range(B)], axis=0)
    yT = cat(lambda r: r["o_yT"])
    y_all = yT.transpose(0, 3, 2, 1).reshape(B, SEQ + TS, D)
    y_prompt = np.ascontiguousarray(y_all[:, :SEQ])
    y_sample = np.ascontiguousarray(y_all[:, SEQ:].reshape(B * NSQ, LS, D))
    p_a_C = cat(lambda r: r["o_paC"])[None]
    p_a_n = cat(lambda r: r["o_pan"].T)[None]
    p_a_m = cat(lambda r: r["o_pam"][0])[None]
    p_b_k = cat(lambda r: r["o_pbk"].transpose(2, 1, 0).reshape(512, 8, 64))[None]
    p_b_v = cat(lambda r: r["o_pbv"].reshape(512, 8, 64))[None]
    p_c_conv = cat(lambda r: r["o_pconv"].transpose(2, 1, 0).reshape(3, D))[None]
    p_c_h = cat(lambda r: r["o_pch"].T.reshape(D))[None]
    s_a_C = cat(lambda r: r["o_saC"].reshape(4, 4, 128, 128)).reshape(1, 32, 4, 128, 128)
    s_a_n = cat(lambda r: r["o_san"].T.reshape(4, 4, 128)).reshape(1, 32, 4, 128)
    s_a_m = cat(lambda r: r["o_sam"][0].reshape(4, 4)).reshape(1, 32, 4)
    s_b_k = cat(lambda r: r["o_sbk"].reshape(128, 4, NSQ, LS).transpose(2, 3, 1, 0).reshape(NSQ, LS, 8, 64)).reshape(1, 32, LS, 8, 64)
    s_b_v = cat(lambda r: r["o_sbv"].reshape(NSQ, LS, 8, 64)).reshape(1, 32, LS, 8, 64)
    s_c_conv = cat(lambda r: r["o_sconv"].transpose(2, 3, 1, 0).reshape(NSQ, 3, D)).reshape(1, 32, 3, D)
    s_c_h = cat(lambda r: r["o_sch"].transpose(2, 1, 0).reshape(NSQ, D)).reshape(1, 32, D)
    outs = (y_prompt, y_sample, p_a_C, p_a_n, p_a_m, p_b_k, p_b_v, p_c_conv, p_c_h,
            s_a_C, s_a_n, s_a_m, s_b_k, s_b_v, s_c_conv, s_c_h)
    return tuple(np.ascontiguousarray(o, dtype=np.float32) for o in outs)
```
